# Optimizing a Trainium2 kernel written in Bass

```python
import jax, jax.numpy as jnp
from jax import lax
import numpy as np

D_MODEL = 1024
BATCH = 8
SEQ = 2048
DEPTH = 2
DEC_BATCH = 128
DEC_SEQ = 8
PAST_LEN = 16384
PAGE_SIZE = 128

N_META = 16
N_EVEN = (DEPTH + 1) // 2
N_ODD = DEPTH // 2
H_A = 4
DK_A = D_MODEL // (2 * H_A)
DV_A = D_MODEL // (2 * H_A)
H_B = 4
DK_B = D_MODEL // (2 * H_B)
DV_B = D_MODEL // (2 * H_B)
DK_C = 128
H_C = D_MODEL // DK_C
DV_C = D_MODEL // H_C
D_FF = 2816
CONV_W = 3
CHUNK_A = 128
CHUNK_B = 128
CHUNK_C = 32
ROPE_BASE = 10000.0
EPS = 1e-6
F_BIAS_LO = 3.0
F_BIAS_HI = 6.0
AB_SIZES = (H_A * DK_A, H_A * DK_A, H_A * DV_A, H_A * DV_A, H_A, H_A, H_B * DK_B, H_B * DK_B, H_B * DV_B, H_B * DV_B)
C_SIZES = (H_C * DK_C, H_C * DK_C, H_C * DV_C, H_C * DV_C)
MIX_AB = H_A * DV_A + H_B * DV_B

kernel_name = 'hybrid_mlstm_retention_hgrn2_convffn_step'


def split_cols(z, sizes):
    out, start = [], 0
    for s in sizes:
        out.append(z[..., start:start + s])
        start += s
    return out


def rms_norm(x, gain):
    xf = x.astype(jnp.float32)
    y = xf * lax.rsqrt(jnp.mean(xf * xf, axis=-1, keepdims=True) + EPS)
    return (y * gain.astype(jnp.float32)).astype(x.dtype)


def head_layer_norm(h, gain):
    mu = jnp.mean(h, axis=-1, keepdims=True)
    c = h - mu
    y = c * lax.rsqrt(jnp.mean(c * c, axis=-1, keepdims=True) + EPS)
    return y.reshape(h.shape[:2] + (-1,)) * gain.astype(jnp.float32)


def head_rms_norm(h, gain):
    y = h * lax.rsqrt(jnp.mean(h * h, axis=-1, keepdims=True) + EPS)
    return y.reshape(h.shape[:2] + (-1,)) * gain.astype(jnp.float32)


def rotary(x, pos):
    half = x.shape[-1] // 2
    inv = 1.0 / (ROPE_BASE ** jnp.linspace(0.0, 1.0, half, dtype=jnp.float32))
    ang = pos[:, None] * inv[None, :]
    cos = jnp.cos(ang)[None, :, None, :]
    sin = jnp.sin(ang)[None, :, None, :]
    x1, x2 = x[..., :half], x[..., half:]
    return jnp.concatenate([x1 * cos - x2 * sin, x2 * cos + x1 * sin], axis=-1)


def run_blocks(step, state, xs, chunk, prompt):
    if not prompt:
        return step(state, *xs)
    state, y_meta = step(state, *(a[:, :N_META] for a in xs))

    def to_chunks(a):
        a = a[:, N_META:]
        bsz, t = a.shape[:2]
        return jnp.moveaxis(a.reshape((bsz, t // chunk, chunk) + a.shape[2:]), 1, 0)

    state, ys = lax.scan(lambda s, c: step(s, *c), state, tuple(to_chunks(a) for a in xs))
    ys = jnp.moveaxis(ys, 0, 1)
    ys = ys.reshape((ys.shape[0], ys.shape[1] * ys.shape[2]) + ys.shape[3:])
    return state, jnp.concatenate([y_meta, ys], axis=1)


def mlstm_block(state, q, k, v, li, lf):
    c0, n0, m0 = state
    L = q.shape[1]
    causal = jnp.tril(jnp.ones((L, L), dtype=bool))
    b = jnp.cumsum(lf, axis=1)
    log_d = b[:, :, None, :] - b[:, None, :, :] + li[:, None, :, :]
    log_d = jnp.where(causal[None, :, :, None], log_d, -jnp.inf)
    log_inter = b + m0[:, None, :]
    m = jnp.maximum(log_inter, jnp.max(log_d, axis=2))
    d = jnp.exp(log_d - m[:, :, None, :])
    inter = jnp.exp(log_inter - m)
    s = jnp.einsum('bthd,bshd->btsh', q, k) * d
    num = jnp.einsum('btsh,bshv->bthv', s, v) + inter[..., None] * jnp.einsum('bthd,bhdv->bthv', q, c0)
    den = jnp.sum(s, axis=2) + inter * jnp.einsum('bthd,bhd->bth', q, n0)
    h = num / jnp.maximum(jnp.abs(den), jnp.exp(-m))[..., None]
    m_new = m[:, -1]
    w = jnp.exp(li + b[:, -1:, :] - b - m_new[:, None, :])
    carry = jnp.exp(log_inter[:, -1] - m_new)
    c_new = carry[..., None, None] * c0 + jnp.einsum('bsh,bshd,bshv->bhdv', w, k, v)
    n_new = carry[..., None] * n0 + jnp.einsum('bsh,bshd->bhd', w, k)
    return (c_new, n_new, m_new), h


def retention_block(s0, q, k, v, log_gamma):
    L = q.shape[1]
    idx = jnp.arange(L, dtype=jnp.float32)
    causal = idx[:, None] >= idx[None, :]
    rel = jnp.where(causal, idx[:, None] - idx[None, :], 0.0)
    decay = jnp.where(causal[..., None], jnp.exp(rel[..., None] * log_gamma), 0.0)
    s = jnp.einsum('bthd,bshd->btsh', q, k) * decay[None]
    inner = jnp.exp((idx + 1.0)[:, None] * log_gamma)
    o = jnp.einsum('btsh,bshv->bthv', s, v) + jnp.einsum('bthd,bhdv->bthv', q, s0) * inner[None, :, :, None]
    tail = jnp.exp((L - 1.0 - idx)[:, None] * log_gamma)
    s_new = jnp.exp(L * log_gamma)[None, :, None, None] * s0 + jnp.einsum('bshd,bshv,sh->bhdv', k, v, tail)
    return s_new, o


def hgrn_block(s0, q, k, v, log_f):
    L = q.shape[1]
    causal = jnp.tril(jnp.ones((L, L), dtype=bool))
    b = jnp.cumsum(log_f, axis=1)
    diff = b[:, :, None] - b[:, None, :]
    decay = jnp.exp(jnp.where(causal[None, :, :, None, None], diff, -jnp.inf))
    a = jnp.einsum('bthc,bshc,btshc->btsh', q, k, decay)
    o = jnp.einsum('btsh,bshv->bthv', a, v) + jnp.einsum('bthc,bhcv->bthv', q * jnp.exp(b), s0)
    b_last = b[:, -1]
    s_new = jnp.exp(b_last)[..., None] * s0 + jnp.einsum('bshc,bshv->bhcv', k * jnp.exp(b_last[:, None] - b), v)
    return s_new, o


def ab_mixer(xn, pos, prompt, w_in, b_i, b_f, g_a, g_b, w_out, state_a, state_b):
    bsz, t = xn.shape[:2]
    q_a, k_a, v_a, o_a, i_a, f_a, q_b, k_b, v_b, g_gate = split_cols(xn @ w_in, AB_SIZES)
    heads = lambda a, h: a.astype(jnp.float32).reshape(bsz, t, h, -1)
    li = i_a.astype(jnp.float32) + b_i.astype(jnp.float32)
    lf = jax.nn.log_sigmoid(f_a.astype(jnp.float32) + b_f.astype(jnp.float32))
    xs_a = (heads(q_a, H_A), heads(k_a, H_A) * DK_A ** -0.5, heads(v_a, H_A), li, lf)
    state_a, h_a = run_blocks(mlstm_block, state_a, xs_a, CHUNK_A, prompt)
    h_a = jax.nn.sigmoid(o_a.astype(jnp.float32)) * head_layer_norm(h_a, g_a)
    log_gamma = jnp.log1p(-jnp.exp2(-5.0 - jnp.arange(H_B, dtype=jnp.float32)))
    ret_step = lambda s, q, k, v: retention_block(s, q, k, v, log_gamma)
    xs_b = (rotary(heads(q_b, H_B), pos), rotary(heads(k_b, H_B), pos) * DK_B ** -0.5, heads(v_b, H_B))
    state_b, h_b = run_blocks(ret_step, state_b, xs_b, CHUNK_B, prompt)
    h_b = jax.nn.silu(g_gate.astype(jnp.float32)) * head_layer_norm(h_b, g_b)
    mixed = jnp.concatenate([h_a, h_b], axis=-1).astype(xn.dtype)
    return mixed @ w_out, state_a, state_b


def hgrn_lower_bound(lb_logits, layer):
    p = jax.nn.softmax(lb_logits.astype(jnp.float32), axis=0)
    return jnp.cumsum(p, axis=0)[layer] - p[0]


def hgrn_mixer(xn, prompt, w_in, lb, gain, w_out, s0):
    bsz, t = xn.shape[:2]
    q, f, i, g = split_cols(xn @ w_in, C_SIZES)
    f = f.astype(jnp.float32)
    log_f = jnp.log(lb + (1.0 - lb) * jax.nn.sigmoid(f))
    k = (1.0 - lb) * jax.nn.sigmoid(-f)
    heads = lambda a, h: a.astype(jnp.float32).reshape(bsz, t, h, -1)
    xs = (heads(q, H_C), heads(k, H_C), heads(i, H_C), heads(log_f, H_C))
    s_new, o = run_blocks(hgrn_block, s0, xs, CHUNK_C, prompt)
    o = head_rms_norm(o, gain) * jax.nn.silu(g.astype(jnp.float32))
    return o.astype(xn.dtype) @ w_out, s_new


def conv_ffn(xn, w_in, conv_w, conv_b, w_out, buf):
    t = xn.shape[1]
    u, gate = jnp.split(xn @ w_in, 2, axis=-1)
    padded = jnp.concatenate([buf.astype(u.dtype), u], axis=1)
    cw = conv_w.astype(jnp.float32)
    conv = conv_b.astype(jnp.float32) + sum(padded[:, j:j + t].astype(jnp.float32) * cw[j] for j in range(CONV_W))
    h = (jax.nn.silu(conv) * gate.astype(jnp.float32)).astype(xn.dtype)
    return h @ w_out, padded[:, t:]


def trunk(x, pos, prompt, st_c, st_n, st_m, st_r, st_s, st_conv,
          norm_mix, w_in_ab, b_igate, b_fgate, gn_mlstm, gn_ret, w_out_ab,
          lb_logits, w_in_c, gn_hgrn, w_out_c,
          norm_ffn, w_ffn_in, conv_w, conv_b, w_ffn_out, norm_final):
    f32 = jnp.float32
    new_c, new_n, new_m, new_r, new_s, new_conv = [], [], [], [], [], []
    for layer in range(DEPTH):
        j = layer // 2
        xn = rms_norm(x, norm_mix[layer])
        if layer % 2 == 0:
            out, (c, n, m), r = ab_mixer(xn, pos, prompt, w_in_ab[j], b_igate[j], b_fgate[j], gn_mlstm[j], gn_ret[j], w_out_ab[j],
                                         (st_c[j].astype(f32), st_n[j].astype(f32), st_m[j].astype(f32)), st_r[j].astype(f32))
            new_c.append(c)
            new_n.append(n)
            new_m.append(m)
            new_r.append(r)
        else:
            lb = hgrn_lower_bound(lb_logits, layer)
            out, s = hgrn_mixer(xn, prompt, w_in_c[j], lb, gn_hgrn[j], w_out_c[j], st_s[j].astype(f32))
            new_s.append(s)
        x = x + out.astype(x.dtype)
        h, buf = conv_ffn(rms_norm(x, norm_ffn[layer]), w_ffn_in[layer], conv_w[layer], conv_b[layer], w_ffn_out[layer], st_conv[layer])
        x = x + h.astype(x.dtype)
        new_conv.append(buf)
    y = rms_norm(x, norm_final)
    return y, jnp.stack(new_c), jnp.stack(new_n), jnp.stack(new_m), jnp.stack(new_r), jnp.stack(new_s), jnp.stack(new_conv)


def setup_inputs(seed: int = 0) -> dict:
    key = jax.random.key(seed)
    ks = jax.random.split(key, 32)
    nrm = lambda k, shape, scale: scale * jax.random.normal(k, shape, jnp.float32)
    ab_width = sum(AB_SIZES)
    c_width = sum(C_SIZES)
    return {
        'x_prompt': nrm(ks[0], (BATCH, SEQ, D_MODEL), 1.0),
        'x_sample': nrm(ks[1], (DEC_BATCH, DEC_SEQ, D_MODEL), 1.0),
        'state_mlstm_C': nrm(ks[2], (N_EVEN, DEC_BATCH, H_A, DK_A, DV_A), 0.1),
        'state_mlstm_n': nrm(ks[3], (N_EVEN, DEC_BATCH, H_A, DK_A), 0.1),
        'state_mlstm_m': nrm(ks[4], (N_EVEN, DEC_BATCH, H_A), 1.0),
        'state_ret_S': nrm(ks[5], (N_EVEN, DEC_BATCH, H_B, DK_B, DV_B), 0.5),
        'state_hgrn_S': nrm(ks[6], (N_ODD, DEC_BATCH, H_C, DK_C, DV_C), 0.5),
        'state_ffn_conv': nrm(ks[7], (DEPTH, DEC_BATCH, CONV_W - 1, D_FF), 1.0),
        'meta_tokens': nrm(ks[8], (N_META, D_MODEL), 1.0),
        'norm_mix': 1.0 + nrm(ks[9], (DEPTH, D_MODEL), 0.02),
        'w_in_ab': nrm(ks[10], (N_EVEN, D_MODEL, ab_width), D_MODEL ** -0.5),
        'b_igate': nrm(ks[11], (N_EVEN, H_A), 0.1),
        'b_fgate': jnp.linspace(F_BIAS_LO, F_BIAS_HI, H_A, dtype=jnp.float32)[None, :] + nrm(ks[12], (N_EVEN, H_A), 0.1),
        'gn_mlstm': 1.0 + nrm(ks[13], (N_EVEN, H_A * DV_A), 0.02),
        'gn_ret': 1.0 + nrm(ks[14], (N_EVEN, H_B * DV_B), 0.02),
        'w_out_ab': nrm(ks[15], (N_EVEN, MIX_AB, D_MODEL), MIX_AB ** -0.5),
        'lb_logits': nrm(ks[16], (DEPTH, H_C * DK_C), 1.0),
        'w_in_c': nrm(ks[17], (N_ODD, D_MODEL, c_width), D_MODEL ** -0.5),
        'gn_hgrn': 1.0 + nrm(ks[18], (N_ODD, H_C * DV_C), 0.02),
        'w_out_c': nrm(ks[19], (N_ODD, H_C * DV_C, D_MODEL), (H_C * DV_C) ** -0.5),
        'norm_ffn': 1.0 + nrm(ks[20], (DEPTH, D_MODEL), 0.02),
        'w_ffn_in': nrm(ks[21], (DEPTH, D_MODEL, 2 * D_FF), D_MODEL ** -0.5),
        'conv_w': nrm(ks[22], (DEPTH, CONV_W, D_FF), CONV_W ** -0.5),
        'conv_b': nrm(ks[23], (DEPTH, D_FF), 0.02),
        'w_ffn_out': nrm(ks[24], (DEPTH, D_FF, D_MODEL), D_FF ** -0.5),
        'norm_final': 1.0 + nrm(ks[25], (D_MODEL,), 0.02),
    }


def reference(x_prompt, x_sample, state_mlstm_C, state_mlstm_n, state_mlstm_m, state_ret_S, state_hgrn_S, state_ffn_conv,
              meta_tokens, norm_mix, w_in_ab, b_igate, b_fgate, gn_mlstm, gn_ret, w_out_ab,
              lb_logits, w_in_c, gn_hgrn, w_out_c, norm_ffn, w_ffn_in, conv_w, conv_b, w_ffn_out, norm_final):
    f32 = jnp.float32
    bp = x_prompt.shape[0]
    meta = jnp.broadcast_to(meta_tokens.astype(x_prompt.dtype)[None], (bp, N_META, D_MODEL))
    xp = jnp.concatenate([meta, x_prompt], axis=1)
    pos_p = jnp.arange(xp.shape[1], dtype=f32)
    pos_s = PAST_LEN + jnp.arange(x_sample.shape[1], dtype=f32)

    yp, cp, n_p, mp, rp, sp, convp = trunk(
        xp, pos_p, True,
        jnp.zeros((N_EVEN, bp, H_A, DK_A, DV_A), f32), jnp.zeros((N_EVEN, bp, H_A, DK_A), f32),
        jnp.zeros((N_EVEN, bp, H_A), f32), jnp.zeros((N_EVEN, bp, H_B, DK_B, DV_B), f32),
        jnp.zeros((N_ODD, bp, H_C, DK_C, DV_C), f32), jnp.zeros((DEPTH, bp, CONV_W - 1, D_FF), x_prompt.dtype),
        norm_mix, w_in_ab, b_igate, b_fgate, gn_mlstm, gn_ret, w_out_ab,
        lb_logits, w_in_c, gn_hgrn, w_out_c, norm_ffn, w_ffn_in, conv_w, conv_b, w_ffn_out, norm_final)

    ys, cs, n_s, ms, rs, ss, convs = trunk(
        x_sample, pos_s, False,
        state_mlstm_C, state_mlstm_n, state_mlstm_m, state_ret_S, state_hgrn_S, state_ffn_conv,
        norm_mix, w_in_ab, b_igate, b_fgate, gn_mlstm, gn_ret, w_out_ab,
        lb_logits, w_in_c, gn_hgrn, w_out_c, norm_ffn, w_ffn_in, conv_w, conv_b, w_ffn_out, norm_final)

    y_prompt = yp[:, N_META:]
    return (y_prompt, ys,
            cp.astype(state_mlstm_C.dtype), cs.astype(state_mlstm_C.dtype),
            n_p.astype(state_mlstm_n.dtype), n_s.astype(state_mlstm_n.dtype),
            mp.astype(state_mlstm_m.dtype), ms.astype(state_mlstm_m.dtype),
            rp.astype(state_ret_S.dtype), rs.astype(state_ret_S.dtype),
            sp.astype(state_hgrn_S.dtype), ss.astype(state_hgrn_S.dtype),
            convp.astype(state_ffn_conv.dtype), convs.astype(state_ffn_conv.dtype))
```

```python
import numpy as np
import concourse.bass as bass
import concourse.mybir as mybir

F32 = mybir.dt.float32
BF16 = mybir.dt.bfloat16
I32 = mybir.dt.int32
AF = mybir.ActivationFunctionType
ALU = mybir.AluOpType


class _Rec:
    __slots__ = ("lo", "hi", "w", "rs")

    def __init__(self, lo, hi, w, rs):
        self.lo, self.hi, self.w, self.rs = lo, hi, w, rs


def _ap_interval(ap):
    pat = ap.ap
    name = ap.tensor.name
    off = int(ap.offset)
    space = str(ap.space)
    if "DRAM" in space.upper() or "HBM" in space.upper():
        ext = sum((c - 1) * abs(s) for s, c in pat) + 1
        return ("d:" + name, off, off + ext)
    if "PSUM" in space.upper():
        return ("p:" + name, 0, 1 << 30)
    pstride = pat[0][0] if pat[0][0] > 0 else 1 << 30
    lo = off % pstride if pat[0][0] > 0 else off
    ext = sum((c - 1) * abs(s) for s, c in pat[1:]) + 1
    return ("s:" + name, lo, lo + ext)


class Tracker:
    COMPUTE = ("pe", "act", "dve", "pool")

    def __init__(self, nc, ring=20, same_eng_sync=True):
        self.nc = nc
        self.engobj = {"pe": nc.tensor, "act": nc.scalar, "dve": nc.vector,
                       "pool": nc.gpsimd, "sp": nc.sync}
        self.prog = {e: [] for e in self.engobj}
        self.count = {e: 0 for e in self.COMPUTE}
        self.sems = {}
        self.waited = {e: {} for e in self.engobj}
        self.bufs = {}
        self.same_eng_sync = same_eng_sync
        self._ctx = []
        for e in self.COMPUTE:
            self.sems[e] = self._mk_sem("c_" + e)
        self.rings = {}
        for q in ("sp", "pool", "act"):
            self.rings[q] = {"sems": [self._mk_sem("q_%s_%d" % (q, i)) for i in range(ring)],
                             "uses": [0] * ring, "n": 0}
        self.ninstr = {e: 0 for e in self.engobj}
        self._cap = None
        self._unit = None

    def _mk_sem(self, name):
        cm = self.nc.semaphore(name)
        h = cm.__enter__()
        self._ctx.append(cm)
        return h

    def _access(self, key, lo, hi, write, tok, rkey):
        recs = self.bufs.get(key)
        if recs is None:
            recs = []
        deps = []
        new = []
        cov = []
        for r in recs:
            if r.hi <= lo or r.lo >= hi:
                new.append(r)
                continue
            if r.w is not None:
                deps.append(r.w)
            if write:
                deps.extend(r.rs.values())
            if r.lo < lo:
                new.append(_Rec(r.lo, lo, r.w, dict(r.rs)))
            if r.hi > hi:
                new.append(_Rec(hi, r.hi, r.w, dict(r.rs)))
            if not write:
                mid = _Rec(max(r.lo, lo), min(r.hi, hi), r.w, dict(r.rs))
                mid.rs[rkey] = tok
                new.append(mid)
                cov.append((mid.lo, mid.hi))
        if write:
            new.append(_Rec(lo, hi, tok, {}))
        else:
            cov.sort()
            cur = lo
            for a, b in cov:
                if a > cur:
                    new.append(_Rec(cur, a, None, {rkey: tok}))
                cur = max(cur, b)
            if cur < hi:
                new.append(_Rec(cur, hi, None, {rkey: tok}))
        self.bufs[key] = new
        return deps

    def _collect(self, eng, outs, ins, tok, rkey):
        deps = []
        for ap in outs:
            k, lo, hi = _ap_interval(ap)
            deps += self._access(k, lo, hi, True, tok, rkey)
        for ap in ins:
            if ap is None or isinstance(ap, (int, float)):
                continue
            k, lo, hi = _ap_interval(ap)
            deps += self._access(k, lo, hi, False, tok, rkey)
        waits = []
        w = self.waited[eng]
        best = {}
        for (skey, sh, val) in deps:
            if skey == tok[0] and val >= tok[2]:
                continue
            if skey == "c_" + eng:
                if eng == "pe" or not self.same_eng_sync:
                    continue
            if w.get(skey, 0) >= val:
                continue
            if skey not in best or best[skey][1] < val:
                best[skey] = (sh, val)
        for skey, (sh, val) in best.items():
            w[skey] = val
            waits.append((sh, val))
        return waits

    def begin_capture(self):
        self._cap = []
        self._unit = None

    def end_capture(self):
        c = self._cap
        self._cap = None
        self._unit = None
        return c

    def atomic_begin(self):
        if self._cap is not None and self._unit is None:
            self._unit = []
            self._cap.append(self._unit)
            return True
        return False

    def atomic_end(self, opened):
        if opened:
            self._unit = None

    def _record(self, rec):
        if self._unit is not None:
            self._unit.append(rec)
        else:
            self._cap.append([rec])

    def flush_rr(self, caps):
        idx = [0] * len(caps)
        live = True
        while live:
            live = False
            for li, c in enumerate(caps):
                if idx[li] < len(c):
                    live = True
                    for rec in c[idx[li]]:
                        if rec[0] == "op":
                            self.op(*rec[1:])
                        else:
                            self.dma(rec[1], rec[2], rec[3], **rec[4])
                    idx[li] += 1

    def op(self, eng, fn, outs, ins, inc=True):
        if self._cap is not None:
            self._record(("op", eng, fn, outs, ins, inc))
            return
        val = self.count[eng] + 1
        skey = "c_" + eng
        tok = (skey, self.sems[eng], val)
        waits = self._collect(eng, outs, ins, tok, skey)
        if inc:
            self.count[eng] = val
        self.prog[eng].append((waits, fn, (self.sems[eng], 1) if inc else None))
        self.ninstr[eng] += 1

    def dma(self, q, out, in_, **kw):
        if self._cap is not None:
            self._record(("dma", q, out, in_, kw))
            return
        ring = self.rings[q]
        n = ring["n"]
        slot = n % len(ring["sems"])
        ring["n"] = n + 1
        uses = ring["uses"][slot]
        ring["uses"][slot] = uses + 1
        sh = ring["sems"][slot]
        skey = "q_%s_%d" % (q, slot)
        tok = (skey, sh, 16 * (uses + 1))
        waits = self._collect(q, [out], [in_], tok, skey)
        w = self.waited[q]
        if uses > 0 and w.get(skey, 0) < 16 * uses:
            w[skey] = 16 * uses
            waits.append((sh, 16 * uses))
        self.prog[q].append((waits, (lambda e, out=out, in_=in_, kw=kw: e.dma_start(out=out, in_=in_, **kw)),
                             (sh, 16)))
        self.ninstr[q] += 1

    def finish(self):
        waits = []
        for q, ring in self.rings.items():
            for i, sh in enumerate(ring["sems"]):
                if ring["uses"][i] > 0:
                    waits.append((sh, 16 * ring["uses"][i]))
        for e in self.COMPUTE:
            if self.count[e] > 0:
                waits.append((self.sems[e], self.count[e]))
        self.prog["sp"].append((waits, None, None))

    def emit(self):
        nc = self.nc
        with nc.Block() as block:
            def run(name):
                def f(e):
                    for waits, fn, inc in self.prog[name]:
                        for sh, val in waits:
                            e.wait_ge(sh, val)
                        if fn is None:
                            continue
                        ins = fn(e)
                        if inc is not None:
                            ins.then_inc(inc[0], inc[1])
                return f
            block.sync(run("sp"))
            block.tensor(run("pe"))
            block.scalar(run("act"))
            block.vector(run("dve"))
            block.gpsimd(run("pool"))

    def close(self):
        for cm in reversed(self._ctx):
            cm.__exit__(None, None, None)


import ml_dtypes
from concourse.bass_utils import run_bass_kernel_spmd

D = 1024
NT = 2192
HC = 1168
NF = 22
EPS = 1e-6
ARENA = 32768
STAGE = 99
ONLY = None
HALVES = (0, 1)
NLANES = 2
HSEG = 64
FAST_ROWS = True
ALT_N = True
SEQ_FLUSH = False
SAME_ENG_SYNC = True


def _halves():
    A = dict(c0=0, n=144, g0=0, kind="A", groups=[("S", 0, 128), ("P", 128, 16)])
    B = [dict(c0=(144 + 512 * i) if i < 2 else 512 * (i - 2), n=512, g0=144 + 512 * i, kind="B",
              groups=[("P", 128 * j, 128) for j in range(4)]) for i in range(4)]
    return [[A, B[0], B[1]], [B[2], B[3]]]


C_ID, C_NEGC, C_NEG8, C_M32, C_M8, C_DEC, C_INN, C_TAIL, C_SEG16, C_SEG4, C_NSA, C_NSB, C_END = (
    0, 128, 256, 384, 512, 640, 640 + 1024, 640 + 2048, 2700, 2716, 2720, 2864, 3376)
R_SEL, R_NSP, R_NSS, R_RSP, R_RSS, R_ID4, R_END = 0, 512, 640, 768, 896, 1024, 1028


def _consts():
    c = np.zeros((128, C_END), np.float32)
    s = np.arange(128)[:, None]
    t = np.arange(128)[None, :]
    c[:, C_ID:C_ID + 128] = np.eye(128)
    caus = (s <= t)
    bd8 = caus & ((s // 8) == (t // 8))
    bd32 = caus & ((s // HSEG) == (t // HSEG))
    c[:, C_NEGC:C_NEGC + 128] = np.where(caus, 0.0, -30000.0)
    c[:, C_NEG8:C_NEG8 + 128] = np.where(bd8, 0.0, -30000.0)
    c[:, C_M32:C_M32 + 128] = bd32
    c[:, C_M8:C_M8 + 128] = bd8
    lg = np.log1p(-np.exp2(-5.0 - np.arange(4, dtype=np.float32))).astype(np.float64)
    for ty, m in enumerate((caus, bd8)):
        for h in range(4):
            o = C_DEC + (ty * 4 + h) * 128
            c[:, o:o + 128] = np.where(m, np.exp(np.maximum(t - s, 0) * lg[h]), 0.0)
            tp = t if ty == 0 else (t % 8)
            o = C_INN + (ty * 4 + h) * 128
            c[:, o:o + 128] = np.exp((tp + 1.0) * lg[h]) * np.ones((128, 1))
    sp = np.arange(128)
    for h in range(4):
        c[:, C_TAIL + 0 + h] = np.exp(np.maximum(127.0 - sp, 0) * lg[h])
        c[:, C_TAIL + 4 + h] = np.exp(np.maximum(15.0 - sp, 0) * lg[h])
        c[:, C_TAIL + 8 + h] = np.exp((7.0 - sp % 8) * lg[h])
    c[:, C_SEG16:C_SEG16 + 16] = (s // 8) == np.arange(16)[None, :]
    c[:, C_SEG4:C_SEG4 + 4] = (s // HSEG) == np.arange(4)[None, :]
    nsa = np.ones(144, np.float32)
    nsa[0:128:8] = 0.0
    nsa[128] = 0.0
    nsb = np.ones(512, np.float32)
    nsb[0::HSEG] = 0.0
    c[:, C_NSA:C_NSA + 144] = nsa[None, :]
    c[:, C_NSB:C_NSB + 512] = nsb[None, :]
    r = np.zeros((4, R_END), np.float32)
    for h in range(4):
        r[h, R_SEL + h * 128:R_SEL + (h + 1) * 128] = 1.0
    r[:, R_NSP:R_NSP + 128] = 1.0
    nss = np.ones(128, np.float32)
    nss[0::8] = 0.0
    r[:, R_NSS:R_NSS + 128] = nss[None, :]
    r[:, R_RSP:R_RSP + 128] = 0.0
    r[:, R_RSS:R_RSS + 128] = np.where(nss == 0.0, -1e30, 0.0)[None, :]
    r[:, R_ID4:R_ID4 + 4] = np.eye(4)
    gl = [[float(np.exp(L * lg[h])) for h in range(4)] for L in (128, 16, 8)]
    pos = np.concatenate([np.tile(16384.0 + np.arange(8), 16), np.arange(16), 16.0 + np.arange(2048)]).astype(np.float32)
    inv = (1.0 / (np.float32(10000.0) ** np.linspace(0.0, 1.0, 64, dtype=np.float32))).astype(np.float32)
    ang = (pos[:, None] * inv[None, :]).astype(np.float32)
    cs, sn = np.cos(ang).astype(np.float32), np.sin(ang).astype(np.float32)
    rot = np.zeros((128, 2, NT), np.float32)
    rot[0:64, 0, :] = cs.T
    rot[64:128, 0, :] = cs.T
    rot[0:64, 1, :] = -sn.T
    rot[64:128, 1, :] = sn.T
    return c, r, rot, gl


P_GAIN, P_GNM, P_GNR, P_GNH, P_LB0, P_LB1, P_CW, P_CB, P_END = 0, 40, 44, 48, 56, 64, 72, 204, 248


def _fm(v):
    v = np.asarray(v, np.float32)
    n = v.shape[-1] // 128
    v = v.reshape(v.shape[:-1] + (n, 128))
    return np.ascontiguousarray(np.moveaxis(v, -1, 0))


def _wl(w):
    w = np.asarray(w, np.float32)
    k = w.shape[0] // 128
    return np.ascontiguousarray(w.reshape(k, 128, w.shape[1]).transpose(1, 0, 2))


def build_program(gl):
    nc = bass.Bass("TRN2", target_bir_lowering=False)
    tr = Tracker(nc, ring=20, same_eng_sync=SAME_ENG_SYNC)

    def din(name, shape):
        return nc.dram_tensor(name, list(shape), F32, kind="ExternalInput").ap()

    def dout(name, shape):
        return nc.dram_tensor(name, list(shape), F32, kind="ExternalOutput").ap()

    xT_d = din("xT", (128, 8, NT))
    w_ab = din("w_ab", (128, 8, 5128))
    w_oab = din("w_oab", (128, 8, 1024))
    w_c = din("w_c", (128, 8, 2, 2048))
    w_oc = din("w_oc", (128, 8, 1024))
    w_fi = din("w_fi", (2, 128, 8, NF * 256))
    w_fo = din("w_fo", (2, 128, NF, 1024))
    stC = din("stC", (128, 4, 16, 128))
    stn = din("stn", (128, 4, 16))
    stm = din("stm", (4, 16))
    stR = din("stR", (128, 4, 16, 128))
    stH = din("stH", (128, 8, 16, 128))
    stcv = din("stcv", (128, 2, NF, 32))
    prm = din("prm", (128, P_END))
    prm4 = din("prm4", (4, 2))
    c128 = din("c128", (128, C_END))
    c4 = din("c4", (4, R_END))
    rot = din("rot", (128, 2, NT))
    yT_d = dout("yT", (128, 8, NT))
    oC_p = dout("oC_p", (128, 4, 128))
    oC_s = dout("oC_s", (128, 4, 16, 128))
    on_p = dout("on_p", (128, 4))
    on_s = dout("on_s", (128, 4, 16))
    om_p = dout("om_p", (4, 1))
    om_s = dout("om_s", (4, 16))
    oR_p = dout("oR_p", (128, 4, 128))
    oR_s = dout("oR_s", (128, 4, 16, 128))
    oH_p = dout("oH_p", (128, 8, 128))
    oH_s = dout("oH_s", (128, 8, 16, 128))
    ocv_p = dout("ocv_p", (128, 2, NF, 2))
    ocv_s = dout("ocv_s", (128, 2, NF, 32))

    ctxs = []

    def sb(name, shape, dt):
        cm = nc.sbuf_tensor(name, list(shape), dt)
        t = cm.__enter__()
        ctxs.append(cm)
        return t

    def ps(name, shape, dt):
        cm = nc.psum_tensor(name, list(shape), dt)
        t = cm.__enter__()
        ctxs.append(cm)
        return t

    xT = sb("xTs", (128, 8, HC), F32)
    xn = sb("xn", (128, 8, HC), BF16)
    arena = sb("arena", (128, ARENA), BF16)
    cst = sb("cst", (128, C_END), F32)
    crow = sb("crow", (4, R_END), F32)
    prms = sb("prms", (128, P_END), F32)
    prm4s = sb("prm4s", (4, 4), F32)
    ident_b = sb("ident_b", (128, 128), BF16)
    ones_b = sb("ones_b", (128, 128), BF16)
    segm_b = sb("segm_b", (128, 20), BF16)
    onesdiv = sb("onesdiv", (128, 128), F32)
    Cst = sb("Cst", (128, 4, 128), F32)
    nst = sb("nst", (128, 4), F32)
    Cb = sb("Cb", (128, 4, 128), BF16)
    nbc = sb("nbc", (128, 4, 128), BF16)
    Rst = sb("Rst", (128, 4, 128), F32)
    Rb = sb("Rb", (128, 4, 128), BF16)
    Hst = sb("Hst", (128, 8, 128), F32)
    Hb = sb("Hb", (128, 8, 128), BF16)
    mstate = sb("mstate", (4, 4), F32)
    tailbuf = sb("tailbuf", (128, 2, NF, 2), F32)
    cvout = sb("cvout", (128, NF, 32), F32)
    lbv = sb("lbv", (128, 24), F32)
    WF = 8300
    WB = 10000
    wkf = sb("wkf", (128, WF), F32)
    wkb = sb("wkb", (128, WB), BF16)

    PB = [ps("pb%d" % i, (128, 512), F32) for i in range(8)]
    PA = [PB[0], PB[1]]
    LPS = [dict(G=PB[2], N=PB[3], Z=PB[4]), dict(G=PB[5], N=PB[6], Z=PB[7])]
    for lp in LPS:
        lp["T"] = lp["Z"][:, 448:512].bitcast(BF16)

    def isap(x):
        return x is not None and not isinstance(x, (int, float))

    def MM(out, lhsT, rhs, start=True, stop=True, inc=True):
        tr.op("pe", lambda e: e.matmul(out, lhsT=lhsT, rhs=rhs, start=start, stop=stop), [out], [lhsT, rhs], inc=inc)

    def MMK(out, pairs):
        n = len(pairs)
        o = tr.atomic_begin()
        for i, (l, r) in enumerate(pairs):
            MM(out, l, r, start=(i == 0), stop=(i == n - 1), inc=(i == n - 1))
        tr.atomic_end(o)

    def TP(out, in_, ident):
        tr.op("pe", lambda e: e.transpose(out=out, in_=in_, identity=ident), [out], [in_, ident])

    def ACT(out, in_, func, bias=None, scale=None):
        kw = {}
        if bias is not None:
            kw["bias"] = bias
        if scale is not None:
            kw["scale"] = scale
        ins = [in_] + [a for a in (bias, scale) if isap(a)]
        tr.op("act", lambda e: e.activation(out=out, in_=in_, func=func, **kw), [out], ins)

    def TT(eng, out, in0, in1, op):
        tr.op(eng, lambda e: e.tensor_tensor(out=out, in0=in0, in1=in1, op=op), [out], [in0, in1])

    def TS(eng, out, in0, s1, op0, s2=None, op1=None):
        ins = [in0] + [a for a in (s1, s2) if isap(a)]
        if op1 is None:
            tr.op(eng, lambda e: e.tensor_scalar(out=out, in0=in0, scalar1=s1, scalar2=None, op0=op0), [out], ins)
        else:
            tr.op(eng, lambda e: e.tensor_scalar(out=out, in0=in0, scalar1=s1, scalar2=s2, op0=op0, op1=op1), [out], ins)

    def STT(out, in0, scalar, in1, op0, op1):
        ins = [in0, in1] + ([scalar] if isap(scalar) else [])
        tr.op("dve", lambda e: e.scalar_tensor_tensor(out=out, in0=in0, scalar=scalar, in1=in1, op0=op0, op1=op1), [out], ins)

    def CP(eng, out, in_):
        if eng == "act":
            tr.op("act", lambda e: e.activation(out=out, in_=in_, func=AF.Copy), [out], [in_])
        else:
            tr.op(eng, lambda e: e.tensor_copy(out=out, in_=in_), [out], [in_])

    def SCAN(out, d0, d1, init, op0, op1):
        tr.op("dve", lambda e: e.tensor_tensor_scan(out=out, data0=d0, data1=d1, initial=init, op0=op0, op1=op1), [out], [d0, d1])

    def RECIP(out, in_):
        tr.op("dve", lambda e: e.reciprocal(out=out, in_=in_), [out], [in_])

    def MEMSET(eng, ap, val):
        tr.op(eng, lambda e: e.memset(ap, val), [ap], [])

    def DMA(q, out, in_):
        tr.dma(q, out, in_)

    MUL, ADD, SUB, MAX = ALU.mult, ALU.add, ALU.subtract, ALU.max

    def bc(ap, shape):
        return ap.broadcast_to(list(shape))

    def carve(base, plan, off=0):
        out = {}
        for name, shape in plan:
            sz = int(np.prod(shape))
            v = base[:, off:off + sz]
            v = v.rearrange("p (a b) -> p a b", a=shape[0])
            out[name] = v
            off += sz
        assert off <= base.shape[1], (off, base.shape)
        return out, off

    def carve_lanes(base, shared, lane, extra):
        S, o = carve(base, shared)
        L0, o1 = carve(base, lane, o)
        L1, o2 = carve(base, lane, o1)
        X, o3 = carve(base, extra, o1)
        S.update(X)
        return S, [L0, L1]

    ar = {"off": 0}

    def load_weights(spec, prev_region):
        tot = sum(a.shape[1] * a.shape[2] for a in spec)
        off = ar["off"]
        if off + tot > ARENA:
            off = 0
        if prev_region is not None:
            plo, phi = prev_region
            if not (off + tot <= plo or off >= phi):
                return None
        views = []
        o = off
        for a in spec:
            K, C = a.shape[1], a.shape[2]
            v = arena[:, o:o + K * C].rearrange("p (k c) -> p k c", k=K)
            for k in range(K):
                for c0 in range(0, C, 2048):
                    c1 = min(C, c0 + 2048)
                    DMA("pool", v[:, k, c0:c1], a[:, k, c0:c1])
            views.append(v)
            o += K * C
        ar["off"] = o
        return views, (off, o)

    DMA("sp", cst[:], c128)
    DMA("sp", crow[:], c4)
    DMA("sp", prms[:], prm)
    DMA("sp", prm4s[:, 0:2], prm4)
    CP("dve", ident_b[:], cst[:, C_ID:C_ID + 128])
    MEMSET("dve", ones_b[:], 1.0)
    CP("dve", segm_b[:], cst[:, C_SEG16:C_SEG16 + 20])
    MEMSET("dve", onesdiv[:], 1.0 / 128.0)
    for t_ in (Cst, nst, Rst, Hst, mstate, tailbuf):
        MEMSET("pool", t_[:], 0.0)
    for t_ in (Cb, nbc, Rb, Hb):
        MEMSET("pool", t_[:], 0.0)
    TS("dve", prm4s[:, 2:3], prm4s[:, 1:2], -1.0, MUL)
    TT("dve", lbv[:, 16:24], prms[:, P_LB1:P_LB1 + 8], prms[:, P_LB0:P_LB0 + 8], SUB)
    ACT(lbv[:, 0:8], lbv[:, 16:24], AF.Sigmoid)
    TS("dve", lbv[:, 8:16], lbv[:, 0:8], -1.0, MUL, 1.0, ADD)
    TS("dve", lbv[:, 16:24], lbv[:, 8:16], -1.0, MUL)

    sel4 = lambda h: crow[:, R_SEL + h * 128:R_SEL + (h + 1) * 128]
    ident4 = crow[:, R_ID4:R_ID4 + 4]

    def norm_tile(t, gidx, final):
        c0, n = t["c0"], t["n"]
        wf, _ = carve(wkf, [("lnv", (1, 512)), ("rstd", (1, 512)), ("yo", (2, 512))])
        wb_, _ = carve(wkb, [("sq", (8, 512))])
        for k in range(8):
            ACT(wb_["sq"][:, k, 0:n], xT[:, k, c0:c0 + n], AF.Square)
        MMK(PA[0][:, 0:n], [(ones_b[:], wb_["sq"][:, k, 0:n]) for k in range(8)])
        ACT(wf["lnv"][:, 0, 0:n], PA[0][:, 0:n], AF.Ln, bias=EPS, scale=1.0 / D)
        ACT(wf["rstd"][:, 0, 0:n], wf["lnv"][:, 0, 0:n], AF.Exp, scale=-0.5)
        for k in range(8):
            g = prms[:, P_GAIN + gidx * 8 + k:P_GAIN + gidx * 8 + k + 1]
            if final:
                yo = wf["yo"][:, k % 2, 0:n]
                STT(yo, xT[:, k, c0:c0 + n], g, wf["rstd"][:, 0, 0:n], MUL, MUL)
                DMA("sp", yT_d[:, k, t["g0"]:t["g0"] + n], yo)
            else:
                STT(xn[:, k, c0:c0 + n], xT[:, k, c0:c0 + n], g, wf["rstd"][:, 0, 0:n], MUL, MUL)

    def ln_gate(W, P, L, gaincol, gate, out, mean, sq_src=None):
        PN = P["N"]
        hT = W["hT"][:, 0, 0:L]
        if sq_src is None:
            ACT(W["hsq"][:, 0, 0:L], hT, AF.Square)
        if mean and L == 128:
            both = bass.AP(hT.tensor, hT.offset, [list(hT.ap[0]), [1, 256]])
            assert W["hsq"][:, 0, 0:L].offset == hT.offset + 128
            MM(PN[:, 256:512], onesdiv[:], both)
        else:
            if mean:
                MM(PN[:, 256:256 + L], onesdiv[:], hT)
            MM(PN[:, 384:384 + L], onesdiv[:], W["hsq"][:, 0, 0:L])
        if mean:
            ACT(W["msq"][:, 0, 0:L], PN[:, 256:256 + L], AF.Square)
            TT("dve", W["var"][:, 0, 0:L], PN[:, 384:384 + L], W["msq"][:, 0, 0:L], SUB)
            ACT(W["lnv"][:, 0, 0:L], W["var"][:, 0, 0:L], AF.Ln, bias=EPS)
            TT("dve", W["cc"][:, 0, 0:L], hT, PN[:, 256:256 + L], SUB)
            cc = W["cc"][:, 0, 0:L]
        else:
            ACT(W["lnv"][:, 0, 0:L], PN[:, 384:384 + L], AF.Ln, bias=EPS)
            cc = hT
        ACT(W["rs"][:, 0, 0:L], W["lnv"][:, 0, 0:L], AF.Exp, scale=-0.5)
        STT(W["yy"][:, 0, 0:L], cc, gaincol, W["rs"][:, 0, 0:L], MUL, MUL)
        TT("pool", out, W["yy"][:, 0, 0:L], gate, MUL)

    def outproj(t, Wout, SB_):
        c0, n = t["c0"], t["n"]
        pacc = [PB[0], PB[1], PB[2], PB[5]]
        for d in range(8):
            pa = pacc[d % 4]
            MMK(pa[:, 0:n], [(Wout[:, kk, d * 128:(d + 1) * 128], SB_["mixed"][:, kk, 0:n]) for kk in range(4)])
            TT("dve", xT[:, d, c0:c0 + n], xT[:, d, c0:c0 + n], pa[:, 0:n], ADD)

    LN_F = [("hT", (1, 128)), ("hsq", (1, 128)), ("mu", (1, 128)), ("msq", (1, 128)), ("var", (1, 128)),
            ("lnv", (1, 128)), ("rs", (1, 128)), ("cc", (1, 128)), ("yy", (1, 128))]
    X_F = [("q0", (2, 512)), ("qn", (2, 512)), ("n0s", (4, 16)), ("nnew", (1, 16))]
    SH_B = [("Vtok", (4, 512)), ("mixed", (4, 512))]
    LN_B = [("qT", (1, 512)), ("kT", (1, 512)), ("gate", (1, 512)), ("Khat", (1, 512)),
            ("SD", (1, 128)), ("qp", (1, 128)), ("Kh", (1, 128)), ("Vb", (4, 128))]
    X_B = [("s0b", (2, 512)), ("nbq", (2, 512))]

    def vtok_proj(t, Wv, SB_, banks=None):
        for gi, (gt, off, L) in enumerate(t["groups"]):
            pv = (banks or PA)[gi % 2]
            MMK(pv[0:L, 0:512], [(xn[:, k, t["c0"] + off:t["c0"] + off + L], Wv(k)) for k in range(8)])
            CP("act", SB_["Vtok"][0:L, gi, :], pv[0:L, 0:512])

    def pipeline_groups(ng, pro, main, tail):
        pro(0)
        for gi in range(ng):
            main(gi)
            if gi + 1 < ng:
                pro(gi + 1)
            tail(gi)

    def run_heads(t, nheads, body, extra=None):
        nl = 1 if t["kind"] == "A" else NLANES
        for h0 in range(0, nheads, nl):
            caps = []
            if extra is not None and h0 == 0:
                tr.begin_capture()
                extra()
                caps.append(tr.end_capture())
            for ln in range(min(nl, nheads - h0)):
                tr.begin_capture()
                body(h0 + ln, ln)
                caps.append(tr.end_capture())
            if SEQ_FLUSH:
                for c in caps:
                    tr.flush_rr([c])
            else:
                tr.flush_rr(caps)

    def sample_quarters(S, SB_, P, Vb, h, gi, st_in, st_out, qsrc, Kh, PDen, nidx, scale_ap, scale_imm, Vb2=None):
        hb = slice(h * 128, (h + 1) * 128)
        PN = P["N"]
        PUs = [P["Z"], LPS[1]["Z"]]
        Vbs = [Vb, Vb2] if Vb2 is not None else [Vb, Vb]
        for qd in range(4):
            b = qd % 2
            PU = PUs[b]
            Vb = Vbs[b]
            q0 = S["q0"][:, b, :]
            q0v = q0.rearrange("p (a b) -> p a b", a=4)
            DMA("sp", q0v, st_in[:, 4 * qd:4 * qd + 4, :])
            s0b = SB_["s0b"][:, b, :].rearrange("p (a b) -> p a b", a=4)
            CP("act", s0b, q0v)
            if nidx is not None:
                nbq = SB_["nbq"][:, b, :].rearrange("p (a b) -> p a b", a=4)
                CP("pool", nbq, bc(S["n0s"][:, nidx, 4 * qd:4 * qd + 4].unsqueeze(2), (128, 4, 128)))
            for jj in range(4):
                j = 4 * qd + jj
                sc = slice(8 * j, 8 * j + 8)
                MM(PN[:, sc], s0b[:, jj, :], qsrc(sc), start=False, stop=(j == 15))
                if nidx is not None:
                    MM(PDen[:, sc], nbq[:, jj, :], qsrc(sc), start=False, stop=(j == 15))
            TT("pool", Vb, bc(SB_["Vtok"][:, gi, hb].unsqueeze(1), (128, 4, 128)),
               bc(cst[:, C_SEG16 + 4 * qd:C_SEG16 + 4 * qd + 4].unsqueeze(2), (128, 4, 128)), MUL)
            MM(PU[:, 0:512], Kh, Vb.rearrange("p a b -> p (a b)"))
            qn = S["qn"][:, b, :]
            if scale_ap is not None:
                TT("pool", qn.rearrange("p (a b) -> p a b", a=4), q0v,
                   bc(scale_ap[:, 4 * qd:4 * qd + 4].unsqueeze(2), (128, 4, 128)), MUL)
                TT("dve", qn, qn, PU[:, 0:512], ADD)
            else:
                STT(qn, q0, scale_imm, PU[:, 0:512], MUL, ADD)
            DMA("sp", st_out[:, 4 * qd:4 * qd + 4, :], qn.rearrange("p (a b) -> p a b", a=4))

    def pass_m_tile(t, Wm, Wout):
        c0, n = t["c0"], t["n"]
        S, LW = carve_lanes(wkf,
                            [("li", (1, 512)), ("ef", (1, 512)), ("sp", (1, 512)), ("R4", (4, 400)), ("cs", (4, 128)),
                             ("a2", (1, 128)), ("AW", (2, 128)), ("m0s", (1, 16)), ("mns", (1, 16)), ("acol", (4, 4)),
                             ("wcol", (4, 4))],
                            LN_F + [("tmp", (1, 128)), ("Dm", (1, 128)), ("ibc", (1, 128)), ("e2", (1, 128)),
                                    ("dmax", (1, 128)), ("rden", (1, 128)), ("absd", (1, 128)), ("carry", (1, 16))],
                            X_F)
        SB_, LB = carve_lanes(wkb, SH_B, LN_B, X_B)
        xk = lambda k: xn[:, k, c0:c0 + n]
        PG0 = LPS[0]["G"]
        MMK(PA[0][0:4, 0:n], [(Wm[:, k, 2048:2052], xk(k)) for k in range(8)])
        MMK(PA[1][0:4, 0:n], [(Wm[:, k, 2052:2056], xk(k)) for k in range(8)])
        li, ef, sp_ = S["li"][0:4, 0, :], S["ef"][0:4, 0, :], S["sp"][0:4, 0, :]
        MEMSET("dve", S["R4"][0:4, :, :], 0.0)
        ACT(li[:, 0:n], PA[0][0:4, 0:n], AF.Identity, bias=prm4s[:, 0:1])
        ACT(ef[:, 0:n], PA[1][0:4, 0:n], AF.Exp, bias=prm4s[:, 2:3], scale=-1.0)
        ACT(sp_[:, 0:n], ef[:, 0:n], AF.Ln, bias=1.0)
        vt_m = lambda: vtok_proj(t, lambda k: Wm[:, k, 1024:1536], SB_, [LPS[0]["N"], LPS[1]["N"]])
        fast_rows = (t["kind"] == "B") and FAST_ROWS
        if fast_rows:
            m0g = S["m0s"][0:4, 0, 0:8]
            Abuf = S["ef"][0:4, 0, :]
            A2buf = sp_
            WRbuf = li
            nsP = crow[:, R_NSP:R_NSP + 128]
            rsP = crow[:, R_RSP:R_RSP + 128]
            G4 = [(gi, off) for gi, (gt, off, L) in enumerate(t["groups"])]
            for gi, off in G4:
                SCAN(S["cs"][0:4, gi, 0:128], nsP, sp_[:, off:off + 128], 0.0, MUL, ADD)
            for gi, off in G4:
                TT("dve", Abuf[:, off:off + 128], li[:, off:off + 128], S["cs"][0:4, gi, 0:128], ADD)
            CP("dve", m0g[:, 0:1], mstate[:, 0:1])
            for gi, off in G4:
                R = S["R4"][0:4, gi, :]
                TS("dve", A2buf[:, off:off + 128], Abuf[:, off:off + 128], m0g[:, gi:gi + 1], MAX)
                SCAN(R[:, 0:128], rsP, A2buf[:, off:off + 128], -1e30, ADD, MAX)
                TT("dve", m0g[:, gi + 1:gi + 2], R[:, 127:128], S["cs"][0:4, gi, 127:128], SUB)
            CP("dve", mstate[:, 0:1], m0g[:, 4:5])
            for gi, off in G4:
                R = S["R4"][0:4, gi, :]
                TS("dve", R[:, 128:256], R[:, 0:128], -1.0, MUL, m0g[:, gi:gi + 1], ADD)
                TS("dve", R[:, 384:385], R[:, 127:128], -1.0, MUL, m0g[:, gi:gi + 1], ADD)
                TS("dve", WRbuf[:, off:off + 128], Abuf[:, off:off + 128], R[:, 127:128], SUB)
                TT("dve", R[:, 256:384], S["cs"][0:4, gi, 0:128], R[:, 0:128], SUB)
            for gi, off in G4:
                TP(PG0[0:128, 400:404], Abuf[:, off:off + 128], ident4)
                CP("act", S["acol"][0:128, gi, :], PG0[0:128, 400:404])
                TP(LPS[1]["G"][0:128, 404:408], WRbuf[:, off:off + 128], ident4)
                ACT(S["wcol"][0:128, gi, :], LPS[1]["G"][0:128, 404:408], AF.Exp)
        for gi, (gt, off, L) in enumerate(t["groups"]):
            if fast_rows:
                break
            isS = gt == "S"
            nseg, Ls = (16, 8) if isS else (1, L)
            R = S["R4"][0:4, gi, :]
            cs = S["cs"][0:4, gi, 0:L]
            a2 = S["a2"][0:4, 0, 0:L]
            a = S["AW"][0:4, 0, 0:L]
            wr = S["AW"][0:4, 1, 0:L]
            ns = crow[:, (R_NSS if isS else R_NSP):(R_NSS if isS else R_NSP) + L]
            rs = crow[:, (R_RSS if isS else R_RSP):(R_RSS if isS else R_RSP) + L]
            SCAN(cs, ns, sp_[:, off:off + L], 0.0, MUL, ADD)
            TT("dve", a, li[:, off:off + L], cs, ADD)
            Mx = R[:, 0:L]
            if isS:
                m0s = S["m0s"][0:4, 0, :]
                DMA("sp", m0s, stm)
                v3 = lambda ap: ap.rearrange("p (a b) -> p a b", a=16)
                m0b = bc(m0s.unsqueeze(2), (4, 16, 8))
                TT("dve", v3(a2), v3(a), m0b, MAX)
            else:
                TS("dve", a2, a, mstate[:, 0:1], MAX)
            SCAN(Mx, rs, a2, -1e30, ADD, MAX)
            Ml = R[:, Ls - 1:L:Ls]
            csl = cs[:, Ls - 1:L:Ls]
            if isS:
                TT("dve", v3(R[:, 128:128 + L]), m0b, v3(Mx), SUB)
                TT("dve", R[:, 384:400], m0s, Ml, SUB)
                TT("dve", v3(wr), v3(a), bc(Ml.unsqueeze(2), (4, 16, 8)), SUB)
                mns = S["mns"][0:4, 0, :]
                TT("dve", mns, Ml, csl, SUB)
                DMA("sp", om_s, mns)
            else:
                TS("dve", R[:, 128:128 + L], Mx, -1.0, MUL, mstate[:, 0:1], ADD)
                TS("dve", R[:, 384:385], Ml, -1.0, MUL, mstate[:, 0:1], ADD)
                TS("dve", wr, a, Ml, SUB)
            TT("dve", R[:, 256:256 + L], cs, Mx, SUB)
            if not isS:
                TT("dve", mstate[:, 0:1], Ml, csl, SUB)
            TP(PG0[0:L, 400:404], a, ident4)
            TP(PG0[0:L, 404:408], wr, ident4)
            CP("act", S["acol"][0:L, gi, :], PG0[0:L, 400:404])
            ACT(S["wcol"][0:L, gi, :], PG0[0:L, 404:408], AF.Exp)
        if t["kind"] == "A":
            DMA("sp", S["n0s"][:, :, :], stn)

        def body(h, ln):
            PAl = PA[ln]
            W, B, P = LW[ln], LB[ln], LPS[ln]
            PG, PN, PZ, PT = P["G"], P["N"], P["Z"], P["T"]
            hb = slice(h * 128, (h + 1) * 128)
            qT, kT, gate = B["qT"][:, 0, :], B["kT"][:, 0, :], B["gate"][:, 0, :]
            MMK(PAl[:, 0:n], [(Wm[:, k, h * 128:(h + 1) * 128], xk(k)) for k in range(8)])
            CP("act", qT[:, 0:n], PAl[:, 0:n])
            MMK(PAl[:, 0:n], [(Wm[:, k, 512 + h * 128:512 + (h + 1) * 128], xk(k)) for k in range(8)])
            ACT(kT[:, 0:n], PAl[:, 0:n], AF.Copy, scale=128.0 ** -0.5)
            MMK(PAl[:, 0:n], [(Wm[:, k, 1536 + h * 128:1536 + (h + 1) * 128], xk(k)) for k in range(8)])
            ACT(gate[:, 0:n], PAl[:, 0:n], AF.Sigmoid)
            def pro(gi):
                gt, off, L = t["groups"][gi]
                isS = gt == "S"
                nseg = 16 if isS else 1
                gc = slice(off, off + L)
                R = S["R4"][0:4, gi, :]
                MM(PG[:, 0:400], sel4(h), R)
                negm = cst[0:L, (C_NEG8 if isS else C_NEGC):(C_NEG8 if isS else C_NEGC) + L]
                tmp, Dm = W["tmp"][0:L, 0, 0:L], W["Dm"][0:L, 0, 0:L]
                TT("dve", tmp, negm, PG[0:L, 0:L], SUB)
                ACT(Dm, tmp, AF.Exp, bias=S["acol"][0:L, gi, h:h + 1])
                MM(PZ[0:L, 0:L], kT[:, gc], qT[:, gc])
                SD = B["SD"][0:L, 0, 0:L]
                TT("dve", SD, PZ[0:L, 0:L], Dm, MUL)
                ibc, e2, carry = W["ibc"][:, 0, 0:L], W["e2"][:, 0, 0:L], W["carry"][:, 0, 0:nseg]
                ACT(ibc, PG[:, 128:128 + L], AF.Exp)
                qp = B["qp"][:, 0, 0:L]
                TT("pool", qp, qT[:, gc], ibc, MUL)
                ACT(e2, PG[:, 256:256 + L], AF.Exp)
                ACT(carry, PG[:, 384:384 + nseg], AF.Exp)
                TP(PT[0:L, 0:128], kT[:, gc], ident_b[:])
                Kh = B["Kh"][0:L, 0, :]
                TS("dve", Kh, PT[0:L, 0:128], S["wcol"][0:L, gi, h:h + 1], MUL)

            def main(gi):
                gt, off, L = t["groups"][gi]
                isS = gt == "S"
                nseg = 16 if isS else 1
                SD = B["SD"][0:L, 0, 0:L]
                qp = B["qp"][:, 0, 0:L]
                Kh = B["Kh"][0:L, 0, :]
                e2, carry = W["e2"][:, 0, 0:L], W["carry"][:, 0, 0:nseg]
                if not isS:
                    MM(PN[:, 0:L], SB_["Vtok"][0:L, gi, hb], SD, start=True, stop=False)
                    MM(PN[:, 0:L], Cb[:, h, :], qp, start=False, stop=True)
                    MM(PN[:, 128:128 + L], ones_b[0:L, :], SD, start=True, stop=False)
                    MM(PN[:, 128:128 + L], nbc[:, h, :], qp, start=False, stop=True)
                    PDen = PN[:, 128:128 + L]
                    MM(PZ[:, 128:256], Kh, SB_["Vtok"][0:L, gi, hb])
                    MM(PZ[:, 256:257], Kh, ones_b[0:L, 0:1])
                    STT(Cb[:, h, :], Cst[:, h, :], carry[:, 0:1], PZ[:, 128:256], MUL, ADD)
                    STT(Cst[:, h, :], Cst[:, h, :], carry[:, 0:1], PZ[:, 128:256], MUL, ADD)
                    TS("dve", nbc[:, h, :], bc(nst[:, h:h + 1], (128, 128)), carry[:, 0:1], MUL, PZ[:, 256:257], ADD)
                    STT(nst[:, h:h + 1], nst[:, h:h + 1], carry[:, 0:1], PZ[:, 256:257], MUL, ADD)
                else:
                    PDb = LPS[1]["N"]
                    MM(PN[:, 0:L], SB_["Vtok"][0:L, gi, hb], SD, start=True, stop=False)
                    MM(PDb[:, 0:L], ones_b[0:L, :], SD, start=True, stop=False)
                    sample_quarters(S, SB_, P, B["Vb"][:, :, :], h, gi, stC[:, h, :, :], oC_s[:, h, :, :], lambda sc: qp[:, sc], Kh,
                                    PDb, h, carry, None, Vb2=LB[1]["Vb"][:, :, :])
                    PDen = PDb[:, 0:L]
                    MM(PG[:, 408:424], Kh, segm_b[:, 0:16])
                    nn = S["nnew"][:, 0, :]
                    TT("pool", nn, S["n0s"][:, h, :], carry, MUL)
                    TT("dve", nn, nn, PG[:, 408:424], ADD)
                    DMA("sp", on_s[:, h, :], nn)
                dmax, rden, hT = W["dmax"][:, 0, 0:L], W["rden"][:, 0, 0:L], W["hT"][:, 0, 0:L]
                ACT(W["absd"][:, 0, 0:L], PDen, AF.Abs)
                TT("dve", dmax, W["absd"][:, 0, 0:L], e2, MAX)
                RECIP(rden, dmax)
                TT("dve", hT, PN[:, 0:L], rden, MUL)

            def tail(gi):
                gt, off, L = t["groups"][gi]
                gc = slice(off, off + L)
                ln_gate(W, P, L, prms[:, P_GNM + h:P_GNM + h + 1], gate[:, gc], SB_["mixed"][:, h, gc], True)

            pipeline_groups(len(t["groups"]), pro, main, tail)

        run_heads(t, 4, body, vt_m)
        outproj(t, Wout, SB_)

    def pass_r_tile(t, Wr, Wout):
        c0, n = t["c0"], t["n"]
        S, LW = carve_lanes(wkf, [("rt", (2, 512))], LN_F + [("t1", (1, 512)), ("t2", (1, 512))], X_F)
        SB_, LB = carve_lanes(wkb, SH_B, LN_B, X_B)
        xk = lambda k: xn[:, k, c0:c0 + n]
        rt = S["rt"]
        DMA("sp", rt[:, :, 0:n], rot[:, :, t["g0"]:t["g0"] + n])
        vt_r = lambda: vtok_proj(t, lambda k: Wr[:, k, 1024:1536], SB_, [PB[2], PB[5]])

        def body(h, ln):
            PAl = PA[ln]
            W, B, P = LW[ln], LB[ln], LPS[ln]
            PN, PZ, PT = P["N"], P["Z"], P["T"]
            hb = slice(h * 128, (h + 1) * 128)
            qr, kr, gate = B["qT"][:, 0, :], B["kT"][:, 0, :], B["gate"][:, 0, :]
            t1, t2 = W["t1"][:, 0, 0:n], W["t2"][:, 0, 0:n]
            for (o1, o2, dst, sc_) in ((0, 2048, qr, 1.0), (512, 2560, kr, 128.0 ** -0.5)):
                MMK(PAl[:, 0:n], [(Wr[:, k, o1 + h * 128:o1 + (h + 1) * 128], xk(k)) for k in range(8)])
                STT(t1, PAl[:, 0:n], sc_, rt[:, 0, 0:n], MUL, MUL)
                MMK(PAl[:, 0:n], [(Wr[:, k, o2 + h * 128:o2 + (h + 1) * 128], xk(k)) for k in range(8)])
                STT(t2, PAl[:, 0:n], sc_, rt[:, 1, 0:n], MUL, MUL)
                TT("pool", dst[:, 0:n], t1, t2, ADD)
            MMK(PAl[:, 0:n], [(Wr[:, k, 1536 + h * 128:1536 + (h + 1) * 128], xk(k)) for k in range(8)])
            ACT(gate[:, 0:n], PAl[:, 0:n], AF.Silu)
            def pro(gi):
                gt, off, L = t["groups"][gi]
                isS = gt == "S"
                ty = 1 if isS else 0
                tix = 2 if isS else (0 if L == 128 else 1)
                gc = slice(off, off + L)
                MM(PZ[0:L, 0:L], kr[:, gc], qr[:, gc])
                dec = cst[0:L, C_DEC + (ty * 4 + h) * 128:C_DEC + (ty * 4 + h) * 128 + L]
                inn = cst[:, C_INN + (ty * 4 + h) * 128:C_INN + (ty * 4 + h) * 128 + L]
                SD = B["SD"][0:L, 0, 0:L]
                TT("dve", SD, PZ[0:L, 0:L], dec, MUL)
                qp = B["qp"][:, 0, 0:L]
                TT("pool", qp, qr[:, gc], inn, MUL)
                TP(PT[0:L, 0:128], kr[:, gc], ident_b[:])
                Kh = B["Kh"][0:L, 0, :]
                TS("dve", Kh, PT[0:L, 0:128], cst[0:L, C_TAIL + tix * 4 + h:C_TAIL + tix * 4 + h + 1], MUL)

            def Pg(gi):
                if not ALT_N:
                    return P
                return dict(P, N=(P["N"] if gi % 2 == 0 else P["G"]))

            def main(gi):
                gt, off, L = t["groups"][gi]
                isS = gt == "S"
                tix = 2 if isS else (0 if L == 128 else 1)
                SD = B["SD"][0:L, 0, 0:L]
                qp = B["qp"][:, 0, 0:L]
                Kh = B["Kh"][0:L, 0, :]
                PN = Pg(gi)["N"]
                MM(PN[:, 0:L], SB_["Vtok"][0:L, gi, hb], SD, start=True, stop=False)
                if not isS:
                    MM(PN[:, 0:L], Rb[:, h, :], qp, start=False, stop=True)
                    MM(PZ[:, 128:256], Kh, SB_["Vtok"][0:L, gi, hb])
                    STT(Rb[:, h, :], Rst[:, h, :], gl[tix][h], PZ[:, 128:256], MUL, ADD)
                    STT(Rst[:, h, :], Rst[:, h, :], gl[tix][h], PZ[:, 128:256], MUL, ADD)
                else:
                    sample_quarters(S, SB_, Pg(gi), B["Vb"][:, :, :], h, gi, stR[:, h, :, :], oR_s[:, h, :, :], lambda sc: qp[:, sc], Kh,
                                    None, None, None, gl[2][h], Vb2=LB[1]["Vb"][:, :, :])
                ACT(W["hsq"][:, 0, 0:L], PN[:, 0:L], AF.Square)
                CP("act", W["hT"][:, 0, 0:L], PN[:, 0:L])

            def tail(gi):
                gt, off, L = t["groups"][gi]
                gc = slice(off, off + L)
                ln_gate(W, Pg(gi), L, prms[:, P_GNR + h:P_GNR + h + 1], gate[:, gc], SB_["mixed"][:, h, gc], True, sq_src=True)

            pipeline_groups(len(t["groups"]), pro, main, tail)

        run_heads(t, 4, body, vt_r)
        outproj(t, Wout, SB_)

    def pass_h_tile(t, Wh, Wout, hp):
        c0, n = t["c0"], t["n"]
        S, LW = carve_lanes(wkf, [], [("hT", (1, 128)), ("hsq", (1, 128)), ("lnv", (1, 128)), ("rs", (1, 128)),
                                      ("yy", (1, 128)), ("sg", (1, 512)), ("lf", (1, 512)), ("kk", (1, 512)),
                                      ("bb", (1, 512)), ("eb", (1, 512)), ("enb", (1, 512))], X_F)
        SB_, LB = carve_lanes(wkb, SH_B, LN_B, X_B)
        xk = lambda k: xn[:, k, c0:c0 + n]
        vt_h = lambda: vtok_proj(t, lambda k: Wh[:, k, 1024:1536], SB_, [PB[2], PB[5]])
        nsH = cst[:, (C_NSA if t["kind"] == "A" else C_NSB):(C_NSA if t["kind"] == "A" else C_NSB) + n]

        def body(hl, ln):
            PAl = PA[ln]
            W, B, P = LW[ln], LB[ln], LPS[ln]
            PN, PZ, PT = P["N"], P["Z"], P["T"]
            hg = hp * 4 + hl
            hb = slice(hl * 128, (hl + 1) * 128)
            Qt, Kt, gate, Khat = B["qT"][:, 0, :], B["kT"][:, 0, :], B["gate"][:, 0, :], B["Khat"][:, 0, :]
            sg, lf, kk, bb, eb, enb = (W[x][:, 0, 0:n] for x in ("sg", "lf", "kk", "bb", "eb", "enb"))
            MMK(PAl[:, 0:n], [(Wh[:, k, 512 + hl * 128:512 + (hl + 1) * 128], xk(k)) for k in range(8)])
            ACT(sg, PAl[:, 0:n], AF.Sigmoid)
            MMK(PAl[:, 0:n], [(Wh[:, k, hl * 128:(hl + 1) * 128], xk(k)) for k in range(8)])
            ACT(lf, sg, AF.Ln, bias=lbv[:, hg:hg + 1], scale=lbv[:, 8 + hg:9 + hg])
            TS("pool", kk, sg, lbv[:, 16 + hg:17 + hg], MUL, lbv[:, 8 + hg:9 + hg], ADD)
            SCAN(bb, nsH, lf, 0.0, MUL, ADD)
            ACT(eb, bb, AF.Exp)
            ACT(enb, bb, AF.Exp, scale=-1.0)
            TT("dve", Qt[:, 0:n], PAl[:, 0:n], eb, MUL)
            MMK(PAl[:, 0:n], [(Wh[:, k, 1536 + hl * 128:1536 + (hl + 1) * 128], xk(k)) for k in range(8)])
            TT("pool", Kt[:, 0:n], kk, enb, MUL)
            v3 = lambda ap: ap.rearrange("p (a b) -> p a b", a=16)
            if t["kind"] == "B":
                vh = lambda ap: ap.rearrange("p (a b) -> p a b", a=512 // HSEG)
                TT("pool", vh(Khat[:, 0:512]), vh(Kt[:, 0:512]),
                   bc(eb[:, HSEG - 1:512:HSEG].unsqueeze(2), (128, 512 // HSEG, HSEG)), MUL)
            else:
                TT("pool", v3(Khat[:, 0:128]), v3(Kt[:, 0:128]), bc(eb[:, 7:128:8].unsqueeze(2), (128, 16, 8)), MUL)
                TT("pool", Khat[:, 128:144], Kt[:, 128:144], bc(eb[:, 143:144], (128, 16)), MUL)
            ACT(gate[:, 0:n], PAl[:, 0:n], AF.Silu)
            def gparams(gi):
                gt, off, L = t["groups"][gi]
                isS = gt == "S"
                nseg, Ls = (16, 8) if isS else ((128 // HSEG, HSEG) if L == 128 else (1, L))
                return gt, off, L, isS, nseg, Ls

            def pro(gi):
                gt, off, L, isS, nseg, Ls = gparams(gi)
                gc = slice(off, off + L)
                MM(PZ[0:L, 0:L], Kt[:, gc], Qt[:, gc])
                msk = cst[0:L, (C_M8 if isS else C_M32):(C_M8 if isS else C_M32) + L]
                SD = B["SD"][0:L, 0, 0:L]
                TT("dve", SD, PZ[0:L, 0:L], msk, MUL)
                TP(PT[0:L, 0:128], Khat[:, gc], ident_b[:])
                Kh = B["Kh"][0:L, 0, :]
                CP("act", Kh, PT[0:L, 0:128])
                if (not isS) and nseg > 1:
                    Vb = B["Vb"][:, 0:nseg, :]
                    TT("pool", Vb, bc(SB_["Vtok"][:, gi, hb].unsqueeze(1), (128, nseg, 128)),
                       bc(cst[:, C_SEG4:C_SEG4 + nseg].unsqueeze(2), (128, nseg, 128)), MUL)

            def Pg(gi):
                if not ALT_N:
                    return P
                return dict(P, N=(P["N"] if gi % 2 == 0 else P["G"]))

            def main(gi):
                gt, off, L, isS, nseg, Ls = gparams(gi)
                SD = B["SD"][0:L, 0, 0:L]
                Kh = B["Kh"][0:L, 0, :]
                G = eb[:, off + Ls - 1:off + L:Ls]
                PN = Pg(gi)["N"]
                MM(PN[:, 0:L], SB_["Vtok"][0:L, gi, hb], SD, start=True, stop=False)
                if not isS:
                    if nseg > 1:
                        Vb = B["Vb"][:, 0:nseg, :]
                        MM(PZ[:, 0:nseg * 128], Kh, Vb.rearrange("p a b -> p (a b)"))
                        uo = 0
                    else:
                        MM(PZ[:, 128:256], Kh, SB_["Vtok"][0:L, gi, hb])
                        uo = 128
                    for j in range(nseg):
                        MM(PN[:, j * Ls:(j + 1) * Ls], Hb[:, hg, :], Qt[:, off + j * Ls:off + (j + 1) * Ls],
                           start=False, stop=(j == nseg - 1))
                        STT(Hb[:, hg, :], Hst[:, hg, :], G[:, j:j + 1], PZ[:, uo + j * 128:uo + (j + 1) * 128], MUL, ADD)
                        STT(Hst[:, hg, :], Hst[:, hg, :], G[:, j:j + 1], PZ[:, uo + j * 128:uo + (j + 1) * 128], MUL, ADD)
                else:
                    sample_quarters(S, SB_, Pg(gi), B["Vb"][:, :, :], hl, gi, stH[:, hg, :, :], oH_s[:, hg, :, :],
                                    lambda sc: Qt[:, sc], Kh, None, None, G, None, Vb2=LB[1]["Vb"][:, :, :])
                ACT(W["hsq"][:, 0, 0:L], PN[:, 0:L], AF.Square)
                CP("act", W["hT"][:, 0, 0:L], PN[:, 0:L])

            def tail(gi):
                gt, off, L, isS, nseg, Ls = gparams(gi)
                gc = slice(off, off + L)
                ln_gate(W, Pg(gi), L, prms[:, P_GNH + hg:P_GNH + hg + 1], gate[:, gc], SB_["mixed"][:, hl, gc], False, sq_src=True)

            pipeline_groups(len(t["groups"]), pro, main, tail)

        run_heads(t, 4, body, vt_h)
        outproj(t, Wout, SB_)

    ffn_state = {"pending": None, "par": 0}

    def pass_f_tile(t, Win, Wo, layer, f0, nf, half_idx):
        c0, n = t["c0"], t["n"]
        W, _ = carve(wkf, [("ue", (2, 520)), ("c1", (2, 512)), ("c2", (2, 512)), ("c3", (2, 512)), ("sl", (2, 512)),
                           ("cvin", (NF, 32))])
        WB_, _ = carve(wkb, [("hT", (10, 512))])
        hsel = ffn_state["par"] * 5
        ffn_state["par"] ^= 1
        isA = t["kind"] == "A"
        xk = lambda k: xn[:, k, c0:c0 + n]
        if isA and f0 == 0:
            DMA("sp", W["cvin"][:, :, :], stcv[:, layer, :, :])
        for fi in range(nf):
            f = f0 + fi
            PU_, PG_ = (PB[0], PB[1]) if fi % 2 == 0 else (PB[2], PB[3])
            cw = lambda j: prms[:, P_CW + (layer * 3 + j) * NF + f:P_CW + (layer * 3 + j) * NF + f + 1]
            cb = prms[:, P_CB + layer * NF + f:P_CB + layer * NF + f + 1]
            MMK(PU_[:, 0:n], [(Win[:, k, fi * 256:fi * 256 + 128], xk(k)) for k in range(8)])
            MMK(PG_[:, 0:n], [(Win[:, k, fi * 256 + 128:fi * 256 + 256], xk(k)) for k in range(8)])
            ue = W["ue"][:, fi % 2, :]
            c1, c2, c3, sl = (W[x][:, fi % 2, :] for x in ("c1", "c2", "c3", "sl"))
            tb = tailbuf[:, layer, f, :]
            if not isA:
                CP("pool", ue[:, 0:2], tb)
                CP("act", ue[:, 2:2 + n], PU_[:, 0:n])
                ACT(c1[:, 0:n], ue[:, 0:n], AF.Identity, bias=cb, scale=cw(0))
                STT(c2[:, 0:n], ue[:, 1:n + 1], cw(1), c1[:, 0:n], MUL, ADD)
                STT(c3[:, 0:n], PU_[:, 0:n], cw(2), c2[:, 0:n], MUL, ADD)
                CP("pool", tb, ue[:, n:n + 2])
            else:
                ues = ue[:, 0:160].rearrange("p (a b) -> p a b", a=16)
                v3 = lambda ap: ap.rearrange("p (a b) -> p a b", a=16)
                CP("pool", ues[:, :, 0:2], W["cvin"][:, f, :].rearrange("p (a b) -> p a b", a=16))
                MEMSET("pool", ue[:, 160:162], 0.0)
                CP("act", ues[:, :, 2:10], v3(PU_[:, 0:128]))
                CP("act", ue[:, 162:178], PU_[:, 128:144])
                ACT(v3(c1[:, 0:128]), ues[:, :, 0:8], AF.Identity, bias=cb, scale=cw(0))
                ACT(c1[:, 128:144], ue[:, 160:176], AF.Identity, bias=cb, scale=cw(0))
                STT(v3(c2[:, 0:128]), ues[:, :, 1:9], cw(1), v3(c1[:, 0:128]), MUL, ADD)
                STT(c2[:, 128:144], ue[:, 161:177], cw(1), c1[:, 128:144], MUL, ADD)
                STT(c3[:, 0:n], PU_[:, 0:n], cw(2), c2[:, 0:n], MUL, ADD)
                CP("pool", tb, ue[:, 176:178])
                CP("pool", cvout[:, f, :].rearrange("p (a b) -> p a b", a=16), ues[:, :, 8:10])
            ACT(sl[:, 0:n], c3[:, 0:n], AF.Silu)
            TT("dve", WB_["hT"][:, hsel + fi, 0:n], sl[:, 0:n], PG_[:, 0:n], MUL)
            if fi == 0 and ffn_state["pending"] is not None:
                ffn_state["pending"]()
                ffn_state["pending"] = None

        def do_out(hsel=hsel, n=n, c0=c0):
            pacc = [PB[4], PB[5], PB[6], PB[7]]
            for d in range(8):
                pa = pacc[d % 4]
                MMK(pa[:, 0:n], [(Wo[:, fi, d * 128:(d + 1) * 128], WB_["hT"][:, hsel + fi, 0:n]) for fi in range(nf)])
                TT("dve", xT[:, d, c0:c0 + n], xT[:, d, c0:c0 + n], pa[:, 0:n], ADD)
        ffn_state["pending"] = do_out
        if isA and f0 + nf == NF:
            DMA("sp", ocv_s[:, layer, :, :], cvout[:, :, :])

    FG = [(0, 5), (5, 5), (10, 4), (14, 4), (18, 4)]
    halves = _halves()
    passes = []
    for hi, tiles in enumerate(halves):
        for layer in range(2):
            if layer == 0:
                passes.append(("m", hi, layer, [w_ab[:, :, 0:2056], w_oab[:, 0:4, :]]))
                passes.append(("r", hi, layer, [w_ab[:, :, 2056:5128], w_oab[:, 4:8, :]]))
            else:
                passes.append(("h0", hi, layer, [w_c[:, :, 0, :], w_oc[:, 0:4, :]]))
                passes.append(("h1", hi, layer, [w_c[:, :, 1, :], w_oc[:, 4:8, :]]))
            for (f0, nf) in FG:
                passes.append(("f", hi, layer, [w_fi[layer][:, :, f0 * 256:(f0 + nf) * 256], w_fo[layer][:, f0:f0 + nf, :]], f0, nf))
    if ONLY is not None:
        passes = [p for p in passes if p[0] in ONLY]
    passes = [p for p in passes if p[1] in HALVES]
    npass = len(passes) if STAGE >= 99 else min(len(passes), STAGE)
    loaded = {}

    def ensure_loaded(i, prev_region):
        if i >= npass or i in loaded:
            return
        r = load_weights(passes[i][3], prev_region)
        if r is not None:
            loaded[i] = r

    for i in range(npass):
        p = passes[i]
        kind, hi, layer = p[0], p[1], p[2]
        tiles = halves[hi]
        if i not in loaded:
            ensure_loaded(i, None)
        views, region = loaded[i]
        ensure_loaded(i + 1, region)
        if kind in ("m", "h0") or (ONLY is not None and "m" not in ONLY and kind == "r"):
            if layer == 0 and (kind == "m" or (ONLY is not None and "m" not in ONLY)):
                DMA("sp", xT[:, :, 0:(HC if hi == 0 else 1024)], xT_d[:, :, (0 if hi == 0 else HC):(HC if hi == 0 else NT)])
            for t in tiles:
                norm_tile(t, layer, False)
        if kind == "f" and p[4] == 0:
            for t in tiles:
                norm_tile(t, 2 + layer, False)
        for t in tiles:
            if kind == "m":
                pass_m_tile(t, views[0], views[1])
            elif kind == "r":
                pass_r_tile(t, views[0], views[1])
            elif kind in ("h0", "h1"):
                pass_h_tile(t, views[0], views[1], int(kind[1]))
            else:
                pass_f_tile(t, views[0], views[1], layer, p[4], p[5], hi)
        if kind == "f" and ffn_state["pending"] is not None:
            ffn_state["pending"]()
            ffn_state["pending"] = None
        ensure_loaded(i + 1, None)
        last_of_half = (i + 1 == len(passes)) or (passes[i + 1][1] != hi) or (i + 1 == npass)
        if last_of_half:
            for t in tiles:
                norm_tile(t, 4, True)
    DMA("sp", oC_p, Cst[:])
    DMA("sp", on_p, nst[:])
    DMA("sp", om_p, mstate[:, 0:1])
    DMA("sp", oR_p, Rst[:])
    DMA("sp", oH_p, Hst[:])
    DMA("sp", ocv_p, tailbuf[:])
    tr.finish()
    tr.emit()
    for cm in reversed(ctxs):
        cm.__exit__(None, None, None)
    tr.close()
    return nc, tr


def _prep(inputs):
    I = {k: np.asarray(v) for k, v in inputs.items()}
    c128, c4, rot, gl = _consts()
    wab = _wl(I["w_in_ab"][0])
    sw = []
    for base in (2056, 2568):
        for h in range(4):
            o = base + h * 128
            sw.append(wab[:, :, o + 64:o + 128])
            sw.append(wab[:, :, o:o + 64])
    w_ab = np.ascontiguousarray(np.concatenate([wab] + sw, axis=2))
    wc = _wl(I["w_in_c"][0])
    parts = []
    for hp in range(2):
        parts.append(np.concatenate([wc[:, :, j * 1024 + hp * 512:j * 1024 + (hp + 1) * 512] for j in range(4)], axis=2))
    w_c = np.ascontiguousarray(np.stack(parts, axis=2))
    wfi = []
    for l in range(2):
        w = _wl(I["w_ffn_in"][l])
        w = np.stack([w[:, :, 0:2816].reshape(128, 8, NF, 128), w[:, :, 2816:].reshape(128, 8, NF, 128)], axis=3)
        wfi.append(w.reshape(128, 8, NF * 256))
    w_fi = np.ascontiguousarray(np.stack(wfi, 0))
    w_fo = np.ascontiguousarray(np.stack([_wl(I["w_ffn_out"][l]) for l in range(2)], 0))
    prm = np.zeros((128, P_END), np.float32)
    gains = np.stack([I["norm_mix"][0], I["norm_mix"][1], I["norm_ffn"][0], I["norm_ffn"][1], I["norm_final"]], 0)
    prm[:, P_GAIN:P_GAIN + 40] = _fm(gains).reshape(128, 40)
    prm[:, P_GNM:P_GNM + 4] = _fm(I["gn_mlstm"][0])
    prm[:, P_GNR:P_GNR + 4] = _fm(I["gn_ret"][0])
    prm[:, P_GNH:P_GNH + 8] = _fm(I["gn_hgrn"][0])
    prm[:, P_LB0:P_LB0 + 8] = _fm(I["lb_logits"][0])
    prm[:, P_LB1:P_LB1 + 8] = _fm(I["lb_logits"][1])
    prm[:, P_CW:P_CW + 132] = _fm(I["conv_w"]).reshape(128, 132)
    prm[:, P_CB:P_CB + 44] = _fm(I["conv_b"]).reshape(128, 44)
    prm4 = np.ascontiguousarray(np.stack([I["b_igate"][0], I["b_fgate"][0]], 1).astype(np.float32))
    shared = dict(w_ab=w_ab, w_oab=_wl(I["w_out_ab"][0]), w_c=w_c, w_oc=_wl(I["w_out_c"][0]), w_fi=w_fi, w_fo=w_fo,
                  prm=prm, prm4=prm4, c128=c128, c4=c4, rot=rot)
    maps = []
    for c in range(8):
        sq = slice(16 * c, 16 * c + 16)
        xa = np.concatenate([I["x_sample"][sq].reshape(128, D), I["meta_tokens"], I["x_prompt"][c]], 0).astype(np.float32)
        m = dict(shared)
        m["xT"] = np.ascontiguousarray(xa.T.reshape(8, 128, NT).transpose(1, 0, 2))
        m["stC"] = np.ascontiguousarray(I["state_mlstm_C"][0, sq].transpose(2, 1, 0, 3))
        m["stn"] = np.ascontiguousarray(I["state_mlstm_n"][0, sq].transpose(2, 1, 0))
        m["stm"] = np.ascontiguousarray(I["state_mlstm_m"][0, sq].T)
        m["stR"] = np.ascontiguousarray(I["state_ret_S"][0, sq].transpose(2, 1, 0, 3))
        m["stH"] = np.ascontiguousarray(I["state_hgrn_S"][0, sq].transpose(2, 1, 0, 3))
        cv = I["state_ffn_conv"][:, sq]
        cv = cv.reshape(2, 16, 2, NF, 128).transpose(4, 0, 3, 1, 2).reshape(128, 2, NF, 32)
        m["stcv"] = np.ascontiguousarray(cv)
        maps.append(m)
    return maps, gl


_CACHE = {}


def kernel(**inputs):
    maps, gl = _prep(inputs)
    if "nc" not in _CACHE:
        _CACHE["nc"] = build_program(gl)[0]
    nc = _CACHE["nc"]
    res = run_bass_kernel_spmd(nc, maps, core_ids=list(range(8)))
    R = res.results
    f32 = np.float32
    yp = np.zeros((8, 2048, D), f32)
    ys = np.zeros((128, 8, D), f32)
    Cp = np.zeros((1, 8, 4, 128, 128), f32)
    Cs = np.zeros((1, 128, 4, 128, 128), f32)
    np_ = np.zeros((1, 8, 4, 128), f32)
    ns = np.zeros((1, 128, 4, 128), f32)
    mp = np.zeros((1, 8, 4), f32)
    ms = np.zeros((1, 128, 4), f32)
    Rp = np.zeros((1, 8, 4, 128, 128), f32)
    Rs = np.zeros((1, 128, 4, 128, 128), f32)
    Hp = np.zeros((1, 8, 8, 128, 128), f32)
    Hs = np.zeros((1, 128, 8, 128, 128), f32)
    cvp = np.zeros((2, 8, 2, 2816), f32)
    cvs = np.zeros((2, 128, 2, 2816), f32)
    for c in range(8):
        r = R[c]
        sq = slice(16 * c, 16 * c + 16)
        ya = np.asarray(r["yT"]).transpose(1, 0, 2).reshape(D, NT).T
        ys[sq] = ya[0:128].reshape(16, 8, D)
        yp[c] = ya[144:]
        Cp[0, c] = np.asarray(r["oC_p"]).transpose(1, 0, 2)
        Cs[0, sq] = np.asarray(r["oC_s"]).transpose(2, 1, 0, 3)
        np_[0, c] = np.asarray(r["on_p"]).T
        ns[0, sq] = np.asarray(r["on_s"]).transpose(2, 1, 0)
        mp[0, c] = np.asarray(r["om_p"])[:, 0]
        ms[0, sq] = np.asarray(r["om_s"]).T
        Rp[0, c] = np.asarray(r["oR_p"]).transpose(1, 0, 2)
        Rs[0, sq] = np.asarray(r["oR_s"]).transpose(2, 1, 0, 3)
        Hp[0, c] = np.asarray(r["oH_p"]).transpose(1, 0, 2)
        Hs[0, sq] = np.asarray(r["oH_s"]).transpose(2, 1, 0, 3)
        cvp[:, c] = np.asarray(r["ocv_p"]).transpose(1, 3, 2, 0).reshape(2, 2, 2816)
        cvs[:, sq] = np.asarray(r["ocv_s"]).reshape(128, 2, NF, 16, 2).transpose(1, 3, 4, 2, 0).reshape(2, 16, 2, 2816)
    return (yp, ys, Cp, Cs, np_, ns, mp, ms, Rp, Rs, Hp, Hs, cvp, cvs)
```

```python
import numpy as np
import concourse.bass as bass
import concourse.mybir as mybir

F32 = mybir.dt.float32
BF16 = mybir.dt.bfloat16
I32 = mybir.dt.int32
AF = mybir.ActivationFunctionType
ALU = mybir.AluOpType


class _Rec:
    __slots__ = ("lo", "hi", "w", "rs")

    def __init__(self, lo, hi, w, rs):
        self.lo, self.hi, self.w, self.rs = lo, hi, w, rs


def _ap_interval(ap):
    pat = ap.ap
    name = ap.tensor.name
    off = int(ap.offset)
    space = str(ap.space)
    if "DRAM" in space.upper() or "HBM" in space.upper():
        ext = sum((c - 1) * abs(s) for s, c in pat) + 1
        return ("d:" + name, off, off + ext)
    if "PSUM" in space.upper():
        return ("p:" + name, 0, 1 << 30)
    pstride = pat[0][0] if pat[0][0] > 0 else 1 << 30
    lo = off % pstride if pat[0][0] > 0 else off
    ext = sum((c - 1) * abs(s) for s, c in pat[1:]) + 1
    return ("s:" + name, lo, lo + ext)


class Tracker:
    COMPUTE = ("pe", "act", "dve", "pool")

    def __init__(self, nc, ring=20, same_eng_sync=True):
        self.nc = nc
        self.engobj = {"pe": nc.tensor, "act": nc.scalar, "dve": nc.vector,
                       "pool": nc.gpsimd, "sp": nc.sync}
        self.prog = {e: [] for e in self.engobj}
        self.count = {e: 0 for e in self.COMPUTE}
        self.sems = {}
        self.waited = {e: {} for e in self.engobj}
        self.bufs = {}
        self.same_eng_sync = same_eng_sync
        self._ctx = []
        for e in self.COMPUTE:
            self.sems[e] = self._mk_sem("c_" + e)
        self.rings = {}
        for q in ("sp", "pool", "act"):
            self.rings[q] = {"sems": [self._mk_sem("q_%s_%d" % (q, i)) for i in range(ring)],
                             "uses": [0] * ring, "n": 0}
        self.ninstr = {e: 0 for e in self.engobj}
        self._cap = None
        self._unit = None

    def _mk_sem(self, name):
        cm = self.nc.semaphore(name)
        h = cm.__enter__()
        self._ctx.append(cm)
        return h

    def _access(self, key, lo, hi, write, tok, rkey):
        recs = self.bufs.get(key)
        if recs is None:
            recs = []
        deps = []
        new = []
        cov = []
        for r in recs:
            if r.hi <= lo or r.lo >= hi:
                new.append(r)
                continue
            if r.w is not None:
                deps.append(r.w)
            if write:
                deps.extend(r.rs.values())
            if r.lo < lo:
                new.append(_Rec(r.lo, lo, r.w, dict(r.rs)))
            if r.hi > hi:
                new.append(_Rec(hi, r.hi, r.w, dict(r.rs)))
            if not write:
                mid = _Rec(max(r.lo, lo), min(r.hi, hi), r.w, dict(r.rs))
                mid.rs[rkey] = tok
                new.append(mid)
                cov.append((mid.lo, mid.hi))
        if write:
            new.append(_Rec(lo, hi, tok, {}))
        else:
            cov.sort()
            cur = lo
            for a, b in cov:
                if a > cur:
                    new.append(_Rec(cur, a, None, {rkey: tok}))
                cur = max(cur, b)
            if cur < hi:
                new.append(_Rec(cur, hi, None, {rkey: tok}))
        self.bufs[key] = new
        return deps

    def _collect(self, eng, outs, ins, tok, rkey):
        deps = []
        for ap in outs:
            k, lo, hi = _ap_interval(ap)
            deps += self._access(k, lo, hi, True, tok, rkey)
        for ap in ins:
            if ap is None or isinstance(ap, (int, float)):
                continue
            k, lo, hi = _ap_interval(ap)
            deps += self._access(k, lo, hi, False, tok, rkey)
        waits = []
        w = self.waited[eng]
        best = {}
        for (skey, sh, val) in deps:
            if skey == tok[0] and val >= tok[2]:
                continue
            if skey == "c_" + eng:
                if eng == "pe" or not self.same_eng_sync:
                    continue
            if w.get(skey, 0) >= val:
                continue
            if skey not in best or best[skey][1] < val:
                best[skey] = (sh, val)
        for skey, (sh, val) in best.items():
            w[skey] = val
            waits.append((sh, val))
        return waits

    def begin_capture(self):
        self._cap = []
        self._unit = None

    def end_capture(self):
        c = self._cap
        self._cap = None
        self._unit = None
        return c

    def atomic_begin(self):
        if self._cap is not None and self._unit is None:
            self._unit = []
            self._cap.append(self._unit)
            return True
        return False

    def atomic_end(self, opened):
        if opened:
            self._unit = None

    def _record(self, rec):
        if self._unit is not None:
            self._unit.append(rec)
        else:
            self._cap.append([rec])

    def flush_rr(self, caps):
        idx = [0] * len(caps)
        live = True
        while live:
            live = False
            for li, c in enumerate(caps):
                if idx[li] < len(c):
                    live = True
                    for rec in c[idx[li]]:
                        if rec[0] == "op":
                            self.op(*rec[1:])
                        else:
                            self.dma(rec[1], rec[2], rec[3], **rec[4])
                    idx[li] += 1

    def op(self, eng, fn, outs, ins, inc=True):
        if self._cap is not None:
            self._record(("op", eng, fn, outs, ins, inc))
            return
        val = self.count[eng] + 1
        skey = "c_" + eng
        tok = (skey, self.sems[eng], val)
        waits = self._collect(eng, outs, ins, tok, skey)
        if inc:
            self.count[eng] = val
        self.prog[eng].append((waits, fn, (self.sems[eng], 1) if inc else None))
        self.ninstr[eng] += 1

    def dma(self, q, out, in_, **kw):
        if self._cap is not None:
            self._record(("dma", q, out, in_, kw))
            return
        ring = self.rings[q]
        n = ring["n"]
        slot = n % len(ring["sems"])
        ring["n"] = n + 1
        uses = ring["uses"][slot]
        ring["uses"][slot] = uses + 1
        sh = ring["sems"][slot]
        skey = "q_%s_%d" % (q, slot)
        tok = (skey, sh, 16 * (uses + 1))
        waits = self._collect(q, [out], [in_], tok, skey)
        w = self.waited[q]
        if uses > 0 and w.get(skey, 0) < 16 * uses:
            w[skey] = 16 * uses
            waits.append((sh, 16 * uses))
        self.prog[q].append((waits, (lambda e, out=out, in_=in_, kw=kw: e.dma_start(out=out, in_=in_, **kw)),
                             (sh, 16)))
        self.ninstr[q] += 1

    def finish(self):
        waits = []
        for q, ring in self.rings.items():
            for i, sh in enumerate(ring["sems"]):
                if ring["uses"][i] > 0:
                    waits.append((sh, 16 * ring["uses"][i]))
        for e in self.COMPUTE:
            if self.count[e] > 0:
                waits.append((self.sems[e], self.count[e]))
        self.prog["sp"].append((waits, None, None))

    def emit(self):
        nc = self.nc
        with nc.Block() as block:
            def run(name):
                def f(e):
                    for waits, fn, inc in self.prog[name]:
                        for sh, val in waits:
                            e.wait_ge(sh, val)
                        if fn is None:
                            continue
                        ins = fn(e)
                        if inc is not None:
                            ins.then_inc(inc[0], inc[1])
                return f
            block.sync(run("sp"))
            block.tensor(run("pe"))
            block.scalar(run("act"))
            block.vector(run("dve"))
            block.gpsimd(run("pool"))

    def close(self):
        for cm in reversed(self._ctx):
            cm.__exit__(None, None, None)


import ml_dtypes
from concourse.bass_utils import run_bass_kernel_spmd

D = 1024
NT = 2192
HC = 1168
NF = 22
EPS = 1e-6
ARENA = 32768
STAGE = 99
ONLY = None
HALVES = (0, 1)
NLANES = 2
HSEG = 64
FAST_ROWS = True
ALT_N = True
SEQ_FLUSH = False
SAME_ENG_SYNC = True


def _halves():
    A = dict(c0=0, n=144, g0=0, kind="A", groups=[("S", 0, 128), ("P", 128, 16)])
    B = [dict(c0=(144 + 512 * i) if i < 2 else 512 * (i - 2), n=512, g0=144 + 512 * i, kind="B",
              groups=[("P", 128 * j, 128) for j in range(4)]) for i in range(4)]
    return [[A, B[0], B[1]], [B[2], B[3]]]


C_ID, C_NEGC, C_NEG8, C_M32, C_M8, C_DEC, C_INN, C_TAIL, C_SEG16, C_SEG4, C_NSA, C_NSB, C_END = (
    0, 128, 256, 384, 512, 640, 640 + 1024, 640 + 2048, 2700, 2716, 2720, 2864, 3376)
R_SEL, R_NSP, R_NSS, R_RSP, R_RSS, R_ID4, R_END = 0, 512, 640, 768, 896, 1024, 1028


def _consts():
    c = np.zeros((128, C_END), np.float32)
    s = np.arange(128)[:, None]
    t = np.arange(128)[None, :]
    c[:, C_ID:C_ID + 128] = np.eye(128)
    caus = (s <= t)
    bd8 = caus & ((s // 8) == (t // 8))
    bd32 = caus & ((s // HSEG) == (t // HSEG))
    c[:, C_NEGC:C_NEGC + 128] = np.where(caus, 0.0, -30000.0)
    c[:, C_NEG8:C_NEG8 + 128] = np.where(bd8, 0.0, -30000.0)
    c[:, C_M32:C_M32 + 128] = bd32
    c[:, C_M8:C_M8 + 128] = bd8
    lg = np.log1p(-np.exp2(-5.0 - np.arange(4, dtype=np.float32))).astype(np.float64)
    for ty, m in enumerate((caus, bd8)):
        for h in range(4):
            o = C_DEC + (ty * 4 + h) * 128
            c[:, o:o + 128] = np.where(m, np.exp(np.maximum(t - s, 0) * lg[h]), 0.0)
            tp = t if ty == 0 else (t % 8)
            o = C_INN + (ty * 4 + h) * 128
            c[:, o:o + 128] = np.exp((tp + 1.0) * lg[h]) * np.ones((128, 1))
    sp = np.arange(128)
    for h in range(4):
        c[:, C_TAIL + 0 + h] = np.exp(np.maximum(127.0 - sp, 0) * lg[h])
        c[:, C_TAIL + 4 + h] = np.exp(np.maximum(15.0 - sp, 0) * lg[h])
        c[:, C_TAIL + 8 + h] = np.exp((7.0 - sp % 8) * lg[h])
    c[:, C_SEG16:C_SEG16 + 16] = (s // 8) == np.arange(16)[None, :]
    c[:, C_SEG4:C_SEG4 + 4] = (s // HSEG) == np.arange(4)[None, :]
    nsa = np.ones(144, np.float32)
    nsa[0:128:8] = 0.0
    nsa[128] = 0.0
    nsb = np.ones(512, np.float32)
    nsb[0::HSEG] = 0.0
    c[:, C_NSA:C_NSA + 144] = nsa[None, :]
    c[:, C_NSB:C_NSB + 512] = nsb[None, :]
    r = np.zeros((4, R_END), np.float32)
    for h in range(4):
        r[h, R_SEL + h * 128:R_SEL + (h + 1) * 128] = 1.0
    r[:, R_NSP:R_NSP + 128] = 1.0
    nss = np.ones(128, np.float32)
    nss[0::8] = 0.0
    r[:, R_NSS:R_NSS + 128] = nss[None, :]
    r[:, R_RSP:R_RSP + 128] = 0.0
    r[:, R_RSS:R_RSS + 128] = np.where(nss == 0.0, -1e30, 0.0)[None, :]
    r[:, R_ID4:R_ID4 + 4] = np.eye(4)
    gl = [[float(np.exp(L * lg[h])) for h in range(4)] for L in (128, 16, 8)]
    pos = np.concatenate([np.tile(16384.0 + np.arange(8), 16), np.arange(16), 16.0 + np.arange(2048)]).astype(np.float32)
    inv = (1.0 / (np.float32(10000.0) ** np.linspace(0.0, 1.0, 64, dtype=np.float32))).astype(np.float32)
    ang = (pos[:, None] * inv[None, :]).astype(np.float32)
    cs, sn = np.cos(ang).astype(np.float32), np.sin(ang).astype(np.float32)
    rot = np.zeros((128, 2, NT), np.float32)
    rot[0:64, 0, :] = cs.T
    rot[64:128, 0, :] = cs.T
    rot[0:64, 1, :] = -sn.T
    rot[64:128, 1, :] = sn.T
    return c, r, rot, gl


P_GAIN, P_GNM, P_GNR, P_GNH, P_LB0, P_LB1, P_CW, P_CB, P_END = 0, 40, 44, 48, 56, 64, 72, 204, 248


def _fm(v):
    v = np.asarray(v, np.float32)
    n = v.shape[-1] // 128
    v = v.reshape(v.shape[:-1] + (n, 128))
    return np.ascontiguousarray(np.moveaxis(v, -1, 0))


def _wl(w):
    w = np.asarray(w, np.float32)
    k = w.shape[0] // 128
    return np.ascontiguousarray(w.reshape(k, 128, w.shape[1]).transpose(1, 0, 2))


def build_program(gl):
    nc = bass.Bass("TRN2", target_bir_lowering=False)
    tr = Tracker(nc, ring=20, same_eng_sync=SAME_ENG_SYNC)

    def din(name, shape):
        return nc.dram_tensor(name, list(shape), F32, kind="ExternalInput").ap()

    def dout(name, shape):
        return nc.dram_tensor(name, list(shape), F32, kind="ExternalOutput").ap()

    xT_d = din("xT", (128, 8, NT))
    w_ab = din("w_ab", (128, 8, 5128))
    w_oab = din("w_oab", (128, 8, 1024))
    w_c = din("w_c", (128, 8, 2, 2048))
    w_oc = din("w_oc", (128, 8, 1024))
    w_fi = din("w_fi", (2, 128, 8, NF * 256))
    w_fo = din("w_fo", (2, 128, NF, 1024))
    stC = din("stC", (128, 4, 16, 128))
    stn = din("stn", (128, 4, 16))
    stm = din("stm", (4, 16))
    stR = din("stR", (128, 4, 16, 128))
    stH = din("stH", (128, 8, 16, 128))
    stcv = din("stcv", (128, 2, NF, 32))
    prm = din("prm", (128, P_END))
    prm4 = din("prm4", (4, 2))
    c128 = din("c128", (128, C_END))
    c4 = din("c4", (4, R_END))
    rot = din("rot", (128, 2, NT))
    yT_d = dout("yT", (128, 8, NT))
    oC_p = dout("oC_p", (128, 4, 128))
    oC_s = dout("oC_s", (128, 4, 16, 128))
    on_p = dout("on_p", (128, 4))
    on_s = dout("on_s", (128, 4, 16))
    om_p = dout("om_p", (4, 1))
    om_s = dout("om_s", (4, 16))
    oR_p = dout("oR_p", (128, 4, 128))
    oR_s = dout("oR_s", (128, 4, 16, 128))
    oH_p = dout("oH_p", (128, 8, 128))
    oH_s = dout("oH_s", (128, 8, 16, 128))
    ocv_p = dout("ocv_p", (128, 2, NF, 2))
    ocv_s = dout("ocv_s", (128, 2, NF, 32))

    ctxs = []

    def sb(name, shape, dt):
        cm = nc.sbuf_tensor(name, list(shape), dt)
        t = cm.__enter__()
        ctxs.append(cm)
        return t

    def ps(name, shape, dt):
        cm = nc.psum_tensor(name, list(shape), dt)
        t = cm.__enter__()
        ctxs.append(cm)
        return t

    xT = sb("xTs", (128, 8, HC), F32)
    xn = sb("xn", (128, 8, HC), BF16)
    arena = sb("arena", (128, ARENA), BF16)
    cst = sb("cst", (128, C_END), F32)
    crow = sb("crow", (4, R_END), F32)
    prms = sb("prms", (128, P_END), F32)
    prm4s = sb("prm4s", (4, 4), F32)
    ident_b = sb("ident_b", (128, 128), BF16)
    ones_b = sb("ones_b", (128, 128), BF16)
    segm_b = sb("segm_b", (128, 20), BF16)
    onesdiv = sb("onesdiv", (128, 128), F32)
    Cst = sb("Cst", (128, 4, 128), F32)
    nst = sb("nst", (128, 4), F32)
    Cb = sb("Cb", (128, 4, 128), BF16)
    nbc = sb("nbc", (128, 4, 128), BF16)
    Rst = sb("Rst", (128, 4, 128), F32)
    Rb = sb("Rb", (128, 4, 128), BF16)
    Hst = sb("Hst", (128, 8, 128), F32)
    Hb = sb("Hb", (128, 8, 128), BF16)
    mstate = sb("mstate", (4, 4), F32)
    tailbuf = sb("tailbuf", (128, 2, NF, 2), F32)
    cvout = sb("cvout", (128, NF, 32), F32)
    lbv = sb("lbv", (128, 24), F32)
    WF = 8300
    WB = 10000
    wkf = sb("wkf", (128, WF), F32)
    wkb = sb("wkb", (128, WB), BF16)

    PB = [ps("pb%d" % i, (128, 512), F32) for i in range(8)]
    PA = [PB[0], PB[1]]
    LPS = [dict(G=PB[2], N=PB[3], Z=PB[4]), dict(G=PB[5], N=PB[6], Z=PB[7])]
    for lp in LPS:
        lp["T"] = lp["Z"][:, 448:512].bitcast(BF16)

    def isap(x):
        return x is not None and not isinstance(x, (int, float))

    def MM(out, lhsT, rhs, start=True, stop=True, inc=True):
        tr.op("pe", lambda e: e.matmul(out, lhsT=lhsT, rhs=rhs, start=start, stop=stop), [out], [lhsT, rhs], inc=inc)

    def MMK(out, pairs):
        n = len(pairs)
        o = tr.atomic_begin()
        for i, (l, r) in enumerate(pairs):
            MM(out, l, r, start=(i == 0), stop=(i == n - 1), inc=(i == n - 1))
        tr.atomic_end(o)

    def TP(out, in_, ident):
        tr.op("pe", lambda e: e.transpose(out=out, in_=in_, identity=ident), [out], [in_, ident])

    def ACT(out, in_, func, bias=None, scale=None):
        kw = {}
        if bias is not None:
            kw["bias"] = bias
        if scale is not None:
            kw["scale"] = scale
        ins = [in_] + [a for a in (bias, scale) if isap(a)]
        tr.op("act", lambda e: e.activation(out=out, in_=in_, func=func, **kw), [out], ins)

    def TT(eng, out, in0, in1, op):
        tr.op(eng, lambda e: e.tensor_tensor(out=out, in0=in0, in1=in1, op=op), [out], [in0, in1])

    def TS(eng, out, in0, s1, op0, s2=None, op1=None):
        ins = [in0] + [a for a in (s1, s2) if isap(a)]
        if op1 is None:
            tr.op(eng, lambda e: e.tensor_scalar(out=out, in0=in0, scalar1=s1, scalar2=None, op0=op0), [out], ins)
        else:
            tr.op(eng, lambda e: e.tensor_scalar(out=out, in0=in0, scalar1=s1, scalar2=s2, op0=op0, op1=op1), [out], ins)

    def STT(out, in0, scalar, in1, op0, op1):
        ins = [in0, in1] + ([scalar] if isap(scalar) else [])
        tr.op("dve", lambda e: e.scalar_tensor_tensor(out=out, in0=in0, scalar=scalar, in1=in1, op0=op0, op1=op1), [out], ins)

    def CP(eng, out, in_):
        if eng == "act":
            tr.op("act", lambda e: e.activation(out=out, in_=in_, func=AF.Copy), [out], [in_])
        else:
            tr.op(eng, lambda e: e.tensor_copy(out=out, in_=in_), [out], [in_])

    def SCAN(out, d0, d1, init, op0, op1):
        tr.op("dve", lambda e: e.tensor_tensor_scan(out=out, data0=d0, data1=d1, initial=init, op0=op0, op1=op1), [out], [d0, d1])

    def RECIP(out, in_):
        tr.op("dve", lambda e: e.reciprocal(out=out, in_=in_), [out], [in_])

    def MEMSET(eng, ap, val):
        tr.op(eng, lambda e: e.memset(ap, val), [ap], [])

    def DMA(q, out, in_):
        tr.dma(q, out, in_)

    MUL, ADD, SUB, MAX = ALU.mult, ALU.add, ALU.subtract, ALU.max

    def bc(ap, shape):
        return ap.broadcast_to(list(shape))

    def carve(base, plan, off=0):
        out = {}
        for name, shape in plan:
            sz = int(np.prod(shape))
            v = base[:, off:off + sz]
            v = v.rearrange("p (a b) -> p a b", a=shape[0])
            out[name] = v
            off += sz
        assert off <= base.shape[1], (off, base.shape)
        return out, off

    def carve_lanes(base, shared, lane, extra):
        S, o = carve(base, shared)
        L0, o1 = carve(base, lane, o)
        L1, o2 = carve(base, lane, o1)
        X, o3 = carve(base, extra, o1)
        S.update(X)
        return S, [L0, L1]

    ar = {"off": 0}

    def load_weights(spec, prev_region):
        tot = sum(a.shape[1] * a.shape[2] for a in spec)
        off = ar["off"]
        if off + tot > ARENA:
            off = 0
        if prev_region is not None:
            plo, phi = prev_region
            if not (off + tot <= plo or off >= phi):
                return None
        views = []
        o = off
        for a in spec:
            K, C = a.shape[1], a.shape[2]
            v = arena[:, o:o + K * C].rearrange("p (k c) -> p k c", k=K)
            for k in range(K):
                for c0 in range(0, C, 2048):
                    c1 = min(C, c0 + 2048)
                    DMA("pool", v[:, k, c0:c1], a[:, k, c0:c1])
            views.append(v)
            o += K * C
        ar["off"] = o
        return views, (off, o)

    DMA("sp", cst[:], c128)
    DMA("sp", crow[:], c4)
    DMA("sp", prms[:], prm)
    DMA("sp", prm4s[:, 0:2], prm4)
    CP("dve", ident_b[:], cst[:, C_ID:C_ID + 128])
    MEMSET("dve", ones_b[:], 1.0)
    CP("dve", segm_b[:], cst[:, C_SEG16:C_SEG16 + 20])
    MEMSET("dve", onesdiv[:], 1.0 / 128.0)
    for t_ in (Cst, nst, Rst, Hst, mstate, tailbuf):
        MEMSET("pool", t_[:], 0.0)
    for t_ in (Cb, nbc, Rb, Hb):
        MEMSET("pool", t_[:], 0.0)
    TS("dve", prm4s[:, 2:3], prm4s[:, 1:2], -1.0, MUL)
    TT("dve", lbv[:, 16:24], prms[:, P_LB1:P_LB1 + 8], prms[:, P_LB0:P_LB0 + 8], SUB)
    ACT(lbv[:, 0:8], lbv[:, 16:24], AF.Sigmoid)
    TS("dve", lbv[:, 8:16], lbv[:, 0:8], -1.0, MUL, 1.0, ADD)
    TS("dve", lbv[:, 16:24], lbv[:, 8:16], -1.0, MUL)

    sel4 = lambda h: crow[:, R_SEL + h * 128:R_SEL + (h + 1) * 128]
    ident4 = crow[:, R_ID4:R_ID4 + 4]

    def norm_tile(t, gidx, final):
        c0, n = t["c0"], t["n"]
        wf, _ = carve(wkf, [("lnv", (1, 512)), ("rstd", (1, 512)), ("yo", (2, 512))])
        wb_, _ = carve(wkb, [("sq", (8, 512))])
        for k in range(8):
            ACT(wb_["sq"][:, k, 0:n], xT[:, k, c0:c0 + n], AF.Square)
        MMK(PA[0][:, 0:n], [(ones_b[:], wb_["sq"][:, k, 0:n]) for k in range(8)])
        ACT(wf["lnv"][:, 0, 0:n], PA[0][:, 0:n], AF.Ln, bias=EPS, scale=1.0 / D)
        ACT(wf["rstd"][:, 0, 0:n], wf["lnv"][:, 0, 0:n], AF.Exp, scale=-0.5)
        for k in range(8):
            g = prms[:, P_GAIN + gidx * 8 + k:P_GAIN + gidx * 8 + k + 1]
            if final:
                yo = wf["yo"][:, k % 2, 0:n]
                STT(yo, xT[:, k, c0:c0 + n], g, wf["rstd"][:, 0, 0:n], MUL, MUL)
                DMA("sp", yT_d[:, k, t["g0"]:t["g0"] + n], yo)
            else:
                STT(xn[:, k, c0:c0 + n], xT[:, k, c0:c0 + n], g, wf["rstd"][:, 0, 0:n], MUL, MUL)

    def ln_gate(W, P, L, gaincol, gate, out, mean, sq_src=None):
        PN = P["N"]
        hT = W["hT"][:, 0, 0:L]
        if sq_src is None:
            ACT(W["hsq"][:, 0, 0:L], hT, AF.Square)
        if mean and L == 128:
            both = bass.AP(hT.tensor, hT.offset, [list(hT.ap[0]), [1, 256]])
            assert W["hsq"][:, 0, 0:L].offset == hT.offset + 128
            MM(PN[:, 256:512], onesdiv[:], both)
        else:
            if mean:
                MM(PN[:, 256:256 + L], onesdiv[:], hT)
            MM(PN[:, 384:384 + L], onesdiv[:], W["hsq"][:, 0, 0:L])
        if mean:
            ACT(W["msq"][:, 0, 0:L], PN[:, 256:256 + L], AF.Square)
            TT("dve", W["var"][:, 0, 0:L], PN[:, 384:384 + L], W["msq"][:, 0, 0:L], SUB)
            ACT(W["lnv"][:, 0, 0:L], W["var"][:, 0, 0:L], AF.Ln, bias=EPS)
            TT("dve", W["cc"][:, 0, 0:L], hT, PN[:, 256:256 + L], SUB)
            cc = W["cc"][:, 0, 0:L]
        else:
            ACT(W["lnv"][:, 0, 0:L], PN[:, 384:384 + L], AF.Ln, bias=EPS)
            cc = hT
        ACT(W["rs"][:, 0, 0:L], W["lnv"][:, 0, 0:L], AF.Exp, scale=-0.5)
        STT(W["yy"][:, 0, 0:L], cc, gaincol, W["rs"][:, 0, 0:L], MUL, MUL)
        TT("pool", out, W["yy"][:, 0, 0:L], gate, MUL)

    def outproj(t, Wout, SB_):
        c0, n = t["c0"], t["n"]
        pacc = [PB[0], PB[1], PB[2], PB[5]]
        for d in range(8):
            pa = pacc[d % 4]
            MMK(pa[:, 0:n], [(Wout[:, kk, d * 128:(d + 1) * 128], SB_["mixed"][:, kk, 0:n]) for kk in range(4)])
            TT("dve", xT[:, d, c0:c0 + n], xT[:, d, c0:c0 + n], pa[:, 0:n], ADD)

    LN_F = [("hT", (1, 128)), ("hsq", (1, 128)), ("mu", (1, 128)), ("msq", (1, 128)), ("var", (1, 128)),
            ("lnv", (1, 128)), ("rs", (1, 128)), ("cc", (1, 128)), ("yy", (1, 128))]
    X_F = [("q0", (2, 512)), ("qn", (2, 512)), ("n0s", (4, 16)), ("nnew", (1, 16))]
    SH_B = [("Vtok", (4, 512)), ("mixed", (4, 512))]
    LN_B = [("qT", (1, 512)), ("kT", (1, 512)), ("gate", (1, 512)), ("Khat", (1, 512)),
            ("SD", (1, 128)), ("qp", (1, 128)), ("Kh", (1, 128)), ("Vb", (4, 128))]
    X_B = [("s0b", (2, 512)), ("nbq", (2, 512))]

    def vtok_proj(t, Wv, SB_, banks=None):
        for gi, (gt, off, L) in enumerate(t["groups"]):
            pv = (banks or PA)[gi % 2]
            MMK(pv[0:L, 0:512], [(xn[:, k, t["c0"] + off:t["c0"] + off + L], Wv(k)) for k in range(8)])
            CP("act", SB_["Vtok"][0:L, gi, :], pv[0:L, 0:512])

    def pipeline_groups(ng, pro, main, tail):
        pro(0)
        for gi in range(ng):
            main(gi)
            if gi + 1 < ng:
                pro(gi + 1)
            tail(gi)

    def run_heads(t, nheads, body, extra=None):
        nl = 1 if t["kind"] == "A" else NLANES
        for h0 in range(0, nheads, nl):
            caps = []
            if extra is not None and h0 == 0:
                tr.begin_capture()
                extra()
                caps.append(tr.end_capture())
            for ln in range(min(nl, nheads - h0)):
                tr.begin_capture()
                body(h0 + ln, ln)
                caps.append(tr.end_capture())
            if SEQ_FLUSH:
                for c in caps:
                    tr.flush_rr([c])
            else:
                tr.flush_rr(caps)

    def sample_quarters(S, SB_, P, Vb, h, gi, st_in, st_out, qsrc, Kh, PDen, nidx, scale_ap, scale_imm):
        hb = slice(h * 128, (h + 1) * 128)
        PN, PU = P["N"], P["Z"]
        for qd in range(4):
            b = qd % 2
            q0 = S["q0"][:, b, :]
            q0v = q0.rearrange("p (a b) -> p a b", a=4)
            DMA("sp", q0v, st_in[:, 4 * qd:4 * qd + 4, :])
            s0b = SB_["s0b"][:, b, :].rearrange("p (a b) -> p a b", a=4)
            CP("act", s0b, q0v)
            if nidx is not None:
                nbq = SB_["nbq"][:, b, :].rearrange("p (a b) -> p a b", a=4)
                CP("pool", nbq, bc(S["n0s"][:, nidx, 4 * qd:4 * qd + 4].unsqueeze(2), (128, 4, 128)))
            for jj in range(4):
                j = 4 * qd + jj
                sc = slice(8 * j, 8 * j + 8)
                MM(PN[:, sc], s0b[:, jj, :], qsrc(sc), start=False, stop=(j == 15))
                if nidx is not None:
                    MM(PDen[:, sc], nbq[:, jj, :], qsrc(sc), start=False, stop=(j == 15))
            TT("pool", Vb, bc(SB_["Vtok"][:, gi, hb].unsqueeze(1), (128, 4, 128)),
               bc(cst[:, C_SEG16 + 4 * qd:C_SEG16 + 4 * qd + 4].unsqueeze(2), (128, 4, 128)), MUL)
            MM(PU[:, 0:512], Kh, Vb.rearrange("p a b -> p (a b)"))
            qn = S["qn"][:, b, :]
            if scale_ap is not None:
                TT("pool", qn.rearrange("p (a b) -> p a b", a=4), q0v,
                   bc(scale_ap[:, 4 * qd:4 * qd + 4].unsqueeze(2), (128, 4, 128)), MUL)
                TT("dve", qn, qn, PU[:, 0:512], ADD)
            else:
                STT(qn, q0, scale_imm, PU[:, 0:512], MUL, ADD)
            DMA("sp", st_out[:, 4 * qd:4 * qd + 4, :], qn.rearrange("p (a b) -> p a b", a=4))

    def pass_m_tile(t, Wm, Wout):
        c0, n = t["c0"], t["n"]
        S, LW = carve_lanes(wkf,
                            [("li", (1, 512)), ("ef", (1, 512)), ("sp", (1, 512)), ("R4", (4, 400)), ("cs", (4, 128)),
                             ("a2", (1, 128)), ("AW", (2, 128)), ("m0s", (1, 16)), ("mns", (1, 16)), ("acol", (4, 4)),
                             ("wcol", (4, 4))],
                            LN_F + [("tmp", (1, 128)), ("Dm", (1, 128)), ("ibc", (1, 128)), ("e2", (1, 128)),
                                    ("dmax", (1, 128)), ("rden", (1, 128)), ("absd", (1, 128)), ("carry", (1, 16))],
                            X_F)
        SB_, LB = carve_lanes(wkb, SH_B, LN_B, X_B)
        xk = lambda k: xn[:, k, c0:c0 + n]
        PG0 = LPS[0]["G"]
        MMK(PA[0][0:4, 0:n], [(Wm[:, k, 2048:2052], xk(k)) for k in range(8)])
        MMK(PA[1][0:4, 0:n], [(Wm[:, k, 2052:2056], xk(k)) for k in range(8)])
        li, ef, sp_ = S["li"][0:4, 0, :], S["ef"][0:4, 0, :], S["sp"][0:4, 0, :]
        MEMSET("dve", S["R4"][0:4, :, :], 0.0)
        ACT(li[:, 0:n], PA[0][0:4, 0:n], AF.Identity, bias=prm4s[:, 0:1])
        ACT(ef[:, 0:n], PA[1][0:4, 0:n], AF.Exp, bias=prm4s[:, 2:3], scale=-1.0)
        ACT(sp_[:, 0:n], ef[:, 0:n], AF.Ln, bias=1.0)
        vtok_proj(t, lambda k: Wm[:, k, 1024:1536], SB_)
        fast_rows = (t["kind"] == "B") and FAST_ROWS
        if fast_rows:
            m0g = S["m0s"][0:4, 0, 0:8]
            Abuf = S["ef"][0:4, 0, :]
            A2buf = sp_
            WRbuf = li
            nsP = crow[:, R_NSP:R_NSP + 128]
            rsP = crow[:, R_RSP:R_RSP + 128]
            G4 = [(gi, off) for gi, (gt, off, L) in enumerate(t["groups"])]
            for gi, off in G4:
                SCAN(S["cs"][0:4, gi, 0:128], nsP, sp_[:, off:off + 128], 0.0, MUL, ADD)
            for gi, off in G4:
                TT("dve", Abuf[:, off:off + 128], li[:, off:off + 128], S["cs"][0:4, gi, 0:128], ADD)
            CP("dve", m0g[:, 0:1], mstate[:, 0:1])
            for gi, off in G4:
                R = S["R4"][0:4, gi, :]
                TS("dve", A2buf[:, off:off + 128], Abuf[:, off:off + 128], m0g[:, gi:gi + 1], MAX)
                SCAN(R[:, 0:128], rsP, A2buf[:, off:off + 128], -1e30, ADD, MAX)
                TT("dve", m0g[:, gi + 1:gi + 2], R[:, 127:128], S["cs"][0:4, gi, 127:128], SUB)
            CP("dve", mstate[:, 0:1], m0g[:, 4:5])
            for gi, off in G4:
                R = S["R4"][0:4, gi, :]
                TS("dve", R[:, 128:256], R[:, 0:128], -1.0, MUL, m0g[:, gi:gi + 1], ADD)
                TS("dve", R[:, 384:385], R[:, 127:128], -1.0, MUL, m0g[:, gi:gi + 1], ADD)
                TS("dve", WRbuf[:, off:off + 128], Abuf[:, off:off + 128], R[:, 127:128], SUB)
                TT("dve", R[:, 256:384], S["cs"][0:4, gi, 0:128], R[:, 0:128], SUB)
            for gi, off in G4:
                TP(PG0[0:128, 400:404], Abuf[:, off:off + 128], ident4)
                CP("act", S["acol"][0:128, gi, :], PG0[0:128, 400:404])
                TP(LPS[1]["G"][0:128, 404:408], WRbuf[:, off:off + 128], ident4)
                ACT(S["wcol"][0:128, gi, :], LPS[1]["G"][0:128, 404:408], AF.Exp)
        for gi, (gt, off, L) in enumerate(t["groups"]):
            if fast_rows:
                break
            isS = gt == "S"
            nseg, Ls = (16, 8) if isS else (1, L)
            R = S["R4"][0:4, gi, :]
            cs = S["cs"][0:4, gi, 0:L]
            a2 = S["a2"][0:4, 0, 0:L]
            a = S["AW"][0:4, 0, 0:L]
            wr = S["AW"][0:4, 1, 0:L]
            ns = crow[:, (R_NSS if isS else R_NSP):(R_NSS if isS else R_NSP) + L]
            rs = crow[:, (R_RSS if isS else R_RSP):(R_RSS if isS else R_RSP) + L]
            SCAN(cs, ns, sp_[:, off:off + L], 0.0, MUL, ADD)
            TT("dve", a, li[:, off:off + L], cs, ADD)
            Mx = R[:, 0:L]
            if isS:
                m0s = S["m0s"][0:4, 0, :]
                DMA("sp", m0s, stm)
                v3 = lambda ap: ap.rearrange("p (a b) -> p a b", a=16)
                m0b = bc(m0s.unsqueeze(2), (4, 16, 8))
                TT("dve", v3(a2), v3(a), m0b, MAX)
            else:
                TS("dve", a2, a, mstate[:, 0:1], MAX)
            SCAN(Mx, rs, a2, -1e30, ADD, MAX)
            Ml = R[:, Ls - 1:L:Ls]
            csl = cs[:, Ls - 1:L:Ls]
            if isS:
                TT("dve", v3(R[:, 128:128 + L]), m0b, v3(Mx), SUB)
                TT("dve", R[:, 384:400], m0s, Ml, SUB)
                TT("dve", v3(wr), v3(a), bc(Ml.unsqueeze(2), (4, 16, 8)), SUB)
                mns = S["mns"][0:4, 0, :]
                TT("dve", mns, Ml, csl, SUB)
                DMA("sp", om_s, mns)
            else:
                TS("dve", R[:, 128:128 + L], Mx, -1.0, MUL, mstate[:, 0:1], ADD)
                TS("dve", R[:, 384:385], Ml, -1.0, MUL, mstate[:, 0:1], ADD)
                TS("dve", wr, a, Ml, SUB)
            TT("dve", R[:, 256:256 + L], cs, Mx, SUB)
            if not isS:
                TT("dve", mstate[:, 0:1], Ml, csl, SUB)
            TP(PG0[0:L, 400:404], a, ident4)
            TP(PG0[0:L, 404:408], wr, ident4)
            CP("act", S["acol"][0:L, gi, :], PG0[0:L, 400:404])
            ACT(S["wcol"][0:L, gi, :], PG0[0:L, 404:408], AF.Exp)
        if t["kind"] == "A":
            DMA("sp", S["n0s"][:, :, :], stn)

        def body(h, ln):
            PAl = PA[ln]
            W, B, P = LW[ln], LB[ln], LPS[ln]
            PG, PN, PZ, PT = P["G"], P["N"], P["Z"], P["T"]
            hb = slice(h * 128, (h + 1) * 128)
            qT, kT, gate = B["qT"][:, 0, :], B["kT"][:, 0, :], B["gate"][:, 0, :]
            MMK(PAl[:, 0:n], [(Wm[:, k, h * 128:(h + 1) * 128], xk(k)) for k in range(8)])
            CP("act", qT[:, 0:n], PAl[:, 0:n])
            MMK(PAl[:, 0:n], [(Wm[:, k, 512 + h * 128:512 + (h + 1) * 128], xk(k)) for k in range(8)])
            ACT(kT[:, 0:n], PAl[:, 0:n], AF.Copy, scale=128.0 ** -0.5)
            MMK(PAl[:, 0:n], [(Wm[:, k, 1536 + h * 128:1536 + (h + 1) * 128], xk(k)) for k in range(8)])
            ACT(gate[:, 0:n], PAl[:, 0:n], AF.Sigmoid)
            def pro(gi):
                gt, off, L = t["groups"][gi]
                isS = gt == "S"
                nseg = 16 if isS else 1
                gc = slice(off, off + L)
                R = S["R4"][0:4, gi, :]
                MM(PG[:, 0:400], sel4(h), R)
                negm = cst[0:L, (C_NEG8 if isS else C_NEGC):(C_NEG8 if isS else C_NEGC) + L]
                tmp, Dm = W["tmp"][0:L, 0, 0:L], W["Dm"][0:L, 0, 0:L]
                TT("dve", tmp, negm, PG[0:L, 0:L], SUB)
                ACT(Dm, tmp, AF.Exp, bias=S["acol"][0:L, gi, h:h + 1])
                MM(PZ[0:L, 0:L], kT[:, gc], qT[:, gc])
                SD = B["SD"][0:L, 0, 0:L]
                TT("dve", SD, PZ[0:L, 0:L], Dm, MUL)
                ibc, e2, carry = W["ibc"][:, 0, 0:L], W["e2"][:, 0, 0:L], W["carry"][:, 0, 0:nseg]
                ACT(ibc, PG[:, 128:128 + L], AF.Exp)
                qp = B["qp"][:, 0, 0:L]
                TT("pool", qp, qT[:, gc], ibc, MUL)
                ACT(e2, PG[:, 256:256 + L], AF.Exp)
                ACT(carry, PG[:, 384:384 + nseg], AF.Exp)
                TP(PT[0:L, 0:128], kT[:, gc], ident_b[:])
                Kh = B["Kh"][0:L, 0, :]
                TS("dve", Kh, PT[0:L, 0:128], S["wcol"][0:L, gi, h:h + 1], MUL)

            def main(gi):
                gt, off, L = t["groups"][gi]
                isS = gt == "S"
                nseg = 16 if isS else 1
                SD = B["SD"][0:L, 0, 0:L]
                qp = B["qp"][:, 0, 0:L]
                Kh = B["Kh"][0:L, 0, :]
                e2, carry = W["e2"][:, 0, 0:L], W["carry"][:, 0, 0:nseg]
                if not isS:
                    MM(PN[:, 0:L], SB_["Vtok"][0:L, gi, hb], SD, start=True, stop=False)
                    MM(PN[:, 0:L], Cb[:, h, :], qp, start=False, stop=True)
                    MM(PN[:, 128:128 + L], ones_b[0:L, :], SD, start=True, stop=False)
                    MM(PN[:, 128:128 + L], nbc[:, h, :], qp, start=False, stop=True)
                    PDen = PN[:, 128:128 + L]
                    MM(PZ[:, 128:256], Kh, SB_["Vtok"][0:L, gi, hb])
                    MM(PZ[:, 256:257], Kh, ones_b[0:L, 0:1])
                    STT(Cb[:, h, :], Cst[:, h, :], carry[:, 0:1], PZ[:, 128:256], MUL, ADD)
                    STT(Cst[:, h, :], Cst[:, h, :], carry[:, 0:1], PZ[:, 128:256], MUL, ADD)
                    TS("dve", nbc[:, h, :], bc(nst[:, h:h + 1], (128, 128)), carry[:, 0:1], MUL, PZ[:, 256:257], ADD)
                    STT(nst[:, h:h + 1], nst[:, h:h + 1], carry[:, 0:1], PZ[:, 256:257], MUL, ADD)
                else:
                    PDb = LPS[1]["N"]
                    MM(PN[:, 0:L], SB_["Vtok"][0:L, gi, hb], SD, start=True, stop=False)
                    MM(PDb[:, 0:L], ones_b[0:L, :], SD, start=True, stop=False)
                    sample_quarters(S, SB_, P, B["Vb"][:, :, :], h, gi, stC[:, h, :, :], oC_s[:, h, :, :], lambda sc: qp[:, sc], Kh,
                                    PDb, h, carry, None)
                    PDen = PDb[:, 0:L]
                    MM(PG[:, 408:424], Kh, segm_b[:, 0:16])
                    nn = S["nnew"][:, 0, :]
                    TT("pool", nn, S["n0s"][:, h, :], carry, MUL)
                    TT("dve", nn, nn, PG[:, 408:424], ADD)
                    DMA("sp", on_s[:, h, :], nn)
                dmax, rden, hT = W["dmax"][:, 0, 0:L], W["rden"][:, 0, 0:L], W["hT"][:, 0, 0:L]
                ACT(W["absd"][:, 0, 0:L], PDen, AF.Abs)
                TT("dve", dmax, W["absd"][:, 0, 0:L], e2, MAX)
                RECIP(rden, dmax)
                TT("dve", hT, PN[:, 0:L], rden, MUL)

            def tail(gi):
                gt, off, L = t["groups"][gi]
                gc = slice(off, off + L)
                ln_gate(W, P, L, prms[:, P_GNM + h:P_GNM + h + 1], gate[:, gc], SB_["mixed"][:, h, gc], True)

            pipeline_groups(len(t["groups"]), pro, main, tail)

        run_heads(t, 4, body)
        outproj(t, Wout, SB_)

    def pass_r_tile(t, Wr, Wout):
        c0, n = t["c0"], t["n"]
        S, LW = carve_lanes(wkf, [("rt", (2, 512))], LN_F + [("t1", (1, 512)), ("t2", (1, 512))], X_F)
        SB_, LB = carve_lanes(wkb, SH_B, LN_B, X_B)
        xk = lambda k: xn[:, k, c0:c0 + n]
        rt = S["rt"]
        DMA("sp", rt[:, :, 0:n], rot[:, :, t["g0"]:t["g0"] + n])
        vt_r = lambda: vtok_proj(t, lambda k: Wr[:, k, 1024:1536], SB_, [PB[2], PB[5]])

        def body(h, ln):
            PAl = PA[ln]
            W, B, P = LW[ln], LB[ln], LPS[ln]
            PN, PZ, PT = P["N"], P["Z"], P["T"]
            hb = slice(h * 128, (h + 1) * 128)
            qr, kr, gate = B["qT"][:, 0, :], B["kT"][:, 0, :], B["gate"][:, 0, :]
            t1, t2 = W["t1"][:, 0, 0:n], W["t2"][:, 0, 0:n]
            for (o1, o2, dst, sc_) in ((0, 2048, qr, 1.0), (512, 2560, kr, 128.0 ** -0.5)):
                MMK(PAl[:, 0:n], [(Wr[:, k, o1 + h * 128:o1 + (h + 1) * 128], xk(k)) for k in range(8)])
                STT(t1, PAl[:, 0:n], sc_, rt[:, 0, 0:n], MUL, MUL)
                MMK(PAl[:, 0:n], [(Wr[:, k, o2 + h * 128:o2 + (h + 1) * 128], xk(k)) for k in range(8)])
                STT(t2, PAl[:, 0:n], sc_, rt[:, 1, 0:n], MUL, MUL)
                TT("pool", dst[:, 0:n], t1, t2, ADD)
            MMK(PAl[:, 0:n], [(Wr[:, k, 1536 + h * 128:1536 + (h + 1) * 128], xk(k)) for k in range(8)])
            ACT(gate[:, 0:n], PAl[:, 0:n], AF.Silu)
            def pro(gi):
                gt, off, L = t["groups"][gi]
                isS = gt == "S"
                ty = 1 if isS else 0
                tix = 2 if isS else (0 if L == 128 else 1)
                gc = slice(off, off + L)
                MM(PZ[0:L, 0:L], kr[:, gc], qr[:, gc])
                dec = cst[0:L, C_DEC + (ty * 4 + h) * 128:C_DEC + (ty * 4 + h) * 128 + L]
                inn = cst[:, C_INN + (ty * 4 + h) * 128:C_INN + (ty * 4 + h) * 128 + L]
                SD = B["SD"][0:L, 0, 0:L]
                TT("dve", SD, PZ[0:L, 0:L], dec, MUL)
                qp = B["qp"][:, 0, 0:L]
                TT("pool", qp, qr[:, gc], inn, MUL)
                TP(PT[0:L, 0:128], kr[:, gc], ident_b[:])
                Kh = B["Kh"][0:L, 0, :]
                TS("dve", Kh, PT[0:L, 0:128], cst[0:L, C_TAIL + tix * 4 + h:C_TAIL + tix * 4 + h + 1], MUL)

            def Pg(gi):
                if not ALT_N:
                    return P
                return dict(P, N=(P["N"] if gi % 2 == 0 else P["G"]))

            def main(gi):
                gt, off, L = t["groups"][gi]
                isS = gt == "S"
                tix = 2 if isS else (0 if L == 128 else 1)
                SD = B["SD"][0:L, 0, 0:L]
                qp = B["qp"][:, 0, 0:L]
                Kh = B["Kh"][0:L, 0, :]
                PN = Pg(gi)["N"]
                MM(PN[:, 0:L], SB_["Vtok"][0:L, gi, hb], SD, start=True, stop=False)
                if not isS:
                    MM(PN[:, 0:L], Rb[:, h, :], qp, start=False, stop=True)
                    MM(PZ[:, 128:256], Kh, SB_["Vtok"][0:L, gi, hb])
                    STT(Rb[:, h, :], Rst[:, h, :], gl[tix][h], PZ[:, 128:256], MUL, ADD)
                    STT(Rst[:, h, :], Rst[:, h, :], gl[tix][h], PZ[:, 128:256], MUL, ADD)
                else:
                    sample_quarters(S, SB_, Pg(gi), B["Vb"][:, :, :], h, gi, stR[:, h, :, :], oR_s[:, h, :, :], lambda sc: qp[:, sc], Kh,
                                    None, None, None, gl[2][h])
                ACT(W["hsq"][:, 0, 0:L], PN[:, 0:L], AF.Square)
                CP("act", W["hT"][:, 0, 0:L], PN[:, 0:L])

            def tail(gi):
                gt, off, L = t["groups"][gi]
                gc = slice(off, off + L)
                ln_gate(W, Pg(gi), L, prms[:, P_GNR + h:P_GNR + h + 1], gate[:, gc], SB_["mixed"][:, h, gc], True, sq_src=True)

            pipeline_groups(len(t["groups"]), pro, main, tail)

        run_heads(t, 4, body, vt_r)
        outproj(t, Wout, SB_)

    def pass_h_tile(t, Wh, Wout, hp):
        c0, n = t["c0"], t["n"]
        S, LW = carve_lanes(wkf, [], [("hT", (1, 128)), ("hsq", (1, 128)), ("lnv", (1, 128)), ("rs", (1, 128)),
                                      ("yy", (1, 128)), ("sg", (1, 512)), ("lf", (1, 512)), ("kk", (1, 512)),
                                      ("bb", (1, 512)), ("eb", (1, 512)), ("enb", (1, 512))], X_F)
        SB_, LB = carve_lanes(wkb, SH_B, LN_B, X_B)
        xk = lambda k: xn[:, k, c0:c0 + n]
        vt_h = lambda: vtok_proj(t, lambda k: Wh[:, k, 1024:1536], SB_, [PB[2], PB[5]])
        nsH = cst[:, (C_NSA if t["kind"] == "A" else C_NSB):(C_NSA if t["kind"] == "A" else C_NSB) + n]

        def body(hl, ln):
            PAl = PA[ln]
            W, B, P = LW[ln], LB[ln], LPS[ln]
            PN, PZ, PT = P["N"], P["Z"], P["T"]
            hg = hp * 4 + hl
            hb = slice(hl * 128, (hl + 1) * 128)
            Qt, Kt, gate, Khat = B["qT"][:, 0, :], B["kT"][:, 0, :], B["gate"][:, 0, :], B["Khat"][:, 0, :]
            sg, lf, kk, bb, eb, enb = (W[x][:, 0, 0:n] for x in ("sg", "lf", "kk", "bb", "eb", "enb"))
            MMK(PAl[:, 0:n], [(Wh[:, k, 512 + hl * 128:512 + (hl + 1) * 128], xk(k)) for k in range(8)])
            ACT(sg, PAl[:, 0:n], AF.Sigmoid)
            MMK(PAl[:, 0:n], [(Wh[:, k, hl * 128:(hl + 1) * 128], xk(k)) for k in range(8)])
            ACT(lf, sg, AF.Ln, bias=lbv[:, hg:hg + 1], scale=lbv[:, 8 + hg:9 + hg])
            TS("pool", kk, sg, lbv[:, 16 + hg:17 + hg], MUL, lbv[:, 8 + hg:9 + hg], ADD)
            SCAN(bb, nsH, lf, 0.0, MUL, ADD)
            ACT(eb, bb, AF.Exp)
            ACT(enb, bb, AF.Exp, scale=-1.0)
            TT("dve", Qt[:, 0:n], PAl[:, 0:n], eb, MUL)
            MMK(PAl[:, 0:n], [(Wh[:, k, 1536 + hl * 128:1536 + (hl + 1) * 128], xk(k)) for k in range(8)])
            TT("pool", Kt[:, 0:n], kk, enb, MUL)
            v3 = lambda ap: ap.rearrange("p (a b) -> p a b", a=16)
            if t["kind"] == "B":
                vh = lambda ap: ap.rearrange("p (a b) -> p a b", a=512 // HSEG)
                TT("pool", vh(Khat[:, 0:512]), vh(Kt[:, 0:512]),
                   bc(eb[:, HSEG - 1:512:HSEG].unsqueeze(2), (128, 512 // HSEG, HSEG)), MUL)
            else:
                TT("pool", v3(Khat[:, 0:128]), v3(Kt[:, 0:128]), bc(eb[:, 7:128:8].unsqueeze(2), (128, 16, 8)), MUL)
                TT("pool", Khat[:, 128:144], Kt[:, 128:144], bc(eb[:, 143:144], (128, 16)), MUL)
            ACT(gate[:, 0:n], PAl[:, 0:n], AF.Silu)
            def gparams(gi):
                gt, off, L = t["groups"][gi]
                isS = gt == "S"
                nseg, Ls = (16, 8) if isS else ((128 // HSEG, HSEG) if L == 128 else (1, L))
                return gt, off, L, isS, nseg, Ls

            def pro(gi):
                gt, off, L, isS, nseg, Ls = gparams(gi)
                gc = slice(off, off + L)
                MM(PZ[0:L, 0:L], Kt[:, gc], Qt[:, gc])
                msk = cst[0:L, (C_M8 if isS else C_M32):(C_M8 if isS else C_M32) + L]
                SD = B["SD"][0:L, 0, 0:L]
                TT("dve", SD, PZ[0:L, 0:L], msk, MUL)
                TP(PT[0:L, 0:128], Khat[:, gc], ident_b[:])
                Kh = B["Kh"][0:L, 0, :]
                CP("act", Kh, PT[0:L, 0:128])
                if (not isS) and nseg > 1:
                    Vb = B["Vb"][:, 0:nseg, :]
                    TT("pool", Vb, bc(SB_["Vtok"][:, gi, hb].unsqueeze(1), (128, nseg, 128)),
                       bc(cst[:, C_SEG4:C_SEG4 + nseg].unsqueeze(2), (128, nseg, 128)), MUL)

            def Pg(gi):
                if not ALT_N:
                    return P
                return dict(P, N=(P["N"] if gi % 2 == 0 else P["G"]))

            def main(gi):
                gt, off, L, isS, nseg, Ls = gparams(gi)
                SD = B["SD"][0:L, 0, 0:L]
                Kh = B["Kh"][0:L, 0, :]
                G = eb[:, off + Ls - 1:off + L:Ls]
                PN = Pg(gi)["N"]
                MM(PN[:, 0:L], SB_["Vtok"][0:L, gi, hb], SD, start=True, stop=False)
                if not isS:
                    if nseg > 1:
                        Vb = B["Vb"][:, 0:nseg, :]
                        MM(PZ[:, 0:nseg * 128], Kh, Vb.rearrange("p a b -> p (a b)"))
                        uo = 0
                    else:
                        MM(PZ[:, 128:256], Kh, SB_["Vtok"][0:L, gi, hb])
                        uo = 128
                    for j in range(nseg):
                        MM(PN[:, j * Ls:(j + 1) * Ls], Hb[:, hg, :], Qt[:, off + j * Ls:off + (j + 1) * Ls],
                           start=False, stop=(j == nseg - 1))
                        STT(Hb[:, hg, :], Hst[:, hg, :], G[:, j:j + 1], PZ[:, uo + j * 128:uo + (j + 1) * 128], MUL, ADD)
                        STT(Hst[:, hg, :], Hst[:, hg, :], G[:, j:j + 1], PZ[:, uo + j * 128:uo + (j + 1) * 128], MUL, ADD)
                else:
                    sample_quarters(S, SB_, Pg(gi), B["Vb"][:, :, :], hl, gi, stH[:, hg, :, :], oH_s[:, hg, :, :],
                                    lambda sc: Qt[:, sc], Kh, None, None, G, None)
                ACT(W["hsq"][:, 0, 0:L], PN[:, 0:L], AF.Square)
                CP("act", W["hT"][:, 0, 0:L], PN[:, 0:L])

            def tail(gi):
                gt, off, L, isS, nseg, Ls = gparams(gi)
                gc = slice(off, off + L)
                ln_gate(W, Pg(gi), L, prms[:, P_GNH + hg:P_GNH + hg + 1], gate[:, gc], SB_["mixed"][:, hl, gc], False, sq_src=True)

            pipeline_groups(len(t["groups"]), pro, main, tail)

        run_heads(t, 4, body, vt_h)
        outproj(t, Wout, SB_)

    ffn_state = {"pending": None, "par": 0}

    def pass_f_tile(t, Win, Wo, layer, f0, nf, half_idx):
        c0, n = t["c0"], t["n"]
        W, _ = carve(wkf, [("ue", (2, 520)), ("c1", (2, 512)), ("c2", (2, 512)), ("c3", (2, 512)), ("sl", (2, 512)),
                           ("cvin", (NF, 32))])
        WB_, _ = carve(wkb, [("hT", (10, 512))])
        hsel = ffn_state["par"] * 5
        ffn_state["par"] ^= 1
        isA = t["kind"] == "A"
        xk = lambda k: xn[:, k, c0:c0 + n]
        if isA and f0 == 0:
            DMA("sp", W["cvin"][:, :, :], stcv[:, layer, :, :])
        for fi in range(nf):
            f = f0 + fi
            PU_, PG_ = (PB[0], PB[1]) if fi % 2 == 0 else (PB[2], PB[3])
            cw = lambda j: prms[:, P_CW + (layer * 3 + j) * NF + f:P_CW + (layer * 3 + j) * NF + f + 1]
            cb = prms[:, P_CB + layer * NF + f:P_CB + layer * NF + f + 1]
            MMK(PU_[:, 0:n], [(Win[:, k, fi * 256:fi * 256 + 128], xk(k)) for k in range(8)])
            MMK(PG_[:, 0:n], [(Win[:, k, fi * 256 + 128:fi * 256 + 256], xk(k)) for k in range(8)])
            ue = W["ue"][:, fi % 2, :]
            c1, c2, c3, sl = (W[x][:, fi % 2, :] for x in ("c1", "c2", "c3", "sl"))
            tb = tailbuf[:, layer, f, :]
            if not isA:
                CP("pool", ue[:, 0:2], tb)
                CP("act", ue[:, 2:2 + n], PU_[:, 0:n])
                ACT(c1[:, 0:n], ue[:, 0:n], AF.Identity, bias=cb, scale=cw(0))
                STT(c2[:, 0:n], ue[:, 1:n + 1], cw(1), c1[:, 0:n], MUL, ADD)
                STT(c3[:, 0:n], PU_[:, 0:n], cw(2), c2[:, 0:n], MUL, ADD)
                CP("pool", tb, ue[:, n:n + 2])
            else:
                ues = ue[:, 0:160].rearrange("p (a b) -> p a b", a=16)
                v3 = lambda ap: ap.rearrange("p (a b) -> p a b", a=16)
                CP("pool", ues[:, :, 0:2], W["cvin"][:, f, :].rearrange("p (a b) -> p a b", a=16))
                MEMSET("pool", ue[:, 160:162], 0.0)
                CP("act", ues[:, :, 2:10], v3(PU_[:, 0:128]))
                CP("act", ue[:, 162:178], PU_[:, 128:144])
                ACT(v3(c1[:, 0:128]), ues[:, :, 0:8], AF.Identity, bias=cb, scale=cw(0))
                ACT(c1[:, 128:144], ue[:, 160:176], AF.Identity, bias=cb, scale=cw(0))
                STT(v3(c2[:, 0:128]), ues[:, :, 1:9], cw(1), v3(c1[:, 0:128]), MUL, ADD)
                STT(c2[:, 128:144], ue[:, 161:177], cw(1), c1[:, 128:144], MUL, ADD)
                STT(c3[:, 0:n], PU_[:, 0:n], cw(2), c2[:, 0:n], MUL, ADD)
                CP("pool", tb, ue[:, 176:178])
                CP("pool", cvout[:, f, :].rearrange("p (a b) -> p a b", a=16), ues[:, :, 8:10])
            ACT(sl[:, 0:n], c3[:, 0:n], AF.Silu)
            TT("dve", WB_["hT"][:, hsel + fi, 0:n], sl[:, 0:n], PG_[:, 0:n], MUL)
            if fi == min(1, nf - 1) and ffn_state["pending"] is not None:
                ffn_state["pending"]()
                ffn_state["pending"] = None

        def do_out(hsel=hsel, n=n, c0=c0):
            pacc = [PB[4], PB[5], PB[6], PB[7]]
            for d in range(8):
                pa = pacc[d % 4]
                MMK(pa[:, 0:n], [(Wo[:, fi, d * 128:(d + 1) * 128], WB_["hT"][:, hsel + fi, 0:n]) for fi in range(nf)])
                TT("dve", xT[:, d, c0:c0 + n], xT[:, d, c0:c0 + n], pa[:, 0:n], ADD)
        ffn_state["pending"] = do_out
        if isA and f0 + nf == NF:
            DMA("sp", ocv_s[:, layer, :, :], cvout[:, :, :])

    FG = [(0, 5), (5, 5), (10, 4), (14, 4), (18, 4)]
    halves = _halves()
    passes = []
    for hi, tiles in enumerate(halves):
        for layer in range(2):
            if layer == 0:
                passes.append(("m", hi, layer, [w_ab[:, :, 0:2056], w_oab[:, 0:4, :]]))
                passes.append(("r", hi, layer, [w_ab[:, :, 2056:5128], w_oab[:, 4:8, :]]))
            else:
                passes.append(("h0", hi, layer, [w_c[:, :, 0, :], w_oc[:, 0:4, :]]))
                passes.append(("h1", hi, layer, [w_c[:, :, 1, :], w_oc[:, 4:8, :]]))
            for (f0, nf) in FG:
                passes.append(("f", hi, layer, [w_fi[layer][:, :, f0 * 256:(f0 + nf) * 256], w_fo[layer][:, f0:f0 + nf, :]], f0, nf))
    if ONLY is not None:
        passes = [p for p in passes if p[0] in ONLY]
    passes = [p for p in passes if p[1] in HALVES]
    npass = len(passes) if STAGE >= 99 else min(len(passes), STAGE)
    loaded = {}

    def ensure_loaded(i, prev_region):
        if i >= npass or i in loaded:
            return
        r = load_weights(passes[i][3], prev_region)
        if r is not None:
            loaded[i] = r

    for i in range(npass):
        p = passes[i]
        kind, hi, layer = p[0], p[1], p[2]
        tiles = halves[hi]
        if i not in loaded:
            ensure_loaded(i, None)
        views, region = loaded[i]
        ensure_loaded(i + 1, region)
        if kind in ("m", "h0") or (ONLY is not None and "m" not in ONLY and kind == "r"):
            if layer == 0 and (kind == "m" or (ONLY is not None and "m" not in ONLY)):
                DMA("sp", xT[:, :, 0:(HC if hi == 0 else 1024)], xT_d[:, :, (0 if hi == 0 else HC):(HC if hi == 0 else NT)])
            for t in tiles:
                norm_tile(t, layer, False)
        if kind == "f" and p[4] == 0:
            for t in tiles:
                norm_tile(t, 2 + layer, False)
        for t in tiles:
            if kind == "m":
                pass_m_tile(t, views[0], views[1])
            elif kind == "r":
                pass_r_tile(t, views[0], views[1])
            elif kind in ("h0", "h1"):
                pass_h_tile(t, views[0], views[1], int(kind[1]))
            else:
                pass_f_tile(t, views[0], views[1], layer, p[4], p[5], hi)
        if kind == "f" and ffn_state["pending"] is not None:
            ffn_state["pending"]()
            ffn_state["pending"] = None
        ensure_loaded(i + 1, None)
        last_of_half = (i + 1 == len(passes)) or (passes[i + 1][1] != hi) or (i + 1 == npass)
        if last_of_half:
            for t in tiles:
                norm_tile(t, 4, True)
    DMA("sp", oC_p, Cst[:])
    DMA("sp", on_p, nst[:])
    DMA("sp", om_p, mstate[:, 0:1])
    DMA("sp", oR_p, Rst[:])
    DMA("sp", oH_p, Hst[:])
    DMA("sp", ocv_p, tailbuf[:])
    tr.finish()
    tr.emit()
    for cm in reversed(ctxs):
        cm.__exit__(None, None, None)
    tr.close()
    return nc, tr


def _prep(inputs):
    I = {k: np.asarray(v) for k, v in inputs.items()}
    c128, c4, rot, gl = _consts()
    wab = _wl(I["w_in_ab"][0])
    sw = []
    for base in (2056, 2568):
        for h in range(4):
            o = base + h * 128
            sw.append(wab[:, :, o + 64:o + 128])
            sw.append(wab[:, :, o:o + 64])
    w_ab = np.ascontiguousarray(np.concatenate([wab] + sw, axis=2))
    wc = _wl(I["w_in_c"][0])
    parts = []
    for hp in range(2):
        parts.append(np.concatenate([wc[:, :, j * 1024 + hp * 512:j * 1024 + (hp + 1) * 512] for j in range(4)], axis=2))
    w_c = np.ascontiguousarray(np.stack(parts, axis=2))
    wfi = []
    for l in range(2):
        w = _wl(I["w_ffn_in"][l])
        w = np.stack([w[:, :, 0:2816].reshape(128, 8, NF, 128), w[:, :, 2816:].reshape(128, 8, NF, 128)], axis=3)
        wfi.append(w.reshape(128, 8, NF * 256))
    w_fi = np.ascontiguousarray(np.stack(wfi, 0))
    w_fo = np.ascontiguousarray(np.stack([_wl(I["w_ffn_out"][l]) for l in range(2)], 0))
    prm = np.zeros((128, P_END), np.float32)
    gains = np.stack([I["norm_mix"][0], I["norm_mix"][1], I["norm_ffn"][0], I["norm_ffn"][1], I["norm_final"]], 0)
    prm[:, P_GAIN:P_GAIN + 40] = _fm(gains).reshape(128, 40)
    prm[:, P_GNM:P_GNM + 4] = _fm(I["gn_mlstm"][0])
    prm[:, P_GNR:P_GNR + 4] = _fm(I["gn_ret"][0])
    prm[:, P_GNH:P_GNH + 8] = _fm(I["gn_hgrn"][0])
    prm[:, P_LB0:P_LB0 + 8] = _fm(I["lb_logits"][0])
    prm[:, P_LB1:P_LB1 + 8] = _fm(I["lb_logits"][1])
    prm[:, P_CW:P_CW + 132] = _fm(I["conv_w"]).reshape(128, 132)
    prm[:, P_CB:P_CB + 44] = _fm(I["conv_b"]).reshape(128, 44)
    prm4 = np.ascontiguousarray(np.stack([I["b_igate"][0], I["b_fgate"][0]], 1).astype(np.float32))
    shared = dict(w_ab=w_ab, w_oab=_wl(I["w_out_ab"][0]), w_c=w_c, w_oc=_wl(I["w_out_c"][0]), w_fi=w_fi, w_fo=w_fo,
                  prm=prm, prm4=prm4, c128=c128, c4=c4, rot=rot)
    maps = []
    for c in range(8):
        sq = slice(16 * c, 16 * c + 16)
        xa = np.concatenate([I["x_sample"][sq].reshape(128, D), I["meta_tokens"], I["x_prompt"][c]], 0).astype(np.float32)
        m = dict(shared)
        m["xT"] = np.ascontiguousarray(xa.T.reshape(8, 128, NT).transpose(1, 0, 2))
        m["stC"] = np.ascontiguousarray(I["state_mlstm_C"][0, sq].transpose(2, 1, 0, 3))
        m["stn"] = np.ascontiguousarray(I["state_mlstm_n"][0, sq].transpose(2, 1, 0))
        m["stm"] = np.ascontiguousarray(I["state_mlstm_m"][0, sq].T)
        m["stR"] = np.ascontiguousarray(I["state_ret_S"][0, sq].transpose(2, 1, 0, 3))
        m["stH"] = np.ascontiguousarray(I["state_hgrn_S"][0, sq].transpose(2, 1, 0, 3))
        cv = I["state_ffn_conv"][:, sq]
        cv = cv.reshape(2, 16, 2, NF, 128).transpose(4, 0, 3, 1, 2).reshape(128, 2, NF, 32)
        m["stcv"] = np.ascontiguousarray(cv)
        maps.append(m)
    return maps, gl


_CACHE = {}


def kernel(**inputs):
    maps, gl = _prep(inputs)
    if "nc" not in _CACHE:
        _CACHE["nc"] = build_program(gl)[0]
    nc = _CACHE["nc"]
    res = run_bass_kernel_spmd(nc, maps, core_ids=list(range(8)))
    R = res.results
    f32 = np.float32
    yp = np.zeros((8, 2048, D), f32)
    ys = np.zeros((128, 8, D), f32)
    Cp = np.zeros((1, 8, 4, 128, 128), f32)
    Cs = np.zeros((1, 128, 4, 128, 128), f32)
    np_ = np.zeros((1, 8, 4, 128), f32)
    ns = np.zeros((1, 128, 4, 128), f32)
    mp = np.zeros((1, 8, 4), f32)
    ms = np.zeros((1, 128, 4), f32)
    Rp = np.zeros((1, 8, 4, 128, 128), f32)
    Rs = np.zeros((1, 128, 4, 128, 128), f32)
    Hp = np.zeros((1, 8, 8, 128, 128), f32)
    Hs = np.zeros((1, 128, 8, 128, 128), f32)
    cvp = np.zeros((2, 8, 2, 2816), f32)
    cvs = np.zeros((2, 128, 2, 2816), f32)
    for c in range(8):
        r = R[c]
        sq = slice(16 * c, 16 * c + 16)
        ya = np.asarray(r["yT"]).transpose(1, 0, 2).reshape(D, NT).T
        ys[sq] = ya[0:128].reshape(16, 8, D)
        yp[c] = ya[144:]
        Cp[0, c] = np.asarray(r["oC_p"]).transpose(1, 0, 2)
        Cs[0, sq] = np.asarray(r["oC_s"]).transpose(2, 1, 0, 3)
        np_[0, c] = np.asarray(r["on_p"]).T
        ns[0, sq] = np.asarray(r["on_s"]).transpose(2, 1, 0)
        mp[0, c] = np.asarray(r["om_p"])[:, 0]
        ms[0, sq] = np.asarray(r["om_s"]).T
        Rp[0, c] = np.asarray(r["oR_p"]).transpose(1, 0, 2)
        Rs[0, sq] = np.asarray(r["oR_s"]).transpose(2, 1, 0, 3)
        Hp[0, c] = np.asarray(r["oH_p"]).transpose(1, 0, 2)
        Hs[0, sq] = np.asarray(r["oH_s"]).transpose(2, 1, 0, 3)
        cvp[:, c] = np.asarray(r["ocv_p"]).transpose(1, 3, 2, 0).reshape(2, 2, 2816)
        cvs[:, sq] = np.asarray(r["ocv_s"]).reshape(128, 2, NF, 16, 2).transpose(1, 3, 4, 2, 0).reshape(2, 16, 2, 2816)
    return (yp, ys, Cp, Cs, np_, ns, mp, ms, Rp, Rs, Hp, Hs, cvp, cvs)
```

```python
import numpy as np
import concourse.bass as bass
import concourse.mybir as mybir

F32 = mybir.dt.float32
BF16 = mybir.dt.bfloat16
I32 = mybir.dt.int32
AF = mybir.ActivationFunctionType
ALU = mybir.AluOpType


class _Rec:
    __slots__ = ("lo", "hi", "w", "rs")

    def __init__(self, lo, hi, w, rs):
        self.lo, self.hi, self.w, self.rs = lo, hi, w, rs


def _ap_interval(ap):
    pat = ap.ap
    name = ap.tensor.name
    off = int(ap.offset)
    space = str(ap.space)
    if "DRAM" in space.upper() or "HBM" in space.upper():
        ext = sum((c - 1) * abs(s) for s, c in pat) + 1
        return ("d:" + name, off, off + ext)
    if "PSUM" in space.upper():
        return ("p:" + name, 0, 1 << 30)
    pstride = pat[0][0] if pat[0][0] > 0 else 1 << 30
    lo = off % pstride if pat[0][0] > 0 else off
    ext = sum((c - 1) * abs(s) for s, c in pat[1:]) + 1
    return ("s:" + name, lo, lo + ext)


class Tracker:
    COMPUTE = ("pe", "act", "dve", "pool")

    def __init__(self, nc, ring=20, same_eng_sync=True):
        self.nc = nc
        self.engobj = {"pe": nc.tensor, "act": nc.scalar, "dve": nc.vector,
                       "pool": nc.gpsimd, "sp": nc.sync}
        self.prog = {e: [] for e in self.engobj}
        self.count = {e: 0 for e in self.COMPUTE}
        self.sems = {}
        self.waited = {e: {} for e in self.engobj}
        self.bufs = {}
        self.same_eng_sync = same_eng_sync
        self._ctx = []
        for e in self.COMPUTE:
            self.sems[e] = self._mk_sem("c_" + e)
        self.rings = {}
        for q in ("sp", "pool", "act"):
            self.rings[q] = {"sems": [self._mk_sem("q_%s_%d" % (q, i)) for i in range(ring)],
                             "uses": [0] * ring, "n": 0}
        self.ninstr = {e: 0 for e in self.engobj}
        self._cap = None
        self._unit = None

    def _mk_sem(self, name):
        cm = self.nc.semaphore(name)
        h = cm.__enter__()
        self._ctx.append(cm)
        return h

    def _access(self, key, lo, hi, write, tok, rkey):
        recs = self.bufs.get(key)
        if recs is None:
            recs = []
        deps = []
        new = []
        cov = []
        for r in recs:
            if r.hi <= lo or r.lo >= hi:
                new.append(r)
                continue
            if r.w is not None:
                deps.append(r.w)
            if write:
                deps.extend(r.rs.values())
            if r.lo < lo:
                new.append(_Rec(r.lo, lo, r.w, dict(r.rs)))
            if r.hi > hi:
                new.append(_Rec(hi, r.hi, r.w, dict(r.rs)))
            if not write:
                mid = _Rec(max(r.lo, lo), min(r.hi, hi), r.w, dict(r.rs))
                mid.rs[rkey] = tok
                new.append(mid)
                cov.append((mid.lo, mid.hi))
        if write:
            new.append(_Rec(lo, hi, tok, {}))
        else:
            cov.sort()
            cur = lo
            for a, b in cov:
                if a > cur:
                    new.append(_Rec(cur, a, None, {rkey: tok}))
                cur = max(cur, b)
            if cur < hi:
                new.append(_Rec(cur, hi, None, {rkey: tok}))
        self.bufs[key] = new
        return deps

    def _collect(self, eng, outs, ins, tok, rkey):
        deps = []
        for ap in outs:
            k, lo, hi = _ap_interval(ap)
            deps += self._access(k, lo, hi, True, tok, rkey)
        for ap in ins:
            if ap is None or isinstance(ap, (int, float)):
                continue
            k, lo, hi = _ap_interval(ap)
            deps += self._access(k, lo, hi, False, tok, rkey)
        waits = []
        w = self.waited[eng]
        best = {}
        for (skey, sh, val) in deps:
            if skey == tok[0] and val >= tok[2]:
                continue
            if skey == "c_" + eng:
                if eng == "pe" or not self.same_eng_sync:
                    continue
            if w.get(skey, 0) >= val:
                continue
            if skey not in best or best[skey][1] < val:
                best[skey] = (sh, val)
        for skey, (sh, val) in best.items():
            w[skey] = val
            waits.append((sh, val))
        return waits

    def begin_capture(self):
        self._cap = []
        self._unit = None

    def end_capture(self):
        c = self._cap
        self._cap = None
        self._unit = None
        return c

    def atomic_begin(self):
        if self._cap is not None and self._unit is None:
            self._unit = []
            self._cap.append(self._unit)
            return True
        return False

    def atomic_end(self, opened):
        if opened:
            self._unit = None

    def _record(self, rec):
        if self._unit is not None:
            self._unit.append(rec)
        else:
            self._cap.append([rec])

    def flush_rr(self, caps):
        idx = [0] * len(caps)
        live = True
        while live:
            live = False
            for li, c in enumerate(caps):
                if idx[li] < len(c):
                    live = True
                    for rec in c[idx[li]]:
                        if rec[0] == "op":
                            self.op(*rec[1:])
                        else:
                            self.dma(rec[1], rec[2], rec[3], **rec[4])
                    idx[li] += 1

    def op(self, eng, fn, outs, ins, inc=True):
        if self._cap is not None:
            self._record(("op", eng, fn, outs, ins, inc))
            return
        val = self.count[eng] + 1
        skey = "c_" + eng
        tok = (skey, self.sems[eng], val)
        waits = self._collect(eng, outs, ins, tok, skey)
        if inc:
            self.count[eng] = val
        self.prog[eng].append((waits, fn, (self.sems[eng], 1) if inc else None))
        self.ninstr[eng] += 1

    def dma(self, q, out, in_, **kw):
        if self._cap is not None:
            self._record(("dma", q, out, in_, kw))
            return
        ring = self.rings[q]
        n = ring["n"]
        slot = n % len(ring["sems"])
        ring["n"] = n + 1
        uses = ring["uses"][slot]
        ring["uses"][slot] = uses + 1
        sh = ring["sems"][slot]
        skey = "q_%s_%d" % (q, slot)
        tok = (skey, sh, 16 * (uses + 1))
        waits = self._collect(q, [out], [in_], tok, skey)
        w = self.waited[q]
        if uses > 0 and w.get(skey, 0) < 16 * uses:
            w[skey] = 16 * uses
            waits.append((sh, 16 * uses))
        self.prog[q].append((waits, (lambda e, out=out, in_=in_, kw=kw: e.dma_start(out=out, in_=in_, **kw)),
                             (sh, 16)))
        self.ninstr[q] += 1

    def finish(self):
        waits = []
        for q, ring in self.rings.items():
            for i, sh in enumerate(ring["sems"]):
                if ring["uses"][i] > 0:
                    waits.append((sh, 16 * ring["uses"][i]))
        for e in self.COMPUTE:
            if self.count[e] > 0:
                waits.append((self.sems[e], self.count[e]))
        self.prog["sp"].append((waits, None, None))

    def emit(self):
        nc = self.nc
        with nc.Block() as block:
            def run(name):
                def f(e):
                    for waits, fn, inc in self.prog[name]:
                        for sh, val in waits:
                            e.wait_ge(sh, val)
                        if fn is None:
                            continue
                        ins = fn(e)
                        if inc is not None:
                            ins.then_inc(inc[0], inc[1])
                return f
            block.sync(run("sp"))
            block.tensor(run("pe"))
            block.scalar(run("act"))
            block.vector(run("dve"))
            block.gpsimd(run("pool"))

    def close(self):
        for cm in reversed(self._ctx):
            cm.__exit__(None, None, None)


import ml_dtypes
from concourse.bass_utils import run_bass_kernel_spmd

D = 1024
NT = 2192
HC = 1168
NF = 22
EPS = 1e-6
ARENA = 32768
STAGE = 99
ONLY = None
HALVES = (0, 1)
NLANES = 2
HSEG = 64
FAST_ROWS = True
ALT_N = True
SEQ_FLUSH = False
SAME_ENG_SYNC = True


def _halves():
    A = dict(c0=0, n=144, g0=0, kind="A", groups=[("S", 0, 128), ("P", 128, 16)])
    B = [dict(c0=(144 + 512 * i) if i < 2 else 512 * (i - 2), n=512, g0=144 + 512 * i, kind="B",
              groups=[("P", 128 * j, 128) for j in range(4)]) for i in range(4)]
    return [[A, B[0], B[1]], [B[2], B[3]]]


C_ID, C_NEGC, C_NEG8, C_M32, C_M8, C_DEC, C_INN, C_TAIL, C_SEG16, C_SEG4, C_NSA, C_NSB, C_END = (
    0, 128, 256, 384, 512, 640, 640 + 1024, 640 + 2048, 2700, 2716, 2720, 2864, 3376)
R_SEL, R_NSP, R_NSS, R_RSP, R_RSS, R_ID4, R_END = 0, 512, 640, 768, 896, 1024, 1028


def _consts():
    c = np.zeros((128, C_END), np.float32)
    s = np.arange(128)[:, None]
    t = np.arange(128)[None, :]
    c[:, C_ID:C_ID + 128] = np.eye(128)
    caus = (s <= t)
    bd8 = caus & ((s // 8) == (t // 8))
    bd32 = caus & ((s // HSEG) == (t // HSEG))
    c[:, C_NEGC:C_NEGC + 128] = np.where(caus, 0.0, -30000.0)
    c[:, C_NEG8:C_NEG8 + 128] = np.where(bd8, 0.0, -30000.0)
    c[:, C_M32:C_M32 + 128] = bd32
    c[:, C_M8:C_M8 + 128] = bd8
    lg = np.log1p(-np.exp2(-5.0 - np.arange(4, dtype=np.float32))).astype(np.float64)
    for ty, m in enumerate((caus, bd8)):
        for h in range(4):
            o = C_DEC + (ty * 4 + h) * 128
            c[:, o:o + 128] = np.where(m, np.exp(np.maximum(t - s, 0) * lg[h]), 0.0)
            tp = t if ty == 0 else (t % 8)
            o = C_INN + (ty * 4 + h) * 128
            c[:, o:o + 128] = np.exp((tp + 1.0) * lg[h]) * np.ones((128, 1))
    sp = np.arange(128)
    for h in range(4):
        c[:, C_TAIL + 0 + h] = np.exp(np.maximum(127.0 - sp, 0) * lg[h])
        c[:, C_TAIL + 4 + h] = np.exp(np.maximum(15.0 - sp, 0) * lg[h])
        c[:, C_TAIL + 8 + h] = np.exp((7.0 - sp % 8) * lg[h])
    c[:, C_SEG16:C_SEG16 + 16] = (s // 8) == np.arange(16)[None, :]
    c[:, C_SEG4:C_SEG4 + 4] = (s // HSEG) == np.arange(4)[None, :]
    nsa = np.ones(144, np.float32)
    nsa[0:128:8] = 0.0
    nsa[128] = 0.0
    nsb = np.ones(512, np.float32)
    nsb[0::HSEG] = 0.0
    c[:, C_NSA:C_NSA + 144] = nsa[None, :]
    c[:, C_NSB:C_NSB + 512] = nsb[None, :]
    r = np.zeros((4, R_END), np.float32)
    for h in range(4):
        r[h, R_SEL + h * 128:R_SEL + (h + 1) * 128] = 1.0
    r[:, R_NSP:R_NSP + 128] = 1.0
    nss = np.ones(128, np.float32)
    nss[0::8] = 0.0
    r[:, R_NSS:R_NSS + 128] = nss[None, :]
    r[:, R_RSP:R_RSP + 128] = 0.0
    r[:, R_RSS:R_RSS + 128] = np.where(nss == 0.0, -1e30, 0.0)[None, :]
    r[:, R_ID4:R_ID4 + 4] = np.eye(4)
    gl = [[float(np.exp(L * lg[h])) for h in range(4)] for L in (128, 16, 8)]
    pos = np.concatenate([np.tile(16384.0 + np.arange(8), 16), np.arange(16), 16.0 + np.arange(2048)]).astype(np.float32)
    inv = (1.0 / (np.float32(10000.0) ** np.linspace(0.0, 1.0, 64, dtype=np.float32))).astype(np.float32)
    ang = (pos[:, None] * inv[None, :]).astype(np.float32)
    cs, sn = np.cos(ang).astype(np.float32), np.sin(ang).astype(np.float32)
    rot = np.zeros((128, 2, NT), np.float32)
    rot[0:64, 0, :] = cs.T
    rot[64:128, 0, :] = cs.T
    rot[0:64, 1, :] = -sn.T
    rot[64:128, 1, :] = sn.T
    return c, r, rot, gl


P_GAIN, P_GNM, P_GNR, P_GNH, P_LB0, P_LB1, P_CW, P_CB, P_END = 0, 40, 44, 48, 56, 64, 72, 204, 248


def _fm(v):
    v = np.asarray(v, np.float32)
    n = v.shape[-1] // 128
    v = v.reshape(v.shape[:-1] + (n, 128))
    return np.ascontiguousarray(np.moveaxis(v, -1, 0))


def _wl(w):
    w = np.asarray(w, np.float32)
    k = w.shape[0] // 128
    return np.ascontiguousarray(w.reshape(k, 128, w.shape[1]).transpose(1, 0, 2))


def build_program(gl):
    nc = bass.Bass("TRN2", target_bir_lowering=False)
    tr = Tracker(nc, ring=20, same_eng_sync=SAME_ENG_SYNC)

    def din(name, shape):
        return nc.dram_tensor(name, list(shape), F32, kind="ExternalInput").ap()

    def dout(name, shape):
        return nc.dram_tensor(name, list(shape), F32, kind="ExternalOutput").ap()

    xT_d = din("xT", (128, 8, NT))
    w_ab = din("w_ab", (128, 8, 5128))
    w_oab = din("w_oab", (128, 8, 1024))
    w_c = din("w_c", (128, 8, 2, 2048))
    w_oc = din("w_oc", (128, 8, 1024))
    w_fi = din("w_fi", (2, 128, 8, NF * 256))
    w_fo = din("w_fo", (2, 128, NF, 1024))
    stC = din("stC", (128, 4, 16, 128))
    stn = din("stn", (128, 4, 16))
    stm = din("stm", (4, 16))
    stR = din("stR", (128, 4, 16, 128))
    stH = din("stH", (128, 8, 16, 128))
    stcv = din("stcv", (128, 2, NF, 32))
    prm = din("prm", (128, P_END))
    prm4 = din("prm4", (4, 2))
    c128 = din("c128", (128, C_END))
    c4 = din("c4", (4, R_END))
    rot = din("rot", (128, 2, NT))
    yT_d = dout("yT", (128, 8, NT))
    oC_p = dout("oC_p", (128, 4, 128))
    oC_s = dout("oC_s", (128, 4, 16, 128))
    on_p = dout("on_p", (128, 4))
    on_s = dout("on_s", (128, 4, 16))
    om_p = dout("om_p", (4, 1))
    om_s = dout("om_s", (4, 16))
    oR_p = dout("oR_p", (128, 4, 128))
    oR_s = dout("oR_s", (128, 4, 16, 128))
    oH_p = dout("oH_p", (128, 8, 128))
    oH_s = dout("oH_s", (128, 8, 16, 128))
    ocv_p = dout("ocv_p", (128, 2, NF, 2))
    ocv_s = dout("ocv_s", (128, 2, NF, 32))

    ctxs = []

    def sb(name, shape, dt):
        cm = nc.sbuf_tensor(name, list(shape), dt)
        t = cm.__enter__()
        ctxs.append(cm)
        return t

    def ps(name, shape, dt):
        cm = nc.psum_tensor(name, list(shape), dt)
        t = cm.__enter__()
        ctxs.append(cm)
        return t

    xT = sb("xTs", (128, 8, HC), F32)
    xn = sb("xn", (128, 8, HC), BF16)
    arena = sb("arena", (128, ARENA), BF16)
    cst = sb("cst", (128, C_END), F32)
    crow = sb("crow", (4, R_END), F32)
    prms = sb("prms", (128, P_END), F32)
    prm4s = sb("prm4s", (4, 4), F32)
    ident_b = sb("ident_b", (128, 128), BF16)
    ones_b = sb("ones_b", (128, 128), BF16)
    segm_b = sb("segm_b", (128, 20), BF16)
    onesdiv = sb("onesdiv", (128, 128), F32)
    Cst = sb("Cst", (128, 4, 128), F32)
    nst = sb("nst", (128, 4), F32)
    Cb = sb("Cb", (128, 4, 128), BF16)
    nbc = sb("nbc", (128, 4, 128), BF16)
    Rst = sb("Rst", (128, 4, 128), F32)
    Rb = sb("Rb", (128, 4, 128), BF16)
    Hst = sb("Hst", (128, 8, 128), F32)
    Hb = sb("Hb", (128, 8, 128), BF16)
    mstate = sb("mstate", (4, 4), F32)
    tailbuf = sb("tailbuf", (128, 2, NF, 2), F32)
    cvout = sb("cvout", (128, NF, 32), F32)
    lbv = sb("lbv", (128, 24), F32)
    WF = 8300
    WB = 10000
    wkf = sb("wkf", (128, WF), F32)
    wkb = sb("wkb", (128, WB), BF16)

    PB = [ps("pb%d" % i, (128, 512), F32) for i in range(8)]
    PA = [PB[0], PB[1]]
    LPS = [dict(G=PB[2], N=PB[3], Z=PB[4]), dict(G=PB[5], N=PB[6], Z=PB[7])]
    for lp in LPS:
        lp["T"] = lp["Z"][:, 448:512].bitcast(BF16)

    def isap(x):
        return x is not None and not isinstance(x, (int, float))

    def MM(out, lhsT, rhs, start=True, stop=True, inc=True):
        tr.op("pe", lambda e: e.matmul(out, lhsT=lhsT, rhs=rhs, start=start, stop=stop), [out], [lhsT, rhs], inc=inc)

    def MMK(out, pairs):
        n = len(pairs)
        o = tr.atomic_begin()
        for i, (l, r) in enumerate(pairs):
            MM(out, l, r, start=(i == 0), stop=(i == n - 1), inc=(i == n - 1))
        tr.atomic_end(o)

    def TP(out, in_, ident):
        tr.op("pe", lambda e: e.transpose(out=out, in_=in_, identity=ident), [out], [in_, ident])

    def ACT(out, in_, func, bias=None, scale=None):
        kw = {}
        if bias is not None:
            kw["bias"] = bias
        if scale is not None:
            kw["scale"] = scale
        ins = [in_] + [a for a in (bias, scale) if isap(a)]
        tr.op("act", lambda e: e.activation(out=out, in_=in_, func=func, **kw), [out], ins)

    def TT(eng, out, in0, in1, op):
        tr.op(eng, lambda e: e.tensor_tensor(out=out, in0=in0, in1=in1, op=op), [out], [in0, in1])

    def TS(eng, out, in0, s1, op0, s2=None, op1=None):
        ins = [in0] + [a for a in (s1, s2) if isap(a)]
        if op1 is None:
            tr.op(eng, lambda e: e.tensor_scalar(out=out, in0=in0, scalar1=s1, scalar2=None, op0=op0), [out], ins)
        else:
            tr.op(eng, lambda e: e.tensor_scalar(out=out, in0=in0, scalar1=s1, scalar2=s2, op0=op0, op1=op1), [out], ins)

    def STT(out, in0, scalar, in1, op0, op1):
        ins = [in0, in1] + ([scalar] if isap(scalar) else [])
        tr.op("dve", lambda e: e.scalar_tensor_tensor(out=out, in0=in0, scalar=scalar, in1=in1, op0=op0, op1=op1), [out], ins)

    def CP(eng, out, in_):
        if eng == "act":
            tr.op("act", lambda e: e.activation(out=out, in_=in_, func=AF.Copy), [out], [in_])
        else:
            tr.op(eng, lambda e: e.tensor_copy(out=out, in_=in_), [out], [in_])

    def SCAN(out, d0, d1, init, op0, op1):
        tr.op("dve", lambda e: e.tensor_tensor_scan(out=out, data0=d0, data1=d1, initial=init, op0=op0, op1=op1), [out], [d0, d1])

    def RECIP(out, in_):
        tr.op("dve", lambda e: e.reciprocal(out=out, in_=in_), [out], [in_])

    def MEMSET(eng, ap, val):
        tr.op(eng, lambda e: e.memset(ap, val), [ap], [])

    def DMA(q, out, in_):
        tr.dma(q, out, in_)

    MUL, ADD, SUB, MAX = ALU.mult, ALU.add, ALU.subtract, ALU.max

    def bc(ap, shape):
        return ap.broadcast_to(list(shape))

    def carve(base, plan, off=0):
        out = {}
        for name, shape in plan:
            sz = int(np.prod(shape))
            v = base[:, off:off + sz]
            v = v.rearrange("p (a b) -> p a b", a=shape[0])
            out[name] = v
            off += sz
        assert off <= base.shape[1], (off, base.shape)
        return out, off

    def carve_lanes(base, shared, lane, extra):
        S, o = carve(base, shared)
        L0, o1 = carve(base, lane, o)
        L1, o2 = carve(base, lane, o1)
        X, o3 = carve(base, extra, o1)
        S.update(X)
        return S, [L0, L1]

    ar = {"off": 0}

    def load_weights(spec, prev_region):
        tot = sum(a.shape[1] * a.shape[2] for a in spec)
        off = ar["off"]
        if off + tot > ARENA:
            off = 0
        if prev_region is not None:
            plo, phi = prev_region
            if not (off + tot <= plo or off >= phi):
                return None
        views = []
        o = off
        for a in spec:
            K, C = a.shape[1], a.shape[2]
            v = arena[:, o:o + K * C].rearrange("p (k c) -> p k c", k=K)
            for k in range(K):
                for c0 in range(0, C, 2048):
                    c1 = min(C, c0 + 2048)
                    DMA("pool", v[:, k, c0:c1], a[:, k, c0:c1])
            views.append(v)
            o += K * C
        ar["off"] = o
        return views, (off, o)

    DMA("sp", cst[:], c128)
    DMA("sp", crow[:], c4)
    DMA("sp", prms[:], prm)
    DMA("sp", prm4s[:, 0:2], prm4)
    CP("dve", ident_b[:], cst[:, C_ID:C_ID + 128])
    MEMSET("dve", ones_b[:], 1.0)
    CP("dve", segm_b[:], cst[:, C_SEG16:C_SEG16 + 20])
    MEMSET("dve", onesdiv[:], 1.0 / 128.0)
    for t_ in (Cst, nst, Rst, Hst, mstate, tailbuf):
        MEMSET("pool", t_[:], 0.0)
    for t_ in (Cb, nbc, Rb, Hb):
        MEMSET("pool", t_[:], 0.0)
    TS("dve", prm4s[:, 2:3], prm4s[:, 1:2], -1.0, MUL)
    TT("dve", lbv[:, 16:24], prms[:, P_LB1:P_LB1 + 8], prms[:, P_LB0:P_LB0 + 8], SUB)
    ACT(lbv[:, 0:8], lbv[:, 16:24], AF.Sigmoid)
    TS("dve", lbv[:, 8:16], lbv[:, 0:8], -1.0, MUL, 1.0, ADD)
    TS("dve", lbv[:, 16:24], lbv[:, 8:16], -1.0, MUL)

    sel4 = lambda h: crow[:, R_SEL + h * 128:R_SEL + (h + 1) * 128]
    ident4 = crow[:, R_ID4:R_ID4 + 4]

    def norm_tile(t, gidx, final):
        c0, n = t["c0"], t["n"]
        wf, _ = carve(wkf, [("lnv", (1, 512)), ("rstd", (1, 512)), ("yo", (2, 512))])
        wb_, _ = carve(wkb, [("sq", (8, 512))])
        for k in range(8):
            ACT(wb_["sq"][:, k, 0:n], xT[:, k, c0:c0 + n], AF.Square)
        MMK(PA[0][:, 0:n], [(ones_b[:], wb_["sq"][:, k, 0:n]) for k in range(8)])
        ACT(wf["lnv"][:, 0, 0:n], PA[0][:, 0:n], AF.Ln, bias=EPS, scale=1.0 / D)
        ACT(wf["rstd"][:, 0, 0:n], wf["lnv"][:, 0, 0:n], AF.Exp, scale=-0.5)
        for k in range(8):
            g = prms[:, P_GAIN + gidx * 8 + k:P_GAIN + gidx * 8 + k + 1]
            if final:
                yo = wf["yo"][:, k % 2, 0:n]
                STT(yo, xT[:, k, c0:c0 + n], g, wf["rstd"][:, 0, 0:n], MUL, MUL)
                DMA("sp", yT_d[:, k, t["g0"]:t["g0"] + n], yo)
            else:
                STT(xn[:, k, c0:c0 + n], xT[:, k, c0:c0 + n], g, wf["rstd"][:, 0, 0:n], MUL, MUL)

    def ln_gate(W, P, L, gaincol, gate, out, mean, sq_src=None):
        PN = P["N"]
        hT = W["hT"][:, 0, 0:L]
        if sq_src is None:
            ACT(W["hsq"][:, 0, 0:L], hT, AF.Square)
        if mean and L == 128:
            both = bass.AP(hT.tensor, hT.offset, [list(hT.ap[0]), [1, 256]])
            assert W["hsq"][:, 0, 0:L].offset == hT.offset + 128
            MM(PN[:, 256:512], onesdiv[:], both)
        else:
            if mean:
                MM(PN[:, 256:256 + L], onesdiv[:], hT)
            MM(PN[:, 384:384 + L], onesdiv[:], W["hsq"][:, 0, 0:L])
        if mean:
            ACT(W["msq"][:, 0, 0:L], PN[:, 256:256 + L], AF.Square)
            TT("dve", W["var"][:, 0, 0:L], PN[:, 384:384 + L], W["msq"][:, 0, 0:L], SUB)
            ACT(W["lnv"][:, 0, 0:L], W["var"][:, 0, 0:L], AF.Ln, bias=EPS)
            TT("dve", W["cc"][:, 0, 0:L], hT, PN[:, 256:256 + L], SUB)
            cc = W["cc"][:, 0, 0:L]
        else:
            ACT(W["lnv"][:, 0, 0:L], PN[:, 384:384 + L], AF.Ln, bias=EPS)
            cc = PN[:, 0:L] if sq_src == "psum" else hT
        ACT(W["rs"][:, 0, 0:L], W["lnv"][:, 0, 0:L], AF.Exp, scale=-0.5)
        STT(W["yy"][:, 0, 0:L], cc, gaincol, W["rs"][:, 0, 0:L], MUL, MUL)
        TT("pool", out, W["yy"][:, 0, 0:L], gate, MUL)

    def outproj(t, Wout, SB_):
        c0, n = t["c0"], t["n"]
        pacc = [PB[0], PB[1], PB[2], PB[5]]
        for d in range(8):
            pa = pacc[d % 4]
            MMK(pa[:, 0:n], [(Wout[:, kk, d * 128:(d + 1) * 128], SB_["mixed"][:, kk, 0:n]) for kk in range(4)])
            TT("dve", xT[:, d, c0:c0 + n], xT[:, d, c0:c0 + n], pa[:, 0:n], ADD)

    LN_F = [("hT", (1, 128)), ("hsq", (1, 128)), ("mu", (1, 128)), ("msq", (1, 128)), ("var", (1, 128)),
            ("lnv", (1, 128)), ("rs", (1, 128)), ("cc", (1, 128)), ("yy", (1, 128))]
    X_F = [("q0", (2, 512)), ("qn", (2, 512)), ("n0s", (4, 16)), ("nnew", (1, 16))]
    SH_B = [("Vtok", (4, 512)), ("mixed", (4, 512))]
    LN_B = [("qT", (1, 512)), ("kT", (1, 512)), ("gate", (1, 512)), ("Khat", (1, 512)),
            ("SD", (1, 128)), ("qp", (1, 128)), ("Kh", (1, 128)), ("Vb", (4, 128))]
    X_B = [("s0b", (2, 512)), ("nbq", (2, 512))]

    def vtok_proj(t, Wv, SB_, banks=None):
        for gi, (gt, off, L) in enumerate(t["groups"]):
            pv = (banks or PA)[gi % 2]
            MMK(pv[0:L, 0:512], [(xn[:, k, t["c0"] + off:t["c0"] + off + L], Wv(k)) for k in range(8)])
            CP("act", SB_["Vtok"][0:L, gi, :], pv[0:L, 0:512])

    def pipeline_groups(ng, pro, main, tail):
        pro(0)
        for gi in range(ng):
            main(gi)
            if gi + 1 < ng:
                pro(gi + 1)
            tail(gi)

    def run_heads(t, nheads, body, extra=None):
        nl = 1 if t["kind"] == "A" else NLANES
        for h0 in range(0, nheads, nl):
            caps = []
            if extra is not None and h0 == 0:
                tr.begin_capture()
                extra()
                caps.append(tr.end_capture())
            for ln in range(min(nl, nheads - h0)):
                tr.begin_capture()
                body(h0 + ln, ln)
                caps.append(tr.end_capture())
            if SEQ_FLUSH:
                for c in caps:
                    tr.flush_rr([c])
            else:
                tr.flush_rr(caps)

    def sample_quarters(S, SB_, P, Vb, h, gi, st_in, st_out, qsrc, Kh, PDen, nidx, scale_ap, scale_imm):
        hb = slice(h * 128, (h + 1) * 128)
        PN, PU = P["N"], P["Z"]
        for qd in range(4):
            b = qd % 2
            q0 = S["q0"][:, b, :]
            q0v = q0.rearrange("p (a b) -> p a b", a=4)
            DMA("sp", q0v, st_in[:, 4 * qd:4 * qd + 4, :])
            s0b = SB_["s0b"][:, b, :].rearrange("p (a b) -> p a b", a=4)
            CP("act", s0b, q0v)
            if nidx is not None:
                nbq = SB_["nbq"][:, b, :].rearrange("p (a b) -> p a b", a=4)
                CP("pool", nbq, bc(S["n0s"][:, nidx, 4 * qd:4 * qd + 4].unsqueeze(2), (128, 4, 128)))
            for jj in range(4):
                j = 4 * qd + jj
                sc = slice(8 * j, 8 * j + 8)
                MM(PN[:, sc], s0b[:, jj, :], qsrc(sc), start=False, stop=(j == 15))
                if nidx is not None:
                    MM(PDen[:, sc], nbq[:, jj, :], qsrc(sc), start=False, stop=(j == 15))
            TT("pool", Vb, bc(SB_["Vtok"][:, gi, hb].unsqueeze(1), (128, 4, 128)),
               bc(cst[:, C_SEG16 + 4 * qd:C_SEG16 + 4 * qd + 4].unsqueeze(2), (128, 4, 128)), MUL)
            MM(PU[:, 0:512], Kh, Vb.rearrange("p a b -> p (a b)"))
            qn = S["qn"][:, b, :]
            if scale_ap is not None:
                TT("pool", qn.rearrange("p (a b) -> p a b", a=4), q0v,
                   bc(scale_ap[:, 4 * qd:4 * qd + 4].unsqueeze(2), (128, 4, 128)), MUL)
                TT("dve", qn, qn, PU[:, 0:512], ADD)
            else:
                STT(qn, q0, scale_imm, PU[:, 0:512], MUL, ADD)
            DMA("sp", st_out[:, 4 * qd:4 * qd + 4, :], qn.rearrange("p (a b) -> p a b", a=4))

    def pass_m_tile(t, Wm, Wout):
        c0, n = t["c0"], t["n"]
        S, LW = carve_lanes(wkf,
                            [("li", (1, 512)), ("ef", (1, 512)), ("sp", (1, 512)), ("R4", (4, 400)), ("cs", (4, 128)),
                             ("a2", (1, 128)), ("AW", (2, 128)), ("m0s", (1, 16)), ("mns", (1, 16)), ("acol", (4, 4)),
                             ("wcol", (4, 4))],
                            LN_F + [("tmp", (1, 128)), ("Dm", (1, 128)), ("ibc", (1, 128)), ("e2", (1, 128)),
                                    ("dmax", (1, 128)), ("rden", (1, 128)), ("absd", (1, 128)), ("carry", (1, 16))],
                            X_F)
        SB_, LB = carve_lanes(wkb, SH_B, LN_B, X_B)
        xk = lambda k: xn[:, k, c0:c0 + n]
        PG0 = LPS[0]["G"]
        MMK(PA[0][0:4, 0:n], [(Wm[:, k, 2048:2052], xk(k)) for k in range(8)])
        MMK(PA[1][0:4, 0:n], [(Wm[:, k, 2052:2056], xk(k)) for k in range(8)])
        li, ef, sp_ = S["li"][0:4, 0, :], S["ef"][0:4, 0, :], S["sp"][0:4, 0, :]
        MEMSET("dve", S["R4"][0:4, :, :], 0.0)
        ACT(li[:, 0:n], PA[0][0:4, 0:n], AF.Identity, bias=prm4s[:, 0:1])
        ACT(ef[:, 0:n], PA[1][0:4, 0:n], AF.Exp, bias=prm4s[:, 2:3], scale=-1.0)
        ACT(sp_[:, 0:n], ef[:, 0:n], AF.Ln, bias=1.0)
        vtok_proj(t, lambda k: Wm[:, k, 1024:1536], SB_)
        fast_rows = (t["kind"] == "B") and FAST_ROWS
        if fast_rows:
            m0g = S["m0s"][0:4, 0, 0:8]
            Abuf = S["ef"][0:4, 0, :]
            A2buf = sp_
            WRbuf = li
            nsP = crow[:, R_NSP:R_NSP + 128]
            rsP = crow[:, R_RSP:R_RSP + 128]
            G4 = [(gi, off) for gi, (gt, off, L) in enumerate(t["groups"])]
            for gi, off in G4:
                SCAN(S["cs"][0:4, gi, 0:128], nsP, sp_[:, off:off + 128], 0.0, MUL, ADD)
            for gi, off in G4:
                TT("dve", Abuf[:, off:off + 128], li[:, off:off + 128], S["cs"][0:4, gi, 0:128], ADD)
            CP("dve", m0g[:, 0:1], mstate[:, 0:1])
            for gi, off in G4:
                R = S["R4"][0:4, gi, :]
                TS("dve", A2buf[:, off:off + 128], Abuf[:, off:off + 128], m0g[:, gi:gi + 1], MAX)
                SCAN(R[:, 0:128], rsP, A2buf[:, off:off + 128], -1e30, ADD, MAX)
                TT("dve", m0g[:, gi + 1:gi + 2], R[:, 127:128], S["cs"][0:4, gi, 127:128], SUB)
            CP("dve", mstate[:, 0:1], m0g[:, 4:5])
            for gi, off in G4:
                R = S["R4"][0:4, gi, :]
                TS("dve", R[:, 128:256], R[:, 0:128], -1.0, MUL, m0g[:, gi:gi + 1], ADD)
                TS("dve", R[:, 384:385], R[:, 127:128], -1.0, MUL, m0g[:, gi:gi + 1], ADD)
                TS("dve", WRbuf[:, off:off + 128], Abuf[:, off:off + 128], R[:, 127:128], SUB)
                TT("dve", R[:, 256:384], S["cs"][0:4, gi, 0:128], R[:, 0:128], SUB)
            for gi, off in G4:
                TP(PG0[0:128, 400:404], Abuf[:, off:off + 128], ident4)
                CP("act", S["acol"][0:128, gi, :], PG0[0:128, 400:404])
                TP(LPS[1]["G"][0:128, 404:408], WRbuf[:, off:off + 128], ident4)
                ACT(S["wcol"][0:128, gi, :], LPS[1]["G"][0:128, 404:408], AF.Exp)
        for gi, (gt, off, L) in enumerate(t["groups"]):
            if fast_rows:
                break
            isS = gt == "S"
            nseg, Ls = (16, 8) if isS else (1, L)
            R = S["R4"][0:4, gi, :]
            cs = S["cs"][0:4, gi, 0:L]
            a2 = S["a2"][0:4, 0, 0:L]
            a = S["AW"][0:4, 0, 0:L]
            wr = S["AW"][0:4, 1, 0:L]
            ns = crow[:, (R_NSS if isS else R_NSP):(R_NSS if isS else R_NSP) + L]
            rs = crow[:, (R_RSS if isS else R_RSP):(R_RSS if isS else R_RSP) + L]
            SCAN(cs, ns, sp_[:, off:off + L], 0.0, MUL, ADD)
            TT("dve", a, li[:, off:off + L], cs, ADD)
            Mx = R[:, 0:L]
            if isS:
                m0s = S["m0s"][0:4, 0, :]
                DMA("sp", m0s, stm)
                v3 = lambda ap: ap.rearrange("p (a b) -> p a b", a=16)
                m0b = bc(m0s.unsqueeze(2), (4, 16, 8))
                TT("dve", v3(a2), v3(a), m0b, MAX)
            else:
                TS("dve", a2, a, mstate[:, 0:1], MAX)
            SCAN(Mx, rs, a2, -1e30, ADD, MAX)
            Ml = R[:, Ls - 1:L:Ls]
            csl = cs[:, Ls - 1:L:Ls]
            if isS:
                TT("dve", v3(R[:, 128:128 + L]), m0b, v3(Mx), SUB)
                TT("dve", R[:, 384:400], m0s, Ml, SUB)
                TT("dve", v3(wr), v3(a), bc(Ml.unsqueeze(2), (4, 16, 8)), SUB)
                mns = S["mns"][0:4, 0, :]
                TT("dve", mns, Ml, csl, SUB)
                DMA("sp", om_s, mns)
            else:
                TS("dve", R[:, 128:128 + L], Mx, -1.0, MUL, mstate[:, 0:1], ADD)
                TS("dve", R[:, 384:385], Ml, -1.0, MUL, mstate[:, 0:1], ADD)
                TS("dve", wr, a, Ml, SUB)
            TT("dve", R[:, 256:256 + L], cs, Mx, SUB)
            if not isS:
                TT("dve", mstate[:, 0:1], Ml, csl, SUB)
            TP(PG0[0:L, 400:404], a, ident4)
            TP(PG0[0:L, 404:408], wr, ident4)
            CP("act", S["acol"][0:L, gi, :], PG0[0:L, 400:404])
            ACT(S["wcol"][0:L, gi, :], PG0[0:L, 404:408], AF.Exp)
        if t["kind"] == "A":
            DMA("sp", S["n0s"][:, :, :], stn)

        def body(h, ln):
            PAl = PA[ln]
            W, B, P = LW[ln], LB[ln], LPS[ln]
            PG, PN, PZ, PT = P["G"], P["N"], P["Z"], P["T"]
            hb = slice(h * 128, (h + 1) * 128)
            qT, kT, gate = B["qT"][:, 0, :], B["kT"][:, 0, :], B["gate"][:, 0, :]
            MMK(PAl[:, 0:n], [(Wm[:, k, h * 128:(h + 1) * 128], xk(k)) for k in range(8)])
            CP("act", qT[:, 0:n], PAl[:, 0:n])
            MMK(PAl[:, 0:n], [(Wm[:, k, 512 + h * 128:512 + (h + 1) * 128], xk(k)) for k in range(8)])
            ACT(kT[:, 0:n], PAl[:, 0:n], AF.Copy, scale=128.0 ** -0.5)
            MMK(PAl[:, 0:n], [(Wm[:, k, 1536 + h * 128:1536 + (h + 1) * 128], xk(k)) for k in range(8)])
            ACT(gate[:, 0:n], PAl[:, 0:n], AF.Sigmoid)
            def pro(gi):
                gt, off, L = t["groups"][gi]
                isS = gt == "S"
                nseg = 16 if isS else 1
                gc = slice(off, off + L)
                R = S["R4"][0:4, gi, :]
                MM(PG[:, 0:400], sel4(h), R)
                negm = cst[0:L, (C_NEG8 if isS else C_NEGC):(C_NEG8 if isS else C_NEGC) + L]
                tmp, Dm = W["tmp"][0:L, 0, 0:L], W["Dm"][0:L, 0, 0:L]
                TT("dve", tmp, negm, PG[0:L, 0:L], SUB)
                ACT(Dm, tmp, AF.Exp, bias=S["acol"][0:L, gi, h:h + 1])
                MM(PZ[0:L, 0:L], kT[:, gc], qT[:, gc])
                SD = B["SD"][0:L, 0, 0:L]
                TT("dve", SD, PZ[0:L, 0:L], Dm, MUL)
                ibc, e2, carry = W["ibc"][:, 0, 0:L], W["e2"][:, 0, 0:L], W["carry"][:, 0, 0:nseg]
                ACT(ibc, PG[:, 128:128 + L], AF.Exp)
                qp = B["qp"][:, 0, 0:L]
                TT("pool", qp, qT[:, gc], ibc, MUL)
                ACT(e2, PG[:, 256:256 + L], AF.Exp)
                ACT(carry, PG[:, 384:384 + nseg], AF.Exp)
                TP(PT[0:L, 0:128], kT[:, gc], ident_b[:])
                Kh = B["Kh"][0:L, 0, :]
                TS("dve", Kh, PT[0:L, 0:128], S["wcol"][0:L, gi, h:h + 1], MUL)

            def main(gi):
                gt, off, L = t["groups"][gi]
                isS = gt == "S"
                nseg = 16 if isS else 1
                SD = B["SD"][0:L, 0, 0:L]
                qp = B["qp"][:, 0, 0:L]
                Kh = B["Kh"][0:L, 0, :]
                e2, carry = W["e2"][:, 0, 0:L], W["carry"][:, 0, 0:nseg]
                if not isS:
                    MM(PN[:, 0:L], SB_["Vtok"][0:L, gi, hb], SD, start=True, stop=False)
                    MM(PN[:, 0:L], Cb[:, h, :], qp, start=False, stop=True)
                    MM(PN[:, 128:128 + L], ones_b[0:L, :], SD, start=True, stop=False)
                    MM(PN[:, 128:128 + L], nbc[:, h, :], qp, start=False, stop=True)
                    PDen = PN[:, 128:128 + L]
                    MM(PZ[:, 128:256], Kh, SB_["Vtok"][0:L, gi, hb])
                    MM(PZ[:, 256:257], Kh, ones_b[0:L, 0:1])
                    STT(Cb[:, h, :], Cst[:, h, :], carry[:, 0:1], PZ[:, 128:256], MUL, ADD)
                    STT(Cst[:, h, :], Cst[:, h, :], carry[:, 0:1], PZ[:, 128:256], MUL, ADD)
                    TS("dve", nbc[:, h, :], bc(nst[:, h:h + 1], (128, 128)), carry[:, 0:1], MUL, PZ[:, 256:257], ADD)
                    STT(nst[:, h:h + 1], nst[:, h:h + 1], carry[:, 0:1], PZ[:, 256:257], MUL, ADD)
                else:
                    PDb = LPS[1]["N"]
                    MM(PN[:, 0:L], SB_["Vtok"][0:L, gi, hb], SD, start=True, stop=False)
                    MM(PDb[:, 0:L], ones_b[0:L, :], SD, start=True, stop=False)
                    sample_quarters(S, SB_, P, B["Vb"][:, :, :], h, gi, stC[:, h, :, :], oC_s[:, h, :, :], lambda sc: qp[:, sc], Kh,
                                    PDb, h, carry, None)
                    PDen = PDb[:, 0:L]
                    MM(PG[:, 408:424], Kh, segm_b[:, 0:16])
                    nn = S["nnew"][:, 0, :]
                    TT("pool", nn, S["n0s"][:, h, :], carry, MUL)
                    TT("dve", nn, nn, PG[:, 408:424], ADD)
                    DMA("sp", on_s[:, h, :], nn)
                dmax, rden, hT = W["dmax"][:, 0, 0:L], W["rden"][:, 0, 0:L], W["hT"][:, 0, 0:L]
                ACT(W["absd"][:, 0, 0:L], PDen, AF.Abs)
                TT("dve", dmax, W["absd"][:, 0, 0:L], e2, MAX)
                RECIP(rden, dmax)
                TT("dve", hT, PN[:, 0:L], rden, MUL)

            def tail(gi):
                gt, off, L = t["groups"][gi]
                gc = slice(off, off + L)
                ln_gate(W, P, L, prms[:, P_GNM + h:P_GNM + h + 1], gate[:, gc], SB_["mixed"][:, h, gc], True)

            pipeline_groups(len(t["groups"]), pro, main, tail)

        run_heads(t, 4, body)
        outproj(t, Wout, SB_)

    def pass_r_tile(t, Wr, Wout):
        c0, n = t["c0"], t["n"]
        S, LW = carve_lanes(wkf, [("rt", (2, 512))], LN_F + [("t1", (1, 512)), ("t2", (1, 512))], X_F)
        SB_, LB = carve_lanes(wkb, SH_B, LN_B, X_B)
        xk = lambda k: xn[:, k, c0:c0 + n]
        rt = S["rt"]
        DMA("sp", rt[:, :, 0:n], rot[:, :, t["g0"]:t["g0"] + n])
        vt_r = lambda: vtok_proj(t, lambda k: Wr[:, k, 1024:1536], SB_, [PB[2], PB[5]])

        def body(h, ln):
            PAl = PA[ln]
            W, B, P = LW[ln], LB[ln], LPS[ln]
            PN, PZ, PT = P["N"], P["Z"], P["T"]
            hb = slice(h * 128, (h + 1) * 128)
            qr, kr, gate = B["qT"][:, 0, :], B["kT"][:, 0, :], B["gate"][:, 0, :]
            t1, t2 = W["t1"][:, 0, 0:n], W["t2"][:, 0, 0:n]
            for (o1, o2, dst, sc_) in ((0, 2048, qr, 1.0), (512, 2560, kr, 128.0 ** -0.5)):
                MMK(PAl[:, 0:n], [(Wr[:, k, o1 + h * 128:o1 + (h + 1) * 128], xk(k)) for k in range(8)])
                STT(t1, PAl[:, 0:n], sc_, rt[:, 0, 0:n], MUL, MUL)
                MMK(PAl[:, 0:n], [(Wr[:, k, o2 + h * 128:o2 + (h + 1) * 128], xk(k)) for k in range(8)])
                STT(t2, PAl[:, 0:n], sc_, rt[:, 1, 0:n], MUL, MUL)
                TT("pool", dst[:, 0:n], t1, t2, ADD)
            MMK(PAl[:, 0:n], [(Wr[:, k, 1536 + h * 128:1536 + (h + 1) * 128], xk(k)) for k in range(8)])
            ACT(gate[:, 0:n], PAl[:, 0:n], AF.Silu)
            def pro(gi):
                gt, off, L = t["groups"][gi]
                isS = gt == "S"
                ty = 1 if isS else 0
                tix = 2 if isS else (0 if L == 128 else 1)
                gc = slice(off, off + L)
                MM(PZ[0:L, 0:L], kr[:, gc], qr[:, gc])
                dec = cst[0:L, C_DEC + (ty * 4 + h) * 128:C_DEC + (ty * 4 + h) * 128 + L]
                inn = cst[:, C_INN + (ty * 4 + h) * 128:C_INN + (ty * 4 + h) * 128 + L]
                SD = B["SD"][0:L, 0, 0:L]
                TT("dve", SD, PZ[0:L, 0:L], dec, MUL)
                qp = B["qp"][:, 0, 0:L]
                TT("pool", qp, qr[:, gc], inn, MUL)
                TP(PT[0:L, 0:128], kr[:, gc], ident_b[:])
                Kh = B["Kh"][0:L, 0, :]
                TS("dve", Kh, PT[0:L, 0:128], cst[0:L, C_TAIL + tix * 4 + h:C_TAIL + tix * 4 + h + 1], MUL)

            def Pg(gi):
                if not ALT_N:
                    return P
                return dict(P, N=(P["N"] if gi % 2 == 0 else P["G"]))

            def main(gi):
                gt, off, L = t["groups"][gi]
                isS = gt == "S"
                tix = 2 if isS else (0 if L == 128 else 1)
                SD = B["SD"][0:L, 0, 0:L]
                qp = B["qp"][:, 0, 0:L]
                Kh = B["Kh"][0:L, 0, :]
                PN = Pg(gi)["N"]
                MM(PN[:, 0:L], SB_["Vtok"][0:L, gi, hb], SD, start=True, stop=False)
                if not isS:
                    MM(PN[:, 0:L], Rb[:, h, :], qp, start=False, stop=True)
                    MM(PZ[:, 128:256], Kh, SB_["Vtok"][0:L, gi, hb])
                    STT(Rb[:, h, :], Rst[:, h, :], gl[tix][h], PZ[:, 128:256], MUL, ADD)
                    STT(Rst[:, h, :], Rst[:, h, :], gl[tix][h], PZ[:, 128:256], MUL, ADD)
                else:
                    sample_quarters(S, SB_, Pg(gi), B["Vb"][:, :, :], h, gi, stR[:, h, :, :], oR_s[:, h, :, :], lambda sc: qp[:, sc], Kh,
                                    None, None, None, gl[2][h])
                ACT(W["hsq"][:, 0, 0:L], PN[:, 0:L], AF.Square)
                CP("act", W["hT"][:, 0, 0:L], PN[:, 0:L])

            def tail(gi):
                gt, off, L = t["groups"][gi]
                gc = slice(off, off + L)
                ln_gate(W, Pg(gi), L, prms[:, P_GNR + h:P_GNR + h + 1], gate[:, gc], SB_["mixed"][:, h, gc], True, sq_src=True)

            pipeline_groups(len(t["groups"]), pro, main, tail)

        run_heads(t, 4, body, vt_r)
        outproj(t, Wout, SB_)

    def pass_h_tile(t, Wh, Wout, hp):
        c0, n = t["c0"], t["n"]
        S, LW = carve_lanes(wkf, [], [("hT", (1, 128)), ("hsq", (1, 128)), ("lnv", (1, 128)), ("rs", (1, 128)),
                                      ("yy", (1, 128)), ("sg", (1, 512)), ("lf", (1, 512)), ("kk", (1, 512)),
                                      ("bb", (1, 512)), ("eb", (1, 512)), ("enb", (1, 512))], X_F)
        SB_, LB = carve_lanes(wkb, SH_B, LN_B, X_B)
        xk = lambda k: xn[:, k, c0:c0 + n]
        vt_h = lambda: vtok_proj(t, lambda k: Wh[:, k, 1024:1536], SB_, [PB[2], PB[5]])
        nsH = cst[:, (C_NSA if t["kind"] == "A" else C_NSB):(C_NSA if t["kind"] == "A" else C_NSB) + n]

        def body(hl, ln):
            PAl = PA[ln]
            W, B, P = LW[ln], LB[ln], LPS[ln]
            PN, PZ, PT = P["N"], P["Z"], P["T"]
            hg = hp * 4 + hl
            hb = slice(hl * 128, (hl + 1) * 128)
            Qt, Kt, gate, Khat = B["qT"][:, 0, :], B["kT"][:, 0, :], B["gate"][:, 0, :], B["Khat"][:, 0, :]
            sg, lf, kk, bb, eb, enb = (W[x][:, 0, 0:n] for x in ("sg", "lf", "kk", "bb", "eb", "enb"))
            MMK(PAl[:, 0:n], [(Wh[:, k, 512 + hl * 128:512 + (hl + 1) * 128], xk(k)) for k in range(8)])
            ACT(sg, PAl[:, 0:n], AF.Sigmoid)
            MMK(PAl[:, 0:n], [(Wh[:, k, hl * 128:(hl + 1) * 128], xk(k)) for k in range(8)])
            ACT(lf, sg, AF.Ln, bias=lbv[:, hg:hg + 1], scale=lbv[:, 8 + hg:9 + hg])
            TS("pool", kk, sg, lbv[:, 16 + hg:17 + hg], MUL, lbv[:, 8 + hg:9 + hg], ADD)
            SCAN(bb, nsH, lf, 0.0, MUL, ADD)
            ACT(eb, bb, AF.Exp)
            ACT(enb, bb, AF.Exp, scale=-1.0)
            TT("dve", Qt[:, 0:n], PAl[:, 0:n], eb, MUL)
            MMK(PAl[:, 0:n], [(Wh[:, k, 1536 + hl * 128:1536 + (hl + 1) * 128], xk(k)) for k in range(8)])
            TT("pool", Kt[:, 0:n], kk, enb, MUL)
            v3 = lambda ap: ap.rearrange("p (a b) -> p a b", a=16)
            if t["kind"] == "B":
                vh = lambda ap: ap.rearrange("p (a b) -> p a b", a=512 // HSEG)
                TT("pool", vh(Khat[:, 0:512]), vh(Kt[:, 0:512]),
                   bc(eb[:, HSEG - 1:512:HSEG].unsqueeze(2), (128, 512 // HSEG, HSEG)), MUL)
            else:
                TT("pool", v3(Khat[:, 0:128]), v3(Kt[:, 0:128]), bc(eb[:, 7:128:8].unsqueeze(2), (128, 16, 8)), MUL)
                TT("pool", Khat[:, 128:144], Kt[:, 128:144], bc(eb[:, 143:144], (128, 16)), MUL)
            ACT(gate[:, 0:n], PAl[:, 0:n], AF.Silu)
            def gparams(gi):
                gt, off, L = t["groups"][gi]
                isS = gt == "S"
                nseg, Ls = (16, 8) if isS else ((128 // HSEG, HSEG) if L == 128 else (1, L))
                return gt, off, L, isS, nseg, Ls

            def pro(gi):
                gt, off, L, isS, nseg, Ls = gparams(gi)
                gc = slice(off, off + L)
                MM(PZ[0:L, 0:L], Kt[:, gc], Qt[:, gc])
                msk = cst[0:L, (C_M8 if isS else C_M32):(C_M8 if isS else C_M32) + L]
                SD = B["SD"][0:L, 0, 0:L]
                TT("dve", SD, PZ[0:L, 0:L], msk, MUL)
                TP(PT[0:L, 0:128], Khat[:, gc], ident_b[:])
                Kh = B["Kh"][0:L, 0, :]
                CP("act", Kh, PT[0:L, 0:128])
                if (not isS) and nseg > 1:
                    Vb = B["Vb"][:, 0:nseg, :]
                    TT("pool", Vb, bc(SB_["Vtok"][:, gi, hb].unsqueeze(1), (128, nseg, 128)),
                       bc(cst[:, C_SEG4:C_SEG4 + nseg].unsqueeze(2), (128, nseg, 128)), MUL)

            def Pg(gi):
                if not ALT_N:
                    return P
                return dict(P, N=(P["N"] if gi % 2 == 0 else P["G"]))

            def main(gi):
                gt, off, L, isS, nseg, Ls = gparams(gi)
                SD = B["SD"][0:L, 0, 0:L]
                Kh = B["Kh"][0:L, 0, :]
                G = eb[:, off + Ls - 1:off + L:Ls]
                PN = Pg(gi)["N"]
                MM(PN[:, 0:L], SB_["Vtok"][0:L, gi, hb], SD, start=True, stop=False)
                if not isS:
                    if nseg > 1:
                        Vb = B["Vb"][:, 0:nseg, :]
                        MM(PZ[:, 0:nseg * 128], Kh, Vb.rearrange("p a b -> p (a b)"))
                        uo = 0
                    else:
                        MM(PZ[:, 128:256], Kh, SB_["Vtok"][0:L, gi, hb])
                        uo = 128
                    for j in range(nseg):
                        MM(PN[:, j * Ls:(j + 1) * Ls], Hb[:, hg, :], Qt[:, off + j * Ls:off + (j + 1) * Ls],
                           start=False, stop=(j == nseg - 1))
                        STT(Hb[:, hg, :], Hst[:, hg, :], G[:, j:j + 1], PZ[:, uo + j * 128:uo + (j + 1) * 128], MUL, ADD)
                        STT(Hst[:, hg, :], Hst[:, hg, :], G[:, j:j + 1], PZ[:, uo + j * 128:uo + (j + 1) * 128], MUL, ADD)
                else:
                    sample_quarters(S, SB_, Pg(gi), B["Vb"][:, :, :], hl, gi, stH[:, hg, :, :], oH_s[:, hg, :, :],
                                    lambda sc: Qt[:, sc], Kh, None, None, G, None)
                ACT(W["hsq"][:, 0, 0:L], PN[:, 0:L], AF.Square)

            def tail(gi):
                gt, off, L, isS, nseg, Ls = gparams(gi)
                gc = slice(off, off + L)
                ln_gate(W, Pg(gi), L, prms[:, P_GNH + hg:P_GNH + hg + 1], gate[:, gc], SB_["mixed"][:, hl, gc], False, sq_src="psum")

            pipeline_groups(len(t["groups"]), pro, main, tail)

        run_heads(t, 4, body, vt_h)
        outproj(t, Wout, SB_)

    ffn_state = {"pending": None, "par": 0}

    def pass_f_tile(t, Win, Wo, layer, f0, nf, half_idx):
        c0, n = t["c0"], t["n"]
        W, _ = carve(wkf, [("ue", (2, 520)), ("c1", (2, 512)), ("c2", (2, 512)), ("c3", (2, 512)), ("sl", (2, 512)),
                           ("cvin", (NF, 32))])
        WB_, _ = carve(wkb, [("hT", (10, 512))])
        hsel = ffn_state["par"] * 5
        ffn_state["par"] ^= 1
        isA = t["kind"] == "A"
        xk = lambda k: xn[:, k, c0:c0 + n]
        if isA and f0 == 0:
            DMA("sp", W["cvin"][:, :, :], stcv[:, layer, :, :])
        for fi in range(nf):
            f = f0 + fi
            PU_, PG_ = (PB[0], PB[1]) if fi % 2 == 0 else (PB[2], PB[3])
            cw = lambda j: prms[:, P_CW + (layer * 3 + j) * NF + f:P_CW + (layer * 3 + j) * NF + f + 1]
            cb = prms[:, P_CB + layer * NF + f:P_CB + layer * NF + f + 1]
            MMK(PU_[:, 0:n], [(Win[:, k, fi * 256:fi * 256 + 128], xk(k)) for k in range(8)])
            MMK(PG_[:, 0:n], [(Win[:, k, fi * 256 + 128:fi * 256 + 256], xk(k)) for k in range(8)])
            ue = W["ue"][:, fi % 2, :]
            c1, c2, c3, sl = (W[x][:, fi % 2, :] for x in ("c1", "c2", "c3", "sl"))
            tb = tailbuf[:, layer, f, :]
            if not isA:
                CP("pool", ue[:, 0:2], tb)
                CP("act", ue[:, 2:2 + n], PU_[:, 0:n])
                ACT(c1[:, 0:n], ue[:, 0:n], AF.Identity, bias=cb, scale=cw(0))
                STT(c2[:, 0:n], ue[:, 1:n + 1], cw(1), c1[:, 0:n], MUL, ADD)
                STT(c3[:, 0:n], PU_[:, 0:n], cw(2), c2[:, 0:n], MUL, ADD)
                CP("pool", tb, ue[:, n:n + 2])
            else:
                ues = ue[:, 0:160].rearrange("p (a b) -> p a b", a=16)
                v3 = lambda ap: ap.rearrange("p (a b) -> p a b", a=16)
                CP("pool", ues[:, :, 0:2], W["cvin"][:, f, :].rearrange("p (a b) -> p a b", a=16))
                MEMSET("pool", ue[:, 160:162], 0.0)
                CP("act", ues[:, :, 2:10], v3(PU_[:, 0:128]))
                CP("act", ue[:, 162:178], PU_[:, 128:144])
                ACT(v3(c1[:, 0:128]), ues[:, :, 0:8], AF.Identity, bias=cb, scale=cw(0))
                ACT(c1[:, 128:144], ue[:, 160:176], AF.Identity, bias=cb, scale=cw(0))
                STT(v3(c2[:, 0:128]), ues[:, :, 1:9], cw(1), v3(c1[:, 0:128]), MUL, ADD)
                STT(c2[:, 128:144], ue[:, 161:177], cw(1), c1[:, 128:144], MUL, ADD)
                STT(c3[:, 0:n], PU_[:, 0:n], cw(2), c2[:, 0:n], MUL, ADD)
                CP("pool", tb, ue[:, 176:178])
                CP("pool", cvout[:, f, :].rearrange("p (a b) -> p a b", a=16), ues[:, :, 8:10])
            ACT(sl[:, 0:n], c3[:, 0:n], AF.Silu)
            TT("dve", WB_["hT"][:, hsel + fi, 0:n], sl[:, 0:n], PG_[:, 0:n], MUL)
            if fi == 0 and ffn_state["pending"] is not None:
                ffn_state["pending"]()
                ffn_state["pending"] = None

        def do_out(hsel=hsel, n=n, c0=c0):
            pacc = [PB[4], PB[5], PB[6], PB[7]]
            for d in range(8):
                pa = pacc[d % 4]
                MMK(pa[:, 0:n], [(Wo[:, fi, d * 128:(d + 1) * 128], WB_["hT"][:, hsel + fi, 0:n]) for fi in range(nf)])
                TT("dve", xT[:, d, c0:c0 + n], xT[:, d, c0:c0 + n], pa[:, 0:n], ADD)
        ffn_state["pending"] = do_out
        if isA and f0 + nf == NF:
            DMA("sp", ocv_s[:, layer, :, :], cvout[:, :, :])

    FG = [(0, 5), (5, 5), (10, 4), (14, 4), (18, 4)]
    halves = _halves()
    passes = []
    for hi, tiles in enumerate(halves):
        for layer in range(2):
            if layer == 0:
                passes.append(("m", hi, layer, [w_ab[:, :, 0:2056], w_oab[:, 0:4, :]]))
                passes.append(("r", hi, layer, [w_ab[:, :, 2056:5128], w_oab[:, 4:8, :]]))
            else:
                passes.append(("h0", hi, layer, [w_c[:, :, 0, :], w_oc[:, 0:4, :]]))
                passes.append(("h1", hi, layer, [w_c[:, :, 1, :], w_oc[:, 4:8, :]]))
            for (f0, nf) in FG:
                passes.append(("f", hi, layer, [w_fi[layer][:, :, f0 * 256:(f0 + nf) * 256], w_fo[layer][:, f0:f0 + nf, :]], f0, nf))
    if ONLY is not None:
        passes = [p for p in passes if p[0] in ONLY]
    passes = [p for p in passes if p[1] in HALVES]
    npass = len(passes) if STAGE >= 99 else min(len(passes), STAGE)
    loaded = {}

    def ensure_loaded(i, prev_region):
        if i >= npass or i in loaded:
            return
        r = load_weights(passes[i][3], prev_region)
        if r is not None:
            loaded[i] = r

    for i in range(npass):
        p = passes[i]
        kind, hi, layer = p[0], p[1], p[2]
        tiles = halves[hi]
        if i not in loaded:
            ensure_loaded(i, None)
        views, region = loaded[i]
        ensure_loaded(i + 1, region)
        if kind in ("m", "h0") or (ONLY is not None and "m" not in ONLY and kind == "r"):
            if layer == 0 and (kind == "m" or (ONLY is not None and "m" not in ONLY)):
                DMA("sp", xT[:, :, 0:(HC if hi == 0 else 1024)], xT_d[:, :, (0 if hi == 0 else HC):(HC if hi == 0 else NT)])
            for t in tiles:
                norm_tile(t, layer, False)
        if kind == "f" and p[4] == 0:
            for t in tiles:
                norm_tile(t, 2 + layer, False)
        for t in tiles:
            if kind == "m":
                pass_m_tile(t, views[0], views[1])
            elif kind == "r":
                pass_r_tile(t, views[0], views[1])
            elif kind in ("h0", "h1"):
                pass_h_tile(t, views[0], views[1], int(kind[1]))
            else:
                pass_f_tile(t, views[0], views[1], layer, p[4], p[5], hi)
        if kind == "f" and ffn_state["pending"] is not None:
            ffn_state["pending"]()
            ffn_state["pending"] = None
        ensure_loaded(i + 1, None)
        last_of_half = (i + 1 == len(passes)) or (passes[i + 1][1] != hi) or (i + 1 == npass)
        if last_of_half:
            for t in tiles:
                norm_tile(t, 4, True)
    DMA("sp", oC_p, Cst[:])
    DMA("sp", on_p, nst[:])
    DMA("sp", om_p, mstate[:, 0:1])
    DMA("sp", oR_p, Rst[:])
    DMA("sp", oH_p, Hst[:])
    DMA("sp", ocv_p, tailbuf[:])
    tr.finish()
    tr.emit()
    for cm in reversed(ctxs):
        cm.__exit__(None, None, None)
    tr.close()
    return nc, tr


def _prep(inputs):
    I = {k: np.asarray(v) for k, v in inputs.items()}
    c128, c4, rot, gl = _consts()
    wab = _wl(I["w_in_ab"][0])
    sw = []
    for base in (2056, 2568):
        for h in range(4):
            o = base + h * 128
            sw.append(wab[:, :, o + 64:o + 128])
            sw.append(wab[:, :, o:o + 64])
    w_ab = np.ascontiguousarray(np.concatenate([wab] + sw, axis=2))
    wc = _wl(I["w_in_c"][0])
    parts = []
    for hp in range(2):
        parts.append(np.concatenate([wc[:, :, j * 1024 + hp * 512:j * 1024 + (hp + 1) * 512] for j in range(4)], axis=2))
    w_c = np.ascontiguousarray(np.stack(parts, axis=2))
    wfi = []
    for l in range(2):
        w = _wl(I["w_ffn_in"][l])
        w = np.stack([w[:, :, 0:2816].reshape(128, 8, NF, 128), w[:, :, 2816:].reshape(128, 8, NF, 128)], axis=3)
        wfi.append(w.reshape(128, 8, NF * 256))
    w_fi = np.ascontiguousarray(np.stack(wfi, 0))
    w_fo = np.ascontiguousarray(np.stack([_wl(I["w_ffn_out"][l]) for l in range(2)], 0))
    prm = np.zeros((128, P_END), np.float32)
    gains = np.stack([I["norm_mix"][0], I["norm_mix"][1], I["norm_ffn"][0], I["norm_ffn"][1], I["norm_final"]], 0)
    prm[:, P_GAIN:P_GAIN + 40] = _fm(gains).reshape(128, 40)
    prm[:, P_GNM:P_GNM + 4] = _fm(I["gn_mlstm"][0])
    prm[:, P_GNR:P_GNR + 4] = _fm(I["gn_ret"][0])
    prm[:, P_GNH:P_GNH + 8] = _fm(I["gn_hgrn"][0])
    prm[:, P_LB0:P_LB0 + 8] = _fm(I["lb_logits"][0])
    prm[:, P_LB1:P_LB1 + 8] = _fm(I["lb_logits"][1])
    prm[:, P_CW:P_CW + 132] = _fm(I["conv_w"]).reshape(128, 132)
    prm[:, P_CB:P_CB + 44] = _fm(I["conv_b"]).reshape(128, 44)
    prm4 = np.ascontiguousarray(np.stack([I["b_igate"][0], I["b_fgate"][0]], 1).astype(np.float32))
    shared = dict(w_ab=w_ab, w_oab=_wl(I["w_out_ab"][0]), w_c=w_c, w_oc=_wl(I["w_out_c"][0]), w_fi=w_fi, w_fo=w_fo,
                  prm=prm, prm4=prm4, c128=c128, c4=c4, rot=rot)
    maps = []
    for c in range(8):
        sq = slice(16 * c, 16 * c + 16)
        xa = np.concatenate([I["x_sample"][sq].reshape(128, D), I["meta_tokens"], I["x_prompt"][c]], 0).astype(np.float32)
        m = dict(shared)
        m["xT"] = np.ascontiguousarray(xa.T.reshape(8, 128, NT).transpose(1, 0, 2))
        m["stC"] = np.ascontiguousarray(I["state_mlstm_C"][0, sq].transpose(2, 1, 0, 3))
        m["stn"] = np.ascontiguousarray(I["state_mlstm_n"][0, sq].transpose(2, 1, 0))
        m["stm"] = np.ascontiguousarray(I["state_mlstm_m"][0, sq].T)
        m["stR"] = np.ascontiguousarray(I["state_ret_S"][0, sq].transpose(2, 1, 0, 3))
        m["stH"] = np.ascontiguousarray(I["state_hgrn_S"][0, sq].transpose(2, 1, 0, 3))
        cv = I["state_ffn_conv"][:, sq]
        cv = cv.reshape(2, 16, 2, NF, 128).transpose(4, 0, 3, 1, 2).reshape(128, 2, NF, 32)
        m["stcv"] = np.ascontiguousarray(cv)
        maps.append(m)
    return maps, gl


_CACHE = {}


def kernel(**inputs):
    maps, gl = _prep(inputs)
    if "nc" not in _CACHE:
        _CACHE["nc"] = build_program(gl)[0]
    nc = _CACHE["nc"]
    res = run_bass_kernel_spmd(nc, maps, core_ids=list(range(8)))
    R = res.results
    f32 = np.float32
    yp = np.zeros((8, 2048, D), f32)
    ys = np.zeros((128, 8, D), f32)
    Cp = np.zeros((1, 8, 4, 128, 128), f32)
    Cs = np.zeros((1, 128, 4, 128, 128), f32)
    np_ = np.zeros((1, 8, 4, 128), f32)
    ns = np.zeros((1, 128, 4, 128), f32)
    mp = np.zeros((1, 8, 4), f32)
    ms = np.zeros((1, 128, 4), f32)
    Rp = np.zeros((1, 8, 4, 128, 128), f32)
    Rs = np.zeros((1, 128, 4, 128, 128), f32)
    Hp = np.zeros((1, 8, 8, 128, 128), f32)
    Hs = np.zeros((1, 128, 8, 128, 128), f32)
    cvp = np.zeros((2, 8, 2, 2816), f32)
    cvs = np.zeros((2, 128, 2, 2816), f32)
    for c in range(8):
        r = R[c]
        sq = slice(16 * c, 16 * c + 16)
        ya = np.asarray(r["yT"]).transpose(1, 0, 2).reshape(D, NT).T
        ys[sq] = ya[0:128].reshape(16, 8, D)
        yp[c] = ya[144:]
        Cp[0, c] = np.asarray(r["oC_p"]).transpose(1, 0, 2)
        Cs[0, sq] = np.asarray(r["oC_s"]).transpose(2, 1, 0, 3)
        np_[0, c] = np.asarray(r["on_p"]).T
        ns[0, sq] = np.asarray(r["on_s"]).transpose(2, 1, 0)
        mp[0, c] = np.asarray(r["om_p"])[:, 0]
        ms[0, sq] = np.asarray(r["om_s"]).T
        Rp[0, c] = np.asarray(r["oR_p"]).transpose(1, 0, 2)
        Rs[0, sq] = np.asarray(r["oR_s"]).transpose(2, 1, 0, 3)
        Hp[0, c] = np.asarray(r["oH_p"]).transpose(1, 0, 2)
        Hs[0, sq] = np.asarray(r["oH_s"]).transpose(2, 1, 0, 3)
        cvp[:, c] = np.asarray(r["ocv_p"]).transpose(1, 3, 2, 0).reshape(2, 2, 2816)
        cvs[:, sq] = np.asarray(r["ocv_s"]).reshape(128, 2, NF, 16, 2).transpose(1, 3, 4, 2, 0).reshape(2, 16, 2, 2816)
    return (yp, ys, Cp, Cs, np_, ns, mp, ms, Rp, Rs, Hp, Hs, cvp, cvs)
```

```python
import numpy as np
import concourse.bass as bass
import concourse.mybir as mybir

F32 = mybir.dt.float32
BF16 = mybir.dt.bfloat16
I32 = mybir.dt.int32
AF = mybir.ActivationFunctionType
ALU = mybir.AluOpType


class _Rec:
    __slots__ = ("lo", "hi", "w", "rs")

    def __init__(self, lo, hi, w, rs):
        self.lo, self.hi, self.w, self.rs = lo, hi, w, rs


def _ap_interval(ap):
    pat = ap.ap
    name = ap.tensor.name
    off = int(ap.offset)
    space = str(ap.space)
    if "DRAM" in space.upper() or "HBM" in space.upper():
        ext = sum((c - 1) * abs(s) for s, c in pat) + 1
        return ("d:" + name, off, off + ext)
    if "PSUM" in space.upper():
        return ("p:" + name, 0, 1 << 30)
    pstride = pat[0][0] if pat[0][0] > 0 else 1 << 30
    lo = off % pstride if pat[0][0] > 0 else off
    ext = sum((c - 1) * abs(s) for s, c in pat[1:]) + 1
    return ("s:" + name, lo, lo + ext)


class Tracker:
    COMPUTE = ("pe", "act", "dve", "pool")

    def __init__(self, nc, ring=20, same_eng_sync=True):
        self.nc = nc
        self.engobj = {"pe": nc.tensor, "act": nc.scalar, "dve": nc.vector,
                       "pool": nc.gpsimd, "sp": nc.sync}
        self.prog = {e: [] for e in self.engobj}
        self.count = {e: 0 for e in self.COMPUTE}
        self.sems = {}
        self.waited = {e: {} for e in self.engobj}
        self.bufs = {}
        self.same_eng_sync = same_eng_sync
        self._ctx = []
        for e in self.COMPUTE:
            self.sems[e] = self._mk_sem("c_" + e)
        self.rings = {}
        for q in ("sp", "pool", "act"):
            self.rings[q] = {"sems": [self._mk_sem("q_%s_%d" % (q, i)) for i in range(ring)],
                             "uses": [0] * ring, "n": 0}
        self.ninstr = {e: 0 for e in self.engobj}
        self._cap = None
        self._unit = None

    def _mk_sem(self, name):
        cm = self.nc.semaphore(name)
        h = cm.__enter__()
        self._ctx.append(cm)
        return h

    def _access(self, key, lo, hi, write, tok, rkey):
        recs = self.bufs.get(key)
        if recs is None:
            recs = []
        deps = []
        new = []
        cov = []
        for r in recs:
            if r.hi <= lo or r.lo >= hi:
                new.append(r)
                continue
            if r.w is not None:
                deps.append(r.w)
            if write:
                deps.extend(r.rs.values())
            if r.lo < lo:
                new.append(_Rec(r.lo, lo, r.w, dict(r.rs)))
            if r.hi > hi:
                new.append(_Rec(hi, r.hi, r.w, dict(r.rs)))
            if not write:
                mid = _Rec(max(r.lo, lo), min(r.hi, hi), r.w, dict(r.rs))
                mid.rs[rkey] = tok
                new.append(mid)
                cov.append((mid.lo, mid.hi))
        if write:
            new.append(_Rec(lo, hi, tok, {}))
        else:
            cov.sort()
            cur = lo
            for a, b in cov:
                if a > cur:
                    new.append(_Rec(cur, a, None, {rkey: tok}))
                cur = max(cur, b)
            if cur < hi:
                new.append(_Rec(cur, hi, None, {rkey: tok}))
        self.bufs[key] = new
        return deps

    def _collect(self, eng, outs, ins, tok, rkey):
        deps = []
        for ap in outs:
            k, lo, hi = _ap_interval(ap)
            deps += self._access(k, lo, hi, True, tok, rkey)
        for ap in ins:
            if ap is None or isinstance(ap, (int, float)):
                continue
            k, lo, hi = _ap_interval(ap)
            deps += self._access(k, lo, hi, False, tok, rkey)
        waits = []
        w = self.waited[eng]
        best = {}
        for (skey, sh, val) in deps:
            if skey == tok[0] and val >= tok[2]:
                continue
            if skey == "c_" + eng:
                if eng == "pe" or not self.same_eng_sync:
                    continue
            if w.get(skey, 0) >= val:
                continue
            if skey not in best or best[skey][1] < val:
                best[skey] = (sh, val)
        for skey, (sh, val) in best.items():
            w[skey] = val
            waits.append((sh, val))
        return waits

    def begin_capture(self):
        self._cap = []
        self._unit = None

    def end_capture(self):
        c = self._cap
        self._cap = None
        self._unit = None
        return c

    def atomic_begin(self):
        if self._cap is not None and self._unit is None:
            self._unit = []
            self._cap.append(self._unit)
            return True
        return False

    def atomic_end(self, opened):
        if opened:
            self._unit = None

    def _record(self, rec):
        if self._unit is not None:
            self._unit.append(rec)
        else:
            self._cap.append([rec])

    def flush_rr(self, caps):
        idx = [0] * len(caps)
        live = True
        while live:
            live = False
            for li, c in enumerate(caps):
                if idx[li] < len(c):
                    live = True
                    for rec in c[idx[li]]:
                        if rec[0] == "op":
                            self.op(*rec[1:])
                        else:
                            self.dma(rec[1], rec[2], rec[3], **rec[4])
                    idx[li] += 1

    def op(self, eng, fn, outs, ins, inc=True):
        if self._cap is not None:
            self._record(("op", eng, fn, outs, ins, inc))
            return
        val = self.count[eng] + 1
        skey = "c_" + eng
        tok = (skey, self.sems[eng], val)
        waits = self._collect(eng, outs, ins, tok, skey)
        if inc:
            self.count[eng] = val
        self.prog[eng].append((waits, fn, (self.sems[eng], 1) if inc else None))
        self.ninstr[eng] += 1

    def dma(self, q, out, in_, **kw):
        if self._cap is not None:
            self._record(("dma", q, out, in_, kw))
            return
        ring = self.rings[q]
        n = ring["n"]
        slot = n % len(ring["sems"])
        ring["n"] = n + 1
        uses = ring["uses"][slot]
        ring["uses"][slot] = uses + 1
        sh = ring["sems"][slot]
        skey = "q_%s_%d" % (q, slot)
        tok = (skey, sh, 16 * (uses + 1))
        waits = self._collect(q, [out], [in_], tok, skey)
        w = self.waited[q]
        if uses > 0 and w.get(skey, 0) < 16 * uses:
            w[skey] = 16 * uses
            waits.append((sh, 16 * uses))
        self.prog[q].append((waits, (lambda e, out=out, in_=in_, kw=kw: e.dma_start(out=out, in_=in_, **kw)),
                             (sh, 16)))
        self.ninstr[q] += 1

    def finish(self):
        waits = []
        for q, ring in self.rings.items():
            for i, sh in enumerate(ring["sems"]):
                if ring["uses"][i] > 0:
                    waits.append((sh, 16 * ring["uses"][i]))
        for e in self.COMPUTE:
            if self.count[e] > 0:
                waits.append((self.sems[e], self.count[e]))
        self.prog["sp"].append((waits, None, None))

    def emit(self):
        nc = self.nc
        with nc.Block() as block:
            def run(name):
                def f(e):
                    for waits, fn, inc in self.prog[name]:
                        for sh, val in waits:
                            e.wait_ge(sh, val)
                        if fn is None:
                            continue
                        ins = fn(e)
                        if inc is not None:
                            ins.then_inc(inc[0], inc[1])
                return f
            block.sync(run("sp"))
            block.tensor(run("pe"))
            block.scalar(run("act"))
            block.vector(run("dve"))
            block.gpsimd(run("pool"))

    def close(self):
        for cm in reversed(self._ctx):
            cm.__exit__(None, None, None)


import ml_dtypes
from concourse.bass_utils import run_bass_kernel_spmd

D = 1024
NT = 2192
HC = 1168
NF = 22
EPS = 1e-6
ARENA = 32768
STAGE = 99
ONLY = None
HALVES = (0, 1)
NLANES = 2
HSEG = 64
FAST_ROWS = True
ALT_N = True
SEQ_FLUSH = False
SAME_ENG_SYNC = True


def _halves():
    A = dict(c0=0, n=144, g0=0, kind="A", groups=[("S", 0, 128), ("P", 128, 16)])
    B = [dict(c0=(144 + 512 * i) if i < 2 else 512 * (i - 2), n=512, g0=144 + 512 * i, kind="B",
              groups=[("P", 128 * j, 128) for j in range(4)]) for i in range(4)]
    return [[A, B[0], B[1]], [B[2], B[3]]]


C_ID, C_NEGC, C_NEG8, C_M32, C_M8, C_DEC, C_INN, C_TAIL, C_SEG16, C_SEG4, C_NSA, C_NSB, C_END = (
    0, 128, 256, 384, 512, 640, 640 + 1024, 640 + 2048, 2700, 2716, 2720, 2864, 3376)
R_SEL, R_NSP, R_NSS, R_RSP, R_RSS, R_ID4, R_END = 0, 512, 640, 768, 896, 1024, 1028


def _consts():
    c = np.zeros((128, C_END), np.float32)
    s = np.arange(128)[:, None]
    t = np.arange(128)[None, :]
    c[:, C_ID:C_ID + 128] = np.eye(128)
    caus = (s <= t)
    bd8 = caus & ((s // 8) == (t // 8))
    bd32 = caus & ((s // HSEG) == (t // HSEG))
    c[:, C_NEGC:C_NEGC + 128] = np.where(caus, 0.0, -30000.0)
    c[:, C_NEG8:C_NEG8 + 128] = np.where(bd8, 0.0, -30000.0)
    c[:, C_M32:C_M32 + 128] = bd32
    c[:, C_M8:C_M8 + 128] = bd8
    lg = np.log1p(-np.exp2(-5.0 - np.arange(4, dtype=np.float32))).astype(np.float64)
    for ty, m in enumerate((caus, bd8)):
        for h in range(4):
            o = C_DEC + (ty * 4 + h) * 128
            c[:, o:o + 128] = np.where(m, np.exp(np.maximum(t - s, 0) * lg[h]), 0.0)
            tp = t if ty == 0 else (t % 8)
            o = C_INN + (ty * 4 + h) * 128
            c[:, o:o + 128] = np.exp((tp + 1.0) * lg[h]) * np.ones((128, 1))
    sp = np.arange(128)
    for h in range(4):
        c[:, C_TAIL + 0 + h] = np.exp(np.maximum(127.0 - sp, 0) * lg[h])
        c[:, C_TAIL + 4 + h] = np.exp(np.maximum(15.0 - sp, 0) * lg[h])
        c[:, C_TAIL + 8 + h] = np.exp((7.0 - sp % 8) * lg[h])
    c[:, C_SEG16:C_SEG16 + 16] = (s // 8) == np.arange(16)[None, :]
    c[:, C_SEG4:C_SEG4 + 4] = (s // HSEG) == np.arange(4)[None, :]
    nsa = np.ones(144, np.float32)
    nsa[0:128:8] = 0.0
    nsa[128] = 0.0
    nsb = np.ones(512, np.float32)
    nsb[0::HSEG] = 0.0
    c[:, C_NSA:C_NSA + 144] = nsa[None, :]
    c[:, C_NSB:C_NSB + 512] = nsb[None, :]
    r = np.zeros((4, R_END), np.float32)
    for h in range(4):
        r[h, R_SEL + h * 128:R_SEL + (h + 1) * 128] = 1.0
    r[:, R_NSP:R_NSP + 128] = 1.0
    nss = np.ones(128, np.float32)
    nss[0::8] = 0.0
    r[:, R_NSS:R_NSS + 128] = nss[None, :]
    r[:, R_RSP:R_RSP + 128] = 0.0
    r[:, R_RSS:R_RSS + 128] = np.where(nss == 0.0, -1e30, 0.0)[None, :]
    r[:, R_ID4:R_ID4 + 4] = np.eye(4)
    gl = [[float(np.exp(L * lg[h])) for h in range(4)] for L in (128, 16, 8)]
    pos = np.concatenate([np.tile(16384.0 + np.arange(8), 16), np.arange(16), 16.0 + np.arange(2048)]).astype(np.float32)
    inv = (1.0 / (np.float32(10000.0) ** np.linspace(0.0, 1.0, 64, dtype=np.float32))).astype(np.float32)
    ang = (pos[:, None] * inv[None, :]).astype(np.float32)
    cs, sn = np.cos(ang).astype(np.float32), np.sin(ang).astype(np.float32)
    rot = np.zeros((128, 2, NT), np.float32)
    rot[0:64, 0, :] = cs.T
    rot[64:128, 0, :] = cs.T
    rot[0:64, 1, :] = -sn.T
    rot[64:128, 1, :] = sn.T
    return c, r, rot, gl


P_GAIN, P_GNM, P_GNR, P_GNH, P_LB0, P_LB1, P_CW, P_CB, P_END = 0, 40, 44, 48, 56, 64, 72, 204, 248


def _fm(v):
    v = np.asarray(v, np.float32)
    n = v.shape[-1] // 128
    v = v.reshape(v.shape[:-1] + (n, 128))
    return np.ascontiguousarray(np.moveaxis(v, -1, 0))


def _wl(w):
    w = np.asarray(w, np.float32)
    k = w.shape[0] // 128
    return np.ascontiguousarray(w.reshape(k, 128, w.shape[1]).transpose(1, 0, 2))


def build_program(gl):
    nc = bass.Bass("TRN2", target_bir_lowering=False)
    tr = Tracker(nc, ring=20, same_eng_sync=SAME_ENG_SYNC)

    def din(name, shape):
        return nc.dram_tensor(name, list(shape), F32, kind="ExternalInput").ap()

    def dout(name, shape):
        return nc.dram_tensor(name, list(shape), F32, kind="ExternalOutput").ap()

    xT_d = din("xT", (128, 8, NT))
    w_ab = din("w_ab", (128, 8, 5128))
    w_oab = din("w_oab", (128, 8, 1024))
    w_c = din("w_c", (128, 8, 2, 2048))
    w_oc = din("w_oc", (128, 8, 1024))
    w_fi = din("w_fi", (2, 128, 8, NF * 256))
    w_fo = din("w_fo", (2, 128, NF, 1024))
    stC = din("stC", (128, 4, 16, 128))
    stn = din("stn", (128, 4, 16))
    stm = din("stm", (4, 16))
    stR = din("stR", (128, 4, 16, 128))
    stH = din("stH", (128, 8, 16, 128))
    stcv = din("stcv", (128, 2, NF, 32))
    prm = din("prm", (128, P_END))
    prm4 = din("prm4", (4, 2))
    c128 = din("c128", (128, C_END))
    c4 = din("c4", (4, R_END))
    rot = din("rot", (128, 2, NT))
    yT_d = dout("yT", (128, 8, NT))
    oC_p = dout("oC_p", (128, 4, 128))
    oC_s = dout("oC_s", (128, 4, 16, 128))
    on_p = dout("on_p", (128, 4))
    on_s = dout("on_s", (128, 4, 16))
    om_p = dout("om_p", (4, 1))
    om_s = dout("om_s", (4, 16))
    oR_p = dout("oR_p", (128, 4, 128))
    oR_s = dout("oR_s", (128, 4, 16, 128))
    oH_p = dout("oH_p", (128, 8, 128))
    oH_s = dout("oH_s", (128, 8, 16, 128))
    ocv_p = dout("ocv_p", (128, 2, NF, 2))
    ocv_s = dout("ocv_s", (128, 2, NF, 32))

    ctxs = []

    def sb(name, shape, dt):
        cm = nc.sbuf_tensor(name, list(shape), dt)
        t = cm.__enter__()
        ctxs.append(cm)
        return t

    def ps(name, shape, dt):
        cm = nc.psum_tensor(name, list(shape), dt)
        t = cm.__enter__()
        ctxs.append(cm)
        return t

    xT = sb("xTs", (128, 8, HC), F32)
    xn = sb("xn", (128, 8, HC), BF16)
    arena = sb("arena", (128, ARENA), BF16)
    cst = sb("cst", (128, C_END), F32)
    crow = sb("crow", (4, R_END), F32)
    prms = sb("prms", (128, P_END), F32)
    prm4s = sb("prm4s", (4, 4), F32)
    ident_b = sb("ident_b", (128, 128), BF16)
    ones_b = sb("ones_b", (128, 128), BF16)
    segm_b = sb("segm_b", (128, 20), BF16)
    onesdiv = sb("onesdiv", (128, 128), F32)
    Cst = sb("Cst", (128, 4, 128), F32)
    nst = sb("nst", (128, 4), F32)
    Cb = sb("Cb", (128, 4, 128), BF16)
    nbc = sb("nbc", (128, 4, 128), BF16)
    Rst = sb("Rst", (128, 4, 128), F32)
    Rb = sb("Rb", (128, 4, 128), BF16)
    Hst = sb("Hst", (128, 8, 128), F32)
    Hb = sb("Hb", (128, 8, 128), BF16)
    mstate = sb("mstate", (4, 4), F32)
    tailbuf = sb("tailbuf", (128, 2, NF, 2), F32)
    cvout = sb("cvout", (128, NF, 32), F32)
    lbv = sb("lbv", (128, 24), F32)
    WF = 8300
    WB = 10000
    wkf = sb("wkf", (128, WF), F32)
    wkb = sb("wkb", (128, WB), BF16)

    PB = [ps("pb%d" % i, (128, 512), F32) for i in range(8)]
    PA = [PB[0], PB[1]]
    LPS = [dict(G=PB[2], N=PB[3], Z=PB[4]), dict(G=PB[5], N=PB[6], Z=PB[7])]
    for lp in LPS:
        lp["T"] = lp["Z"][:, 448:512].bitcast(BF16)

    def isap(x):
        return x is not None and not isinstance(x, (int, float))

    def MM(out, lhsT, rhs, start=True, stop=True, inc=True):
        tr.op("pe", lambda e: e.matmul(out, lhsT=lhsT, rhs=rhs, start=start, stop=stop), [out], [lhsT, rhs], inc=inc)

    def MMK(out, pairs):
        n = len(pairs)
        o = tr.atomic_begin()
        for i, (l, r) in enumerate(pairs):
            MM(out, l, r, start=(i == 0), stop=(i == n - 1), inc=(i == n - 1))
        tr.atomic_end(o)

    def TP(out, in_, ident):
        tr.op("pe", lambda e: e.transpose(out=out, in_=in_, identity=ident), [out], [in_, ident])

    def ACT(out, in_, func, bias=None, scale=None):
        kw = {}
        if bias is not None:
            kw["bias"] = bias
        if scale is not None:
            kw["scale"] = scale
        ins = [in_] + [a for a in (bias, scale) if isap(a)]
        tr.op("act", lambda e: e.activation(out=out, in_=in_, func=func, **kw), [out], ins)

    def TT(eng, out, in0, in1, op):
        tr.op(eng, lambda e: e.tensor_tensor(out=out, in0=in0, in1=in1, op=op), [out], [in0, in1])

    def TS(eng, out, in0, s1, op0, s2=None, op1=None):
        ins = [in0] + [a for a in (s1, s2) if isap(a)]
        if op1 is None:
            tr.op(eng, lambda e: e.tensor_scalar(out=out, in0=in0, scalar1=s1, scalar2=None, op0=op0), [out], ins)
        else:
            tr.op(eng, lambda e: e.tensor_scalar(out=out, in0=in0, scalar1=s1, scalar2=s2, op0=op0, op1=op1), [out], ins)

    def STT(out, in0, scalar, in1, op0, op1):
        ins = [in0, in1] + ([scalar] if isap(scalar) else [])
        tr.op("dve", lambda e: e.scalar_tensor_tensor(out=out, in0=in0, scalar=scalar, in1=in1, op0=op0, op1=op1), [out], ins)

    def CP(eng, out, in_):
        if eng == "act":
            tr.op("act", lambda e: e.activation(out=out, in_=in_, func=AF.Copy), [out], [in_])
        else:
            tr.op(eng, lambda e: e.tensor_copy(out=out, in_=in_), [out], [in_])

    def SCAN(out, d0, d1, init, op0, op1):
        tr.op("dve", lambda e: e.tensor_tensor_scan(out=out, data0=d0, data1=d1, initial=init, op0=op0, op1=op1), [out], [d0, d1])

    def RECIP(out, in_):
        tr.op("dve", lambda e: e.reciprocal(out=out, in_=in_), [out], [in_])

    def MEMSET(eng, ap, val):
        tr.op(eng, lambda e: e.memset(ap, val), [ap], [])

    def DMA(q, out, in_):
        tr.dma(q, out, in_)

    MUL, ADD, SUB, MAX = ALU.mult, ALU.add, ALU.subtract, ALU.max

    def bc(ap, shape):
        return ap.broadcast_to(list(shape))

    def carve(base, plan, off=0):
        out = {}
        for name, shape in plan:
            sz = int(np.prod(shape))
            v = base[:, off:off + sz]
            v = v.rearrange("p (a b) -> p a b", a=shape[0])
            out[name] = v
            off += sz
        assert off <= base.shape[1], (off, base.shape)
        return out, off

    def carve_lanes(base, shared, lane, extra):
        S, o = carve(base, shared)
        L0, o1 = carve(base, lane, o)
        L1, o2 = carve(base, lane, o1)
        X, o3 = carve(base, extra, o1)
        S.update(X)
        return S, [L0, L1]

    ar = {"off": 0}

    def load_weights(spec, prev_region):
        tot = sum(a.shape[1] * a.shape[2] for a in spec)
        off = ar["off"]
        if off + tot > ARENA:
            off = 0
        if prev_region is not None:
            plo, phi = prev_region
            if not (off + tot <= plo or off >= phi):
                return None
        views = []
        o = off
        for a in spec:
            K, C = a.shape[1], a.shape[2]
            v = arena[:, o:o + K * C].rearrange("p (k c) -> p k c", k=K)
            for k in range(K):
                for c0 in range(0, C, 2048):
                    c1 = min(C, c0 + 2048)
                    DMA("pool", v[:, k, c0:c1], a[:, k, c0:c1])
            views.append(v)
            o += K * C
        ar["off"] = o
        return views, (off, o)

    DMA("sp", cst[:], c128)
    DMA("sp", crow[:], c4)
    DMA("sp", prms[:], prm)
    DMA("sp", prm4s[:, 0:2], prm4)
    CP("dve", ident_b[:], cst[:, C_ID:C_ID + 128])
    MEMSET("dve", ones_b[:], 1.0)
    CP("dve", segm_b[:], cst[:, C_SEG16:C_SEG16 + 20])
    MEMSET("dve", onesdiv[:], 1.0 / 128.0)
    for t_ in (Cst, nst, Rst, Hst, mstate, tailbuf):
        MEMSET("pool", t_[:], 0.0)
    for t_ in (Cb, nbc, Rb, Hb):
        MEMSET("pool", t_[:], 0.0)
    TS("dve", prm4s[:, 2:3], prm4s[:, 1:2], -1.0, MUL)
    TT("dve", lbv[:, 16:24], prms[:, P_LB1:P_LB1 + 8], prms[:, P_LB0:P_LB0 + 8], SUB)
    ACT(lbv[:, 0:8], lbv[:, 16:24], AF.Sigmoid)
    TS("dve", lbv[:, 8:16], lbv[:, 0:8], -1.0, MUL, 1.0, ADD)
    TS("dve", lbv[:, 16:24], lbv[:, 8:16], -1.0, MUL)

    sel4 = lambda h: crow[:, R_SEL + h * 128:R_SEL + (h + 1) * 128]
    ident4 = crow[:, R_ID4:R_ID4 + 4]

    def norm_tile(t, gidx, final):
        c0, n = t["c0"], t["n"]
        wf, _ = carve(wkf, [("lnv", (1, 512)), ("rstd", (1, 512)), ("yo", (2, 512))])
        wb_, _ = carve(wkb, [("sq", (8, 512))])
        for k in range(8):
            ACT(wb_["sq"][:, k, 0:n], xT[:, k, c0:c0 + n], AF.Square)
        MMK(PA[0][:, 0:n], [(ones_b[:], wb_["sq"][:, k, 0:n]) for k in range(8)])
        ACT(wf["lnv"][:, 0, 0:n], PA[0][:, 0:n], AF.Ln, bias=EPS, scale=1.0 / D)
        ACT(wf["rstd"][:, 0, 0:n], wf["lnv"][:, 0, 0:n], AF.Exp, scale=-0.5)
        for k in range(8):
            g = prms[:, P_GAIN + gidx * 8 + k:P_GAIN + gidx * 8 + k + 1]
            if final:
                yo = wf["yo"][:, k % 2, 0:n]
                STT(yo, xT[:, k, c0:c0 + n], g, wf["rstd"][:, 0, 0:n], MUL, MUL)
                DMA("sp", yT_d[:, k, t["g0"]:t["g0"] + n], yo)
            else:
                STT(xn[:, k, c0:c0 + n], xT[:, k, c0:c0 + n], g, wf["rstd"][:, 0, 0:n], MUL, MUL)

    def ln_gate(W, P, L, gaincol, gate, out, mean, sq_src=None):
        PN = P["N"]
        hT = W["hT"][:, 0, 0:L]
        if sq_src is None:
            ACT(W["hsq"][:, 0, 0:L], hT, AF.Square)
        if mean and L == 128:
            both = bass.AP(hT.tensor, hT.offset, [list(hT.ap[0]), [1, 256]])
            assert W["hsq"][:, 0, 0:L].offset == hT.offset + 128
            MM(PN[:, 256:512], onesdiv[:], both)
        else:
            if mean:
                MM(PN[:, 256:256 + L], onesdiv[:], hT)
            MM(PN[:, 384:384 + L], onesdiv[:], W["hsq"][:, 0, 0:L])
        if mean:
            ACT(W["msq"][:, 0, 0:L], PN[:, 256:256 + L], AF.Square)
            TT("dve", W["var"][:, 0, 0:L], PN[:, 384:384 + L], W["msq"][:, 0, 0:L], SUB)
            ACT(W["lnv"][:, 0, 0:L], W["var"][:, 0, 0:L], AF.Ln, bias=EPS)
            TT("dve", W["cc"][:, 0, 0:L], hT, PN[:, 256:256 + L], SUB)
            cc = W["cc"][:, 0, 0:L]
        else:
            ACT(W["lnv"][:, 0, 0:L], PN[:, 384:384 + L], AF.Ln, bias=EPS)
            cc = hT
        ACT(W["rs"][:, 0, 0:L], W["lnv"][:, 0, 0:L], AF.Exp, scale=-0.5)
        STT(W["yy"][:, 0, 0:L], cc, gaincol, W["rs"][:, 0, 0:L], MUL, MUL)
        TT("pool", out, W["yy"][:, 0, 0:L], gate, MUL)

    def outproj(t, Wout, SB_):
        c0, n = t["c0"], t["n"]
        pacc = [PB[0], PB[1], PB[2], PB[5]]
        for d in range(8):
            pa = pacc[d % 4]
            MMK(pa[:, 0:n], [(Wout[:, kk, d * 128:(d + 1) * 128], SB_["mixed"][:, kk, 0:n]) for kk in range(4)])
            TT("dve", xT[:, d, c0:c0 + n], xT[:, d, c0:c0 + n], pa[:, 0:n], ADD)

    LN_F = [("hT", (1, 128)), ("hsq", (1, 128)), ("mu", (1, 128)), ("msq", (1, 128)), ("var", (1, 128)),
            ("lnv", (1, 128)), ("rs", (1, 128)), ("cc", (1, 128)), ("yy", (1, 128))]
    X_F = [("q0", (2, 512)), ("qn", (2, 512)), ("n0s", (4, 16)), ("nnew", (1, 16))]
    SH_B = [("Vtok", (4, 512)), ("mixed", (4, 512))]
    LN_B = [("qT", (1, 512)), ("kT", (1, 512)), ("gate", (1, 512)), ("Khat", (1, 512)),
            ("SD", (1, 128)), ("qp", (1, 128)), ("Kh", (1, 128)), ("Vb", (4, 128))]
    X_B = [("s0b", (2, 512)), ("nbq", (2, 512))]

    def vtok_proj(t, Wv, SB_, banks=None):
        for gi, (gt, off, L) in enumerate(t["groups"]):
            pv = (banks or PA)[gi % 2]
            MMK(pv[0:L, 0:512], [(xn[:, k, t["c0"] + off:t["c0"] + off + L], Wv(k)) for k in range(8)])
            CP("act", SB_["Vtok"][0:L, gi, :], pv[0:L, 0:512])

    def pipeline_groups(ng, pro, main, tail):
        pro(0)
        for gi in range(ng):
            main(gi)
            if gi + 1 < ng:
                pro(gi + 1)
            tail(gi)

    def run_heads(t, nheads, body, extra=None):
        nl = 1 if t["kind"] == "A" else NLANES
        for h0 in range(0, nheads, nl):
            caps = []
            for ln in range(min(nl, nheads - h0)):
                tr.begin_capture()
                body(h0 + ln, ln)
                caps.append(tr.end_capture())
            if extra is not None and h0 == 0:
                tr.begin_capture()
                extra()
                caps.append(tr.end_capture())
            if SEQ_FLUSH:
                for c in caps:
                    tr.flush_rr([c])
            else:
                tr.flush_rr(caps)

    def sample_quarters(S, SB_, P, Vb, h, gi, st_in, st_out, qsrc, Kh, PDen, nidx, scale_ap, scale_imm):
        hb = slice(h * 128, (h + 1) * 128)
        PN, PU = P["N"], P["Z"]
        for qd in range(4):
            b = qd % 2
            q0 = S["q0"][:, b, :]
            q0v = q0.rearrange("p (a b) -> p a b", a=4)
            DMA("sp", q0v, st_in[:, 4 * qd:4 * qd + 4, :])
            s0b = SB_["s0b"][:, b, :].rearrange("p (a b) -> p a b", a=4)
            CP("act", s0b, q0v)
            if nidx is not None:
                nbq = SB_["nbq"][:, b, :].rearrange("p (a b) -> p a b", a=4)
                CP("pool", nbq, bc(S["n0s"][:, nidx, 4 * qd:4 * qd + 4].unsqueeze(2), (128, 4, 128)))
            for jj in range(4):
                j = 4 * qd + jj
                sc = slice(8 * j, 8 * j + 8)
                MM(PN[:, sc], s0b[:, jj, :], qsrc(sc), start=False, stop=(j == 15))
                if nidx is not None:
                    MM(PDen[:, sc], nbq[:, jj, :], qsrc(sc), start=False, stop=(j == 15))
            TT("pool", Vb, bc(SB_["Vtok"][:, gi, hb].unsqueeze(1), (128, 4, 128)),
               bc(cst[:, C_SEG16 + 4 * qd:C_SEG16 + 4 * qd + 4].unsqueeze(2), (128, 4, 128)), MUL)
            MM(PU[:, 0:512], Kh, Vb.rearrange("p a b -> p (a b)"))
            qn = S["qn"][:, b, :]
            if scale_ap is not None:
                TT("pool", qn.rearrange("p (a b) -> p a b", a=4), q0v,
                   bc(scale_ap[:, 4 * qd:4 * qd + 4].unsqueeze(2), (128, 4, 128)), MUL)
                TT("dve", qn, qn, PU[:, 0:512], ADD)
            else:
                STT(qn, q0, scale_imm, PU[:, 0:512], MUL, ADD)
            DMA("sp", st_out[:, 4 * qd:4 * qd + 4, :], qn.rearrange("p (a b) -> p a b", a=4))

    def pass_m_tile(t, Wm, Wout):
        c0, n = t["c0"], t["n"]
        S, LW = carve_lanes(wkf,
                            [("li", (1, 512)), ("ef", (1, 512)), ("sp", (1, 512)), ("R4", (4, 400)), ("cs", (4, 128)),
                             ("a2", (1, 128)), ("AW", (2, 128)), ("m0s", (1, 16)), ("mns", (1, 16)), ("acol", (4, 4)),
                             ("wcol", (4, 4))],
                            LN_F + [("tmp", (1, 128)), ("Dm", (1, 128)), ("ibc", (1, 128)), ("e2", (1, 128)),
                                    ("dmax", (1, 128)), ("rden", (1, 128)), ("absd", (1, 128)), ("carry", (1, 16))],
                            X_F)
        SB_, LB = carve_lanes(wkb, SH_B, LN_B, X_B)
        xk = lambda k: xn[:, k, c0:c0 + n]
        PG0 = LPS[0]["G"]
        MMK(PA[0][0:4, 0:n], [(Wm[:, k, 2048:2052], xk(k)) for k in range(8)])
        MMK(PA[1][0:4, 0:n], [(Wm[:, k, 2052:2056], xk(k)) for k in range(8)])
        li, ef, sp_ = S["li"][0:4, 0, :], S["ef"][0:4, 0, :], S["sp"][0:4, 0, :]
        MEMSET("dve", S["R4"][0:4, :, :], 0.0)
        ACT(li[:, 0:n], PA[0][0:4, 0:n], AF.Identity, bias=prm4s[:, 0:1])
        ACT(ef[:, 0:n], PA[1][0:4, 0:n], AF.Exp, bias=prm4s[:, 2:3], scale=-1.0)
        ACT(sp_[:, 0:n], ef[:, 0:n], AF.Ln, bias=1.0)
        vtok_proj(t, lambda k: Wm[:, k, 1024:1536], SB_)
        fast_rows = (t["kind"] == "B") and FAST_ROWS
        if fast_rows:
            m0g = S["m0s"][0:4, 0, 0:8]
            Abuf = S["ef"][0:4, 0, :]
            A2buf = sp_
            WRbuf = li
            nsP = crow[:, R_NSP:R_NSP + 128]
            rsP = crow[:, R_RSP:R_RSP + 128]
            G4 = [(gi, off) for gi, (gt, off, L) in enumerate(t["groups"])]
            for gi, off in G4:
                SCAN(S["cs"][0:4, gi, 0:128], nsP, sp_[:, off:off + 128], 0.0, MUL, ADD)
            for gi, off in G4:
                TT("dve", Abuf[:, off:off + 128], li[:, off:off + 128], S["cs"][0:4, gi, 0:128], ADD)
            CP("dve", m0g[:, 0:1], mstate[:, 0:1])
            for gi, off in G4:
                R = S["R4"][0:4, gi, :]
                TS("dve", A2buf[:, off:off + 128], Abuf[:, off:off + 128], m0g[:, gi:gi + 1], MAX)
                SCAN(R[:, 0:128], rsP, A2buf[:, off:off + 128], -1e30, ADD, MAX)
                TT("dve", m0g[:, gi + 1:gi + 2], R[:, 127:128], S["cs"][0:4, gi, 127:128], SUB)
            CP("dve", mstate[:, 0:1], m0g[:, 4:5])
            for gi, off in G4:
                R = S["R4"][0:4, gi, :]
                TS("dve", R[:, 128:256], R[:, 0:128], -1.0, MUL, m0g[:, gi:gi + 1], ADD)
                TS("dve", R[:, 384:385], R[:, 127:128], -1.0, MUL, m0g[:, gi:gi + 1], ADD)
                TS("dve", WRbuf[:, off:off + 128], Abuf[:, off:off + 128], R[:, 127:128], SUB)
                TT("dve", R[:, 256:384], S["cs"][0:4, gi, 0:128], R[:, 0:128], SUB)
            for gi, off in G4:
                TP(PG0[0:128, 400:404], Abuf[:, off:off + 128], ident4)
                CP("act", S["acol"][0:128, gi, :], PG0[0:128, 400:404])
                TP(LPS[1]["G"][0:128, 404:408], WRbuf[:, off:off + 128], ident4)
                ACT(S["wcol"][0:128, gi, :], LPS[1]["G"][0:128, 404:408], AF.Exp)
        for gi, (gt, off, L) in enumerate(t["groups"]):
            if fast_rows:
                break
            isS = gt == "S"
            nseg, Ls = (16, 8) if isS else (1, L)
            R = S["R4"][0:4, gi, :]
            cs = S["cs"][0:4, gi, 0:L]
            a2 = S["a2"][0:4, 0, 0:L]
            a = S["AW"][0:4, 0, 0:L]
            wr = S["AW"][0:4, 1, 0:L]
            ns = crow[:, (R_NSS if isS else R_NSP):(R_NSS if isS else R_NSP) + L]
            rs = crow[:, (R_RSS if isS else R_RSP):(R_RSS if isS else R_RSP) + L]
            SCAN(cs, ns, sp_[:, off:off + L], 0.0, MUL, ADD)
            TT("dve", a, li[:, off:off + L], cs, ADD)
            Mx = R[:, 0:L]
            if isS:
                m0s = S["m0s"][0:4, 0, :]
                DMA("sp", m0s, stm)
                v3 = lambda ap: ap.rearrange("p (a b) -> p a b", a=16)
                m0b = bc(m0s.unsqueeze(2), (4, 16, 8))
                TT("dve", v3(a2), v3(a), m0b, MAX)
            else:
                TS("dve", a2, a, mstate[:, 0:1], MAX)
            SCAN(Mx, rs, a2, -1e30, ADD, MAX)
            Ml = R[:, Ls - 1:L:Ls]
            csl = cs[:, Ls - 1:L:Ls]
            if isS:
                TT("dve", v3(R[:, 128:128 + L]), m0b, v3(Mx), SUB)
                TT("dve", R[:, 384:400], m0s, Ml, SUB)
                TT("dve", v3(wr), v3(a), bc(Ml.unsqueeze(2), (4, 16, 8)), SUB)
                mns = S["mns"][0:4, 0, :]
                TT("dve", mns, Ml, csl, SUB)
                DMA("sp", om_s, mns)
            else:
                TS("dve", R[:, 128:128 + L], Mx, -1.0, MUL, mstate[:, 0:1], ADD)
                TS("dve", R[:, 384:385], Ml, -1.0, MUL, mstate[:, 0:1], ADD)
                TS("dve", wr, a, Ml, SUB)
            TT("dve", R[:, 256:256 + L], cs, Mx, SUB)
            if not isS:
                TT("dve", mstate[:, 0:1], Ml, csl, SUB)
            TP(PG0[0:L, 400:404], a, ident4)
            TP(PG0[0:L, 404:408], wr, ident4)
            CP("act", S["acol"][0:L, gi, :], PG0[0:L, 400:404])
            ACT(S["wcol"][0:L, gi, :], PG0[0:L, 404:408], AF.Exp)
        if t["kind"] == "A":
            DMA("sp", S["n0s"][:, :, :], stn)

        def body(h, ln):
            PAl = PA[ln]
            W, B, P = LW[ln], LB[ln], LPS[ln]
            PG, PN, PZ, PT = P["G"], P["N"], P["Z"], P["T"]
            hb = slice(h * 128, (h + 1) * 128)
            qT, kT, gate = B["qT"][:, 0, :], B["kT"][:, 0, :], B["gate"][:, 0, :]
            MMK(PAl[:, 0:n], [(Wm[:, k, h * 128:(h + 1) * 128], xk(k)) for k in range(8)])
            CP("act", qT[:, 0:n], PAl[:, 0:n])
            MMK(PAl[:, 0:n], [(Wm[:, k, 512 + h * 128:512 + (h + 1) * 128], xk(k)) for k in range(8)])
            ACT(kT[:, 0:n], PAl[:, 0:n], AF.Copy, scale=128.0 ** -0.5)
            MMK(PAl[:, 0:n], [(Wm[:, k, 1536 + h * 128:1536 + (h + 1) * 128], xk(k)) for k in range(8)])
            ACT(gate[:, 0:n], PAl[:, 0:n], AF.Sigmoid)
            def pro(gi):
                gt, off, L = t["groups"][gi]
                isS = gt == "S"
                nseg = 16 if isS else 1
                gc = slice(off, off + L)
                R = S["R4"][0:4, gi, :]
                MM(PG[:, 0:400], sel4(h), R)
                negm = cst[0:L, (C_NEG8 if isS else C_NEGC):(C_NEG8 if isS else C_NEGC) + L]
                tmp, Dm = W["tmp"][0:L, 0, 0:L], W["Dm"][0:L, 0, 0:L]
                TT("dve", tmp, negm, PG[0:L, 0:L], SUB)
                ACT(Dm, tmp, AF.Exp, bias=S["acol"][0:L, gi, h:h + 1])
                MM(PZ[0:L, 0:L], kT[:, gc], qT[:, gc])
                SD = B["SD"][0:L, 0, 0:L]
                TT("dve", SD, PZ[0:L, 0:L], Dm, MUL)
                ibc, e2, carry = W["ibc"][:, 0, 0:L], W["e2"][:, 0, 0:L], W["carry"][:, 0, 0:nseg]
                ACT(ibc, PG[:, 128:128 + L], AF.Exp)
                qp = B["qp"][:, 0, 0:L]
                TT("pool", qp, qT[:, gc], ibc, MUL)
                ACT(e2, PG[:, 256:256 + L], AF.Exp)
                ACT(carry, PG[:, 384:384 + nseg], AF.Exp)
                TP(PT[0:L, 0:128], kT[:, gc], ident_b[:])
                Kh = B["Kh"][0:L, 0, :]
                TS("dve", Kh, PT[0:L, 0:128], S["wcol"][0:L, gi, h:h + 1], MUL)

            def main(gi):
                gt, off, L = t["groups"][gi]
                isS = gt == "S"
                nseg = 16 if isS else 1
                SD = B["SD"][0:L, 0, 0:L]
                qp = B["qp"][:, 0, 0:L]
                Kh = B["Kh"][0:L, 0, :]
                e2, carry = W["e2"][:, 0, 0:L], W["carry"][:, 0, 0:nseg]
                if not isS:
                    MM(PN[:, 0:L], SB_["Vtok"][0:L, gi, hb], SD, start=True, stop=False)
                    MM(PN[:, 0:L], Cb[:, h, :], qp, start=False, stop=True)
                    MM(PN[:, 128:128 + L], ones_b[0:L, :], SD, start=True, stop=False)
                    MM(PN[:, 128:128 + L], nbc[:, h, :], qp, start=False, stop=True)
                    PDen = PN[:, 128:128 + L]
                    MM(PZ[:, 128:256], Kh, SB_["Vtok"][0:L, gi, hb])
                    MM(PZ[:, 256:257], Kh, ones_b[0:L, 0:1])
                    STT(Cb[:, h, :], Cst[:, h, :], carry[:, 0:1], PZ[:, 128:256], MUL, ADD)
                    STT(Cst[:, h, :], Cst[:, h, :], carry[:, 0:1], PZ[:, 128:256], MUL, ADD)
                    TS("dve", nbc[:, h, :], bc(nst[:, h:h + 1], (128, 128)), carry[:, 0:1], MUL, PZ[:, 256:257], ADD)
                    STT(nst[:, h:h + 1], nst[:, h:h + 1], carry[:, 0:1], PZ[:, 256:257], MUL, ADD)
                else:
                    PDb = LPS[1]["N"]
                    MM(PN[:, 0:L], SB_["Vtok"][0:L, gi, hb], SD, start=True, stop=False)
                    MM(PDb[:, 0:L], ones_b[0:L, :], SD, start=True, stop=False)
                    sample_quarters(S, SB_, P, B["Vb"][:, :, :], h, gi, stC[:, h, :, :], oC_s[:, h, :, :], lambda sc: qp[:, sc], Kh,
                                    PDb, h, carry, None)
                    PDen = PDb[:, 0:L]
                    MM(PG[:, 408:424], Kh, segm_b[:, 0:16])
                    nn = S["nnew"][:, 0, :]
                    TT("pool", nn, S["n0s"][:, h, :], carry, MUL)
                    TT("dve", nn, nn, PG[:, 408:424], ADD)
                    DMA("sp", on_s[:, h, :], nn)
                dmax, rden, hT = W["dmax"][:, 0, 0:L], W["rden"][:, 0, 0:L], W["hT"][:, 0, 0:L]
                ACT(W["absd"][:, 0, 0:L], PDen, AF.Abs)
                TT("dve", dmax, W["absd"][:, 0, 0:L], e2, MAX)
                RECIP(rden, dmax)
                TT("dve", hT, PN[:, 0:L], rden, MUL)

            def tail(gi):
                gt, off, L = t["groups"][gi]
                gc = slice(off, off + L)
                ln_gate(W, P, L, prms[:, P_GNM + h:P_GNM + h + 1], gate[:, gc], SB_["mixed"][:, h, gc], True)

            pipeline_groups(len(t["groups"]), pro, main, tail)

        run_heads(t, 4, body)
        outproj(t, Wout, SB_)

    def pass_r_tile(t, Wr, Wout):
        c0, n = t["c0"], t["n"]
        S, LW = carve_lanes(wkf, [("rt", (2, 512))], LN_F + [("t1", (1, 512)), ("t2", (1, 512))], X_F)
        SB_, LB = carve_lanes(wkb, SH_B, LN_B, X_B)
        xk = lambda k: xn[:, k, c0:c0 + n]
        rt = S["rt"]
        DMA("sp", rt[:, :, 0:n], rot[:, :, t["g0"]:t["g0"] + n])
        vt_r = lambda: vtok_proj(t, lambda k: Wr[:, k, 1024:1536], SB_, [PB[2], PB[5]])

        def body(h, ln):
            PAl = PA[ln]
            W, B, P = LW[ln], LB[ln], LPS[ln]
            PN, PZ, PT = P["N"], P["Z"], P["T"]
            hb = slice(h * 128, (h + 1) * 128)
            qr, kr, gate = B["qT"][:, 0, :], B["kT"][:, 0, :], B["gate"][:, 0, :]
            t1, t2 = W["t1"][:, 0, 0:n], W["t2"][:, 0, 0:n]
            for (o1, o2, dst, sc_) in ((0, 2048, qr, 1.0), (512, 2560, kr, 128.0 ** -0.5)):
                MMK(PAl[:, 0:n], [(Wr[:, k, o1 + h * 128:o1 + (h + 1) * 128], xk(k)) for k in range(8)])
                STT(t1, PAl[:, 0:n], sc_, rt[:, 0, 0:n], MUL, MUL)
                MMK(PAl[:, 0:n], [(Wr[:, k, o2 + h * 128:o2 + (h + 1) * 128], xk(k)) for k in range(8)])
                STT(t2, PAl[:, 0:n], sc_, rt[:, 1, 0:n], MUL, MUL)
                TT("pool", dst[:, 0:n], t1, t2, ADD)
            MMK(PAl[:, 0:n], [(Wr[:, k, 1536 + h * 128:1536 + (h + 1) * 128], xk(k)) for k in range(8)])
            ACT(gate[:, 0:n], PAl[:, 0:n], AF.Silu)
            def pro(gi):
                gt, off, L = t["groups"][gi]
                isS = gt == "S"
                ty = 1 if isS else 0
                tix = 2 if isS else (0 if L == 128 else 1)
                gc = slice(off, off + L)
                MM(PZ[0:L, 0:L], kr[:, gc], qr[:, gc])
                dec = cst[0:L, C_DEC + (ty * 4 + h) * 128:C_DEC + (ty * 4 + h) * 128 + L]
                inn = cst[:, C_INN + (ty * 4 + h) * 128:C_INN + (ty * 4 + h) * 128 + L]
                SD = B["SD"][0:L, 0, 0:L]
                TT("dve", SD, PZ[0:L, 0:L], dec, MUL)
                qp = B["qp"][:, 0, 0:L]
                TT("pool", qp, qr[:, gc], inn, MUL)
                TP(PT[0:L, 0:128], kr[:, gc], ident_b[:])
                Kh = B["Kh"][0:L, 0, :]
                TS("dve", Kh, PT[0:L, 0:128], cst[0:L, C_TAIL + tix * 4 + h:C_TAIL + tix * 4 + h + 1], MUL)

            def Pg(gi):
                if not ALT_N:
                    return P
                return dict(P, N=(P["N"] if gi % 2 == 0 else P["G"]))

            def main(gi):
                gt, off, L = t["groups"][gi]
                isS = gt == "S"
                tix = 2 if isS else (0 if L == 128 else 1)
                SD = B["SD"][0:L, 0, 0:L]
                qp = B["qp"][:, 0, 0:L]
                Kh = B["Kh"][0:L, 0, :]
                PN = Pg(gi)["N"]
                MM(PN[:, 0:L], SB_["Vtok"][0:L, gi, hb], SD, start=True, stop=False)
                if not isS:
                    MM(PN[:, 0:L], Rb[:, h, :], qp, start=False, stop=True)
                    MM(PZ[:, 128:256], Kh, SB_["Vtok"][0:L, gi, hb])
                    STT(Rb[:, h, :], Rst[:, h, :], gl[tix][h], PZ[:, 128:256], MUL, ADD)
                    STT(Rst[:, h, :], Rst[:, h, :], gl[tix][h], PZ[:, 128:256], MUL, ADD)
                else:
                    sample_quarters(S, SB_, Pg(gi), B["Vb"][:, :, :], h, gi, stR[:, h, :, :], oR_s[:, h, :, :], lambda sc: qp[:, sc], Kh,
                                    None, None, None, gl[2][h])
                ACT(W["hsq"][:, 0, 0:L], PN[:, 0:L], AF.Square)
                CP("act", W["hT"][:, 0, 0:L], PN[:, 0:L])

            def tail(gi):
                gt, off, L = t["groups"][gi]
                gc = slice(off, off + L)
                ln_gate(W, Pg(gi), L, prms[:, P_GNR + h:P_GNR + h + 1], gate[:, gc], SB_["mixed"][:, h, gc], True, sq_src=True)

            pipeline_groups(len(t["groups"]), pro, main, tail)

        run_heads(t, 4, body, vt_r)
        outproj(t, Wout, SB_)

    def pass_h_tile(t, Wh, Wout, hp):
        c0, n = t["c0"], t["n"]
        S, LW = carve_lanes(wkf, [], [("hT", (1, 128)), ("hsq", (1, 128)), ("lnv", (1, 128)), ("rs", (1, 128)),
                                      ("yy", (1, 128)), ("sg", (1, 512)), ("lf", (1, 512)), ("kk", (1, 512)),
                                      ("bb", (1, 512)), ("eb", (1, 512)), ("enb", (1, 512))], X_F)
        SB_, LB = carve_lanes(wkb, SH_B, LN_B, X_B)
        xk = lambda k: xn[:, k, c0:c0 + n]
        vt_h = lambda: vtok_proj(t, lambda k: Wh[:, k, 1024:1536], SB_, [PB[2], PB[5]])
        nsH = cst[:, (C_NSA if t["kind"] == "A" else C_NSB):(C_NSA if t["kind"] == "A" else C_NSB) + n]

        def body(hl, ln):
            PAl = PA[ln]
            W, B, P = LW[ln], LB[ln], LPS[ln]
            PN, PZ, PT = P["N"], P["Z"], P["T"]
            hg = hp * 4 + hl
            hb = slice(hl * 128, (hl + 1) * 128)
            Qt, Kt, gate, Khat = B["qT"][:, 0, :], B["kT"][:, 0, :], B["gate"][:, 0, :], B["Khat"][:, 0, :]
            sg, lf, kk, bb, eb, enb = (W[x][:, 0, 0:n] for x in ("sg", "lf", "kk", "bb", "eb", "enb"))
            MMK(PAl[:, 0:n], [(Wh[:, k, 512 + hl * 128:512 + (hl + 1) * 128], xk(k)) for k in range(8)])
            ACT(sg, PAl[:, 0:n], AF.Sigmoid)
            MMK(PAl[:, 0:n], [(Wh[:, k, hl * 128:(hl + 1) * 128], xk(k)) for k in range(8)])
            ACT(lf, sg, AF.Ln, bias=lbv[:, hg:hg + 1], scale=lbv[:, 8 + hg:9 + hg])
            TS("pool", kk, sg, lbv[:, 16 + hg:17 + hg], MUL, lbv[:, 8 + hg:9 + hg], ADD)
            SCAN(bb, nsH, lf, 0.0, MUL, ADD)
            ACT(eb, bb, AF.Exp)
            ACT(enb, bb, AF.Exp, scale=-1.0)
            TT("dve", Qt[:, 0:n], PAl[:, 0:n], eb, MUL)
            MMK(PAl[:, 0:n], [(Wh[:, k, 1536 + hl * 128:1536 + (hl + 1) * 128], xk(k)) for k in range(8)])
            TT("pool", Kt[:, 0:n], kk, enb, MUL)
            v3 = lambda ap: ap.rearrange("p (a b) -> p a b", a=16)
            if t["kind"] == "B":
                vh = lambda ap: ap.rearrange("p (a b) -> p a b", a=512 // HSEG)
                TT("pool", vh(Khat[:, 0:512]), vh(Kt[:, 0:512]),
                   bc(eb[:, HSEG - 1:512:HSEG].unsqueeze(2), (128, 512 // HSEG, HSEG)), MUL)
            else:
                TT("pool", v3(Khat[:, 0:128]), v3(Kt[:, 0:128]), bc(eb[:, 7:128:8].unsqueeze(2), (128, 16, 8)), MUL)
                TT("pool", Khat[:, 128:144], Kt[:, 128:144], bc(eb[:, 143:144], (128, 16)), MUL)
            ACT(gate[:, 0:n], PAl[:, 0:n], AF.Silu)
            def gparams(gi):
                gt, off, L = t["groups"][gi]
                isS = gt == "S"
                nseg, Ls = (16, 8) if isS else ((128 // HSEG, HSEG) if L == 128 else (1, L))
                return gt, off, L, isS, nseg, Ls

            def pro(gi):
                gt, off, L, isS, nseg, Ls = gparams(gi)
                gc = slice(off, off + L)
                MM(PZ[0:L, 0:L], Kt[:, gc], Qt[:, gc])
                msk = cst[0:L, (C_M8 if isS else C_M32):(C_M8 if isS else C_M32) + L]
                SD = B["SD"][0:L, 0, 0:L]
                TT("dve", SD, PZ[0:L, 0:L], msk, MUL)
                TP(PT[0:L, 0:128], Khat[:, gc], ident_b[:])
                Kh = B["Kh"][0:L, 0, :]
                CP("act", Kh, PT[0:L, 0:128])
                if (not isS) and nseg > 1:
                    Vb = B["Vb"][:, 0:nseg, :]
                    TT("pool", Vb, bc(SB_["Vtok"][:, gi, hb].unsqueeze(1), (128, nseg, 128)),
                       bc(cst[:, C_SEG4:C_SEG4 + nseg].unsqueeze(2), (128, nseg, 128)), MUL)

            def Pg(gi):
                if not ALT_N:
                    return P
                return dict(P, N=(P["N"] if gi % 2 == 0 else P["G"]))

            def main(gi):
                gt, off, L, isS, nseg, Ls = gparams(gi)
                SD = B["SD"][0:L, 0, 0:L]
                Kh = B["Kh"][0:L, 0, :]
                G = eb[:, off + Ls - 1:off + L:Ls]
                PN = Pg(gi)["N"]
                MM(PN[:, 0:L], SB_["Vtok"][0:L, gi, hb], SD, start=True, stop=False)
                if not isS:
                    if nseg > 1:
                        Vb = B["Vb"][:, 0:nseg, :]
                        MM(PZ[:, 0:nseg * 128], Kh, Vb.rearrange("p a b -> p (a b)"))
                        uo = 0
                    else:
                        MM(PZ[:, 128:256], Kh, SB_["Vtok"][0:L, gi, hb])
                        uo = 128
                    for j in range(nseg):
                        MM(PN[:, j * Ls:(j + 1) * Ls], Hb[:, hg, :], Qt[:, off + j * Ls:off + (j + 1) * Ls],
                           start=False, stop=(j == nseg - 1))
                        STT(Hb[:, hg, :], Hst[:, hg, :], G[:, j:j + 1], PZ[:, uo + j * 128:uo + (j + 1) * 128], MUL, ADD)
                        STT(Hst[:, hg, :], Hst[:, hg, :], G[:, j:j + 1], PZ[:, uo + j * 128:uo + (j + 1) * 128], MUL, ADD)
                else:
                    sample_quarters(S, SB_, Pg(gi), B["Vb"][:, :, :], hl, gi, stH[:, hg, :, :], oH_s[:, hg, :, :],
                                    lambda sc: Qt[:, sc], Kh, None, None, G, None)
                ACT(W["hsq"][:, 0, 0:L], PN[:, 0:L], AF.Square)
                CP("act", W["hT"][:, 0, 0:L], PN[:, 0:L])

            def tail(gi):
                gt, off, L, isS, nseg, Ls = gparams(gi)
                gc = slice(off, off + L)
                ln_gate(W, Pg(gi), L, prms[:, P_GNH + hg:P_GNH + hg + 1], gate[:, gc], SB_["mixed"][:, hl, gc], False, sq_src=True)

            pipeline_groups(len(t["groups"]), pro, main, tail)

        run_heads(t, 4, body, vt_h)
        outproj(t, Wout, SB_)

    ffn_state = {"pending": None, "par": 0}

    def pass_f_tile(t, Win, Wo, layer, f0, nf, half_idx):
        c0, n = t["c0"], t["n"]
        W, _ = carve(wkf, [("ue", (2, 520)), ("c1", (2, 512)), ("c2", (2, 512)), ("c3", (2, 512)), ("sl", (2, 512)),
                           ("cvin", (NF, 32))])
        WB_, _ = carve(wkb, [("hT", (10, 512))])
        hsel = ffn_state["par"] * 5
        ffn_state["par"] ^= 1
        isA = t["kind"] == "A"
        xk = lambda k: xn[:, k, c0:c0 + n]
        if isA and f0 == 0:
            DMA("sp", W["cvin"][:, :, :], stcv[:, layer, :, :])
        for fi in range(nf):
            f = f0 + fi
            PU_, PG_ = (PB[0], PB[1]) if fi % 2 == 0 else (PB[2], PB[3])
            cw = lambda j: prms[:, P_CW + (layer * 3 + j) * NF + f:P_CW + (layer * 3 + j) * NF + f + 1]
            cb = prms[:, P_CB + layer * NF + f:P_CB + layer * NF + f + 1]
            MMK(PU_[:, 0:n], [(Win[:, k, fi * 256:fi * 256 + 128], xk(k)) for k in range(8)])
            MMK(PG_[:, 0:n], [(Win[:, k, fi * 256 + 128:fi * 256 + 256], xk(k)) for k in range(8)])
            ue = W["ue"][:, fi % 2, :]
            c1, c2, c3, sl = (W[x][:, fi % 2, :] for x in ("c1", "c2", "c3", "sl"))
            tb = tailbuf[:, layer, f, :]
            if not isA:
                CP("pool", ue[:, 0:2], tb)
                CP("act", ue[:, 2:2 + n], PU_[:, 0:n])
                ACT(c1[:, 0:n], ue[:, 0:n], AF.Identity, bias=cb, scale=cw(0))
                STT(c2[:, 0:n], ue[:, 1:n + 1], cw(1), c1[:, 0:n], MUL, ADD)
                STT(c3[:, 0:n], PU_[:, 0:n], cw(2), c2[:, 0:n], MUL, ADD)
                CP("pool", tb, ue[:, n:n + 2])
            else:
                ues = ue[:, 0:160].rearrange("p (a b) -> p a b", a=16)
                v3 = lambda ap: ap.rearrange("p (a b) -> p a b", a=16)
                CP("pool", ues[:, :, 0:2], W["cvin"][:, f, :].rearrange("p (a b) -> p a b", a=16))
                MEMSET("pool", ue[:, 160:162], 0.0)
                CP("act", ues[:, :, 2:10], v3(PU_[:, 0:128]))
                CP("act", ue[:, 162:178], PU_[:, 128:144])
                ACT(v3(c1[:, 0:128]), ues[:, :, 0:8], AF.Identity, bias=cb, scale=cw(0))
                ACT(c1[:, 128:144], ue[:, 160:176], AF.Identity, bias=cb, scale=cw(0))
                STT(v3(c2[:, 0:128]), ues[:, :, 1:9], cw(1), v3(c1[:, 0:128]), MUL, ADD)
                STT(c2[:, 128:144], ue[:, 161:177], cw(1), c1[:, 128:144], MUL, ADD)
                STT(c3[:, 0:n], PU_[:, 0:n], cw(2), c2[:, 0:n], MUL, ADD)
                CP("pool", tb, ue[:, 176:178])
                CP("pool", cvout[:, f, :].rearrange("p (a b) -> p a b", a=16), ues[:, :, 8:10])
            ACT(sl[:, 0:n], c3[:, 0:n], AF.Silu)
            TT("dve", WB_["hT"][:, hsel + fi, 0:n], sl[:, 0:n], PG_[:, 0:n], MUL)
            if fi == 0 and ffn_state["pending"] is not None:
                ffn_state["pending"]()
                ffn_state["pending"] = None

        def do_out(hsel=hsel, n=n, c0=c0):
            pacc = [PB[4], PB[5], PB[6], PB[7]]
            for d in range(8):
                pa = pacc[d % 4]
                MMK(pa[:, 0:n], [(Wo[:, fi, d * 128:(d + 1) * 128], WB_["hT"][:, hsel + fi, 0:n]) for fi in range(nf)])
                TT("dve", xT[:, d, c0:c0 + n], xT[:, d, c0:c0 + n], pa[:, 0:n], ADD)
        ffn_state["pending"] = do_out
        if isA and f0 + nf == NF:
            DMA("sp", ocv_s[:, layer, :, :], cvout[:, :, :])

    FG = [(0, 5), (5, 5), (10, 4), (14, 4), (18, 4)]
    halves = _halves()
    passes = []
    for hi, tiles in enumerate(halves):
        for layer in range(2):
            if layer == 0:
                passes.append(("m", hi, layer, [w_ab[:, :, 0:2056], w_oab[:, 0:4, :]]))
                passes.append(("r", hi, layer, [w_ab[:, :, 2056:5128], w_oab[:, 4:8, :]]))
            else:
                passes.append(("h0", hi, layer, [w_c[:, :, 0, :], w_oc[:, 0:4, :]]))
                passes.append(("h1", hi, layer, [w_c[:, :, 1, :], w_oc[:, 4:8, :]]))
            for (f0, nf) in FG:
                passes.append(("f", hi, layer, [w_fi[layer][:, :, f0 * 256:(f0 + nf) * 256], w_fo[layer][:, f0:f0 + nf, :]], f0, nf))
    if ONLY is not None:
        passes = [p for p in passes if p[0] in ONLY]
    passes = [p for p in passes if p[1] in HALVES]
    npass = len(passes) if STAGE >= 99 else min(len(passes), STAGE)
    loaded = {}

    def ensure_loaded(i, prev_region):
        if i >= npass or i in loaded:
            return
        r = load_weights(passes[i][3], prev_region)
        if r is not None:
            loaded[i] = r

    for i in range(npass):
        p = passes[i]
        kind, hi, layer = p[0], p[1], p[2]
        tiles = halves[hi]
        if i not in loaded:
            ensure_loaded(i, None)
        views, region = loaded[i]
        ensure_loaded(i + 1, region)
        if kind in ("m", "h0") or (ONLY is not None and "m" not in ONLY and kind == "r"):
            if layer == 0 and (kind == "m" or (ONLY is not None and "m" not in ONLY)):
                DMA("sp", xT[:, :, 0:(HC if hi == 0 else 1024)], xT_d[:, :, (0 if hi == 0 else HC):(HC if hi == 0 else NT)])
            for t in tiles:
                norm_tile(t, layer, False)
        if kind == "f" and p[4] == 0:
            for t in tiles:
                norm_tile(t, 2 + layer, False)
        for t in tiles:
            if kind == "m":
                pass_m_tile(t, views[0], views[1])
            elif kind == "r":
                pass_r_tile(t, views[0], views[1])
            elif kind in ("h0", "h1"):
                pass_h_tile(t, views[0], views[1], int(kind[1]))
            else:
                pass_f_tile(t, views[0], views[1], layer, p[4], p[5], hi)
        if kind == "f" and ffn_state["pending"] is not None:
            ffn_state["pending"]()
            ffn_state["pending"] = None
        ensure_loaded(i + 1, None)
        last_of_half = (i + 1 == len(passes)) or (passes[i + 1][1] != hi) or (i + 1 == npass)
        if last_of_half:
            for t in tiles:
                norm_tile(t, 4, True)
    DMA("sp", oC_p, Cst[:])
    DMA("sp", on_p, nst[:])
    DMA("sp", om_p, mstate[:, 0:1])
    DMA("sp", oR_p, Rst[:])
    DMA("sp", oH_p, Hst[:])
    DMA("sp", ocv_p, tailbuf[:])
    tr.finish()
    tr.emit()
    for cm in reversed(ctxs):
        cm.__exit__(None, None, None)
    tr.close()
    return nc, tr


def _prep(inputs):
    I = {k: np.asarray(v) for k, v in inputs.items()}
    c128, c4, rot, gl = _consts()
    wab = _wl(I["w_in_ab"][0])
    sw = []
    for base in (2056, 2568):
        for h in range(4):
            o = base + h * 128
            sw.append(wab[:, :, o + 64:o + 128])
            sw.append(wab[:, :, o:o + 64])
    w_ab = np.ascontiguousarray(np.concatenate([wab] + sw, axis=2))
    wc = _wl(I["w_in_c"][0])
    parts = []
    for hp in range(2):
        parts.append(np.concatenate([wc[:, :, j * 1024 + hp * 512:j * 1024 + (hp + 1) * 512] for j in range(4)], axis=2))
    w_c = np.ascontiguousarray(np.stack(parts, axis=2))
    wfi = []
    for l in range(2):
        w = _wl(I["w_ffn_in"][l])
        w = np.stack([w[:, :, 0:2816].reshape(128, 8, NF, 128), w[:, :, 2816:].reshape(128, 8, NF, 128)], axis=3)
        wfi.append(w.reshape(128, 8, NF * 256))
    w_fi = np.ascontiguousarray(np.stack(wfi, 0))
    w_fo = np.ascontiguousarray(np.stack([_wl(I["w_ffn_out"][l]) for l in range(2)], 0))
    prm = np.zeros((128, P_END), np.float32)
    gains = np.stack([I["norm_mix"][0], I["norm_mix"][1], I["norm_ffn"][0], I["norm_ffn"][1], I["norm_final"]], 0)
    prm[:, P_GAIN:P_GAIN + 40] = _fm(gains).reshape(128, 40)
    prm[:, P_GNM:P_GNM + 4] = _fm(I["gn_mlstm"][0])
    prm[:, P_GNR:P_GNR + 4] = _fm(I["gn_ret"][0])
    prm[:, P_GNH:P_GNH + 8] = _fm(I["gn_hgrn"][0])
    prm[:, P_LB0:P_LB0 + 8] = _fm(I["lb_logits"][0])
    prm[:, P_LB1:P_LB1 + 8] = _fm(I["lb_logits"][1])
    prm[:, P_CW:P_CW + 132] = _fm(I["conv_w"]).reshape(128, 132)
    prm[:, P_CB:P_CB + 44] = _fm(I["conv_b"]).reshape(128, 44)
    prm4 = np.ascontiguousarray(np.stack([I["b_igate"][0], I["b_fgate"][0]], 1).astype(np.float32))
    shared = dict(w_ab=w_ab, w_oab=_wl(I["w_out_ab"][0]), w_c=w_c, w_oc=_wl(I["w_out_c"][0]), w_fi=w_fi, w_fo=w_fo,
                  prm=prm, prm4=prm4, c128=c128, c4=c4, rot=rot)
    maps = []
    for c in range(8):
        sq = slice(16 * c, 16 * c + 16)
        xa = np.concatenate([I["x_sample"][sq].reshape(128, D), I["meta_tokens"], I["x_prompt"][c]], 0).astype(np.float32)
        m = dict(shared)
        m["xT"] = np.ascontiguousarray(xa.T.reshape(8, 128, NT).transpose(1, 0, 2))
        m["stC"] = np.ascontiguousarray(I["state_mlstm_C"][0, sq].transpose(2, 1, 0, 3))
        m["stn"] = np.ascontiguousarray(I["state_mlstm_n"][0, sq].transpose(2, 1, 0))
        m["stm"] = np.ascontiguousarray(I["state_mlstm_m"][0, sq].T)
        m["stR"] = np.ascontiguousarray(I["state_ret_S"][0, sq].transpose(2, 1, 0, 3))
        m["stH"] = np.ascontiguousarray(I["state_hgrn_S"][0, sq].transpose(2, 1, 0, 3))
        cv = I["state_ffn_conv"][:, sq]
        cv = cv.reshape(2, 16, 2, NF, 128).transpose(4, 0, 3, 1, 2).reshape(128, 2, NF, 32)
        m["stcv"] = np.ascontiguousarray(cv)
        maps.append(m)
    return maps, gl


_CACHE = {}


def kernel(**inputs):
    maps, gl = _prep(inputs)
    if "nc" not in _CACHE:
        _CACHE["nc"] = build_program(gl)[0]
    nc = _CACHE["nc"]
    res = run_bass_kernel_spmd(nc, maps, core_ids=list(range(8)))
    R = res.results
    f32 = np.float32
    yp = np.zeros((8, 2048, D), f32)
    ys = np.zeros((128, 8, D), f32)
    Cp = np.zeros((1, 8, 4, 128, 128), f32)
    Cs = np.zeros((1, 128, 4, 128, 128), f32)
    np_ = np.zeros((1, 8, 4, 128), f32)
    ns = np.zeros((1, 128, 4, 128), f32)
    mp = np.zeros((1, 8, 4), f32)
    ms = np.zeros((1, 128, 4), f32)
    Rp = np.zeros((1, 8, 4, 128, 128), f32)
    Rs = np.zeros((1, 128, 4, 128, 128), f32)
    Hp = np.zeros((1, 8, 8, 128, 128), f32)
    Hs = np.zeros((1, 128, 8, 128, 128), f32)
    cvp = np.zeros((2, 8, 2, 2816), f32)
    cvs = np.zeros((2, 128, 2, 2816), f32)
    for c in range(8):
        r = R[c]
        sq = slice(16 * c, 16 * c + 16)
        ya = np.asarray(r["yT"]).transpose(1, 0, 2).reshape(D, NT).T
        ys[sq] = ya[0:128].reshape(16, 8, D)
        yp[c] = ya[144:]
        Cp[0, c] = np.asarray(r["oC_p"]).transpose(1, 0, 2)
        Cs[0, sq] = np.asarray(r["oC_s"]).transpose(2, 1, 0, 3)
        np_[0, c] = np.asarray(r["on_p"]).T
        ns[0, sq] = np.asarray(r["on_s"]).transpose(2, 1, 0)
        mp[0, c] = np.asarray(r["om_p"])[:, 0]
        ms[0, sq] = np.asarray(r["om_s"]).T
        Rp[0, c] = np.asarray(r["oR_p"]).transpose(1, 0, 2)
        Rs[0, sq] = np.asarray(r["oR_s"]).transpose(2, 1, 0, 3)
        Hp[0, c] = np.asarray(r["oH_p"]).transpose(1, 0, 2)
        Hs[0, sq] = np.asarray(r["oH_s"]).transpose(2, 1, 0, 3)
        cvp[:, c] = np.asarray(r["ocv_p"]).transpose(1, 3, 2, 0).reshape(2, 2, 2816)
        cvs[:, sq] = np.asarray(r["ocv_s"]).reshape(128, 2, NF, 16, 2).transpose(1, 3, 4, 2, 0).reshape(2, 16, 2, 2816)
    return (yp, ys, Cp, Cs, np_, ns, mp, ms, Rp, Rs, Hp, Hs, cvp, cvs)
```
